# Optimizing a Trainium2 kernel written in Bass

```python
import jax, jax.numpy as jnp
from jax import lax
import numpy as np

D_MODEL = 1024
BATCH = 4
SEQ = 8192
DEPTH = 1
DEC_BATCH = 8
DEC_SEQ = 4096
PAST_LEN = 128

D_RNN = D_MODEL
RG_BLOCKS = 16
RG_BW = D_RNN // RG_BLOCKS
RG_C = 8.0
CONV_W = 4
CONV_LEFT = 2
HEAD_DIM = 64
HEADS_PER_GROUP = 8
ATT_GROUPS = ((128, 1), (512, 4), (2048, 16))
N_GROUPS = len(ATT_GROUPS)
ATT_W = N_GROUPS * HEADS_PER_GROUP * HEAD_DIM
ATT_OUT = HEADS_PER_GROUP * HEAD_DIM
ATT_BLOCK = 64
ROT_DIM = HEAD_DIM // 4
ROPE_THETA = 500000.0
D_FF = ((-(-8 * D_MODEL // 3) + 255) // 256) * 256
N_MOD = 6
EPS = 1e-6
NEG_INF = -1e30
IN_SPLITS = (D_RNN, 2 * D_RNN, 2 * D_RNN + ATT_W, 2 * D_RNN + 2 * ATT_W, 2 * D_RNN + 3 * ATT_W)
IN_COLS = 2 * D_RNN + 3 * ATT_W + 2 * D_MODEL

kernel_name = 'griffin_dilated_window_adaln_encoder'


def _rmsnorm(x, g):
    xf = x.astype(jnp.float32)
    y = xf * lax.rsqrt(jnp.mean(xf * xf, axis=-1, keepdims=True) + EPS)
    return (y * g.astype(jnp.float32)).astype(x.dtype)


def _rope(x):
    S = x.shape[1]
    half = ROT_DIM // 2
    inv = ROPE_THETA ** (-(jnp.arange(0, ROT_DIM, 2, dtype=jnp.float32) / ROT_DIM))
    ang = jnp.arange(S, dtype=jnp.float32)[:, None] * inv[None, :]
    cos = jnp.cos(ang)[None, :, None, None, :]
    sin = jnp.sin(ang)[None, :, None, None, :]
    xf = x.astype(jnp.float32)
    x1 = xf[..., :half]
    x2 = xf[..., half:ROT_DIM]
    out = jnp.concatenate([x1 * cos - x2 * sin, x2 * cos + x1 * sin, xf[..., ROT_DIM:]], axis=-1)
    return out.astype(x.dtype)


def _centred_dwconv(x, w, b):
    S = x.shape[1]
    xp = jnp.pad(x, ((0, 0), (CONV_LEFT, CONV_W - 1 - CONV_LEFT), (0, 0)))
    y = b[None, None, :] + xp[:, 0:S] * w[0]
    for k in range(1, CONV_W):
        y = y + xp[:, k:k + S] * w[k]
    return y


def _rglru(xc, wa, ba, wx, bx, lam, reverse):
    B, S, _ = xc.shape
    xf = xc.astype(jnp.float32)
    xb = xf.reshape(B, S, RG_BLOCKS, RG_BW)
    r = jax.nn.sigmoid(jnp.einsum('bsnc,ncd->bsnd', xb, wa.astype(jnp.float32)) + ba.astype(jnp.float32)).reshape(B, S, D_RNN)
    i = jax.nn.sigmoid(jnp.einsum('bsnc,ncd->bsnd', xb, wx.astype(jnp.float32)) + bx.astype(jnp.float32)).reshape(B, S, D_RNN)
    log_a = -RG_C * r * jax.nn.softplus(-lam.astype(jnp.float32))
    a = jnp.exp(log_a)
    u = jnp.sqrt(-jnp.expm1(2.0 * log_a)) * (i * xf)

    def comb(e1, e2):
        a1, b1 = e1
        a2, b2 = e2
        return a1 * a2, a2 * b1 + b2

    _, h = lax.associative_scan(comb, (a, u), reverse=reverse, axis=1)
    return h


def _dilated_window_attention(q, k, v, dil, radius):
    B, S, H, E = q.shape
    L = S // dil
    nb = -(-L // ATT_BLOCK)
    Lp = nb * ATT_BLOCK

    def fold(t):
        t = t.reshape(B, L, dil, H, E)
        return jnp.pad(t, ((0, 0), (0, Lp - L), (0, 0), (0, 0), (0, 0)))

    def windows(t):
        tp = jnp.pad(fold(t), ((0, 0), (ATT_BLOCK, ATT_BLOCK), (0, 0), (0, 0), (0, 0)))
        tp = tp.reshape(B, nb + 2, ATT_BLOCK, dil, H, E)
        return jnp.concatenate([tp[:, :-2], tp[:, 1:-1], tp[:, 2:]], axis=2)

    qf = fold(q).reshape(B, nb, ATT_BLOCK, dil, H, E)
    kw = windows(k)
    vw = windows(v)
    s = jnp.einsum('bnqrhe,bnkrhe->bnrhqk', qf, kw, preferred_element_type=jnp.float32) * (HEAD_DIM ** -0.5)
    blk = jnp.arange(nb)[:, None, None]
    mq = blk * ATT_BLOCK + jnp.arange(ATT_BLOCK)[None, :, None]
    mk = (blk - 1) * ATT_BLOCK + jnp.arange(3 * ATT_BLOCK)[None, None, :]
    valid = (jnp.abs(mq - mk) <= radius) & (mk >= 0) & (mk < L)
    s = jnp.where(valid[None, :, None, None, :, :], s, NEG_INF)
    lse = jax.nn.logsumexp(s, axis=-1)
    p = jnp.exp(s - lse[..., None])
    o = jnp.einsum('bnrhqk,bnkrhe->bnqrhe', p, vw.astype(jnp.float32))
    o = o.reshape(B, Lp, dil, H, E)[:, :L].reshape(B, S, H, E)
    lse = lse.transpose(0, 1, 4, 2, 3).reshape(B, Lp, dil, H)[:, :L].reshape(B, S, H)
    return o, lse


def _layer(x, c, w_ada, b_ada, norm1_g, w_in, conv_w, conv_b, rg_wa, rg_ba, rg_wx, rg_bx, rg_lambda,
           w_br_rnn, w_br_attn, w_out, norm2_g, w_ffn_in, w_ffn_out):
    B, S, _ = x.shape
    mod = jax.nn.silu(c.astype(jnp.float32)) @ w_ada.astype(jnp.float32) + b_ada.astype(jnp.float32)
    sh1, sc1, gt1, sh2, sc2, gt2 = jnp.split(mod, N_MOD, axis=-1)

    h = (_rmsnorm(x, norm1_g) * (1.0 + sc1[:, None]) + sh1[:, None]).astype(x.dtype)
    z = h @ w_in
    xr, gr, q, k, v, mg = jnp.split(z, IN_SPLITS, axis=-1)

    xc = _centred_dwconv(xr, conv_w, conv_b)
    rec = _rglru(xc, rg_wa[0], rg_ba[0], rg_wx[0], rg_bx[0], rg_lambda[0], False) \
        + _rglru(xc, rg_wa[1], rg_ba[1], rg_wx[1], rg_bx[1], rg_lambda[1], True)
    rnn_out = (rec * jax.nn.gelu(gr.astype(jnp.float32), approximate=True)).astype(x.dtype)

    q = _rope(q.reshape(B, S, N_GROUPS, HEADS_PER_GROUP, HEAD_DIM))
    k = _rope(k.reshape(B, S, N_GROUPS, HEADS_PER_GROUP, HEAD_DIM))
    v = v.reshape(B, S, N_GROUPS, HEADS_PER_GROUP, HEAD_DIM)
    outs = []
    lses = []
    for g, (win, dil) in enumerate(ATT_GROUPS):
        o_g, lse_g = _dilated_window_attention(q[:, :, g], k[:, :, g], v[:, :, g], dil, (win // 2) // dil)
        outs.append(o_g)
        lses.append(lse_g)
    wts = jax.nn.softmax(jnp.stack(lses, axis=0), axis=0)
    att = jnp.einsum('gbsh,gbshe->bshe', wts, jnp.stack(outs, axis=0))
    att_out = att.reshape(B, S, ATT_OUT).astype(x.dtype)

    gate_r, gate_a = jnp.split(jax.nn.sigmoid(mg.astype(jnp.float32)), 2, axis=-1)
    merged = gate_r * (rnn_out @ w_br_rnn) + gate_a * (att_out @ w_br_attn)
    mix = merged.astype(x.dtype) @ w_out
    x = x + (gt1[:, None] * mix).astype(x.dtype)

    h2 = (_rmsnorm(x, norm2_g) * (1.0 + sc2[:, None]) + sh2[:, None]).astype(x.dtype)
    fg, fu = jnp.split(h2 @ w_ffn_in, 2, axis=-1)
    ff = (jax.nn.silu(fg.astype(jnp.float32)) * fu.astype(jnp.float32)).astype(x.dtype) @ w_ffn_out
    x = x + (gt2[:, None] * ff).astype(x.dtype)
    return x


def _encode(x, c, w_ada, b_ada, norm1_g, w_in, conv_w, conv_b, rg_wa, rg_ba, rg_wx, rg_bx, rg_lambda,
            w_br_rnn, w_br_attn, w_out, norm2_g, w_ffn_in, w_ffn_out, final_g):
    for l in range(DEPTH):
        x = _layer(x, c, w_ada[l], b_ada[l], norm1_g[l], w_in[l], conv_w[l], conv_b[l], rg_wa[l], rg_ba[l],
                   rg_wx[l], rg_bx[l], rg_lambda[l], w_br_rnn[l], w_br_attn[l], w_out[l], norm2_g[l],
                   w_ffn_in[l], w_ffn_out[l])
    return _rmsnorm(x, final_g)


def setup_inputs(seed: int = 0) -> dict:
    key = jax.random.key(seed)
    ks = jax.random.split(key, 24)

    def nrm(k, shape, s):
        return jax.random.normal(k, shape, jnp.float32) * s

    a0 = jax.random.uniform(ks[15], (DEPTH, 2, D_RNN), jnp.float32, minval=0.9, maxval=0.999)
    return {
        'x_prompt': nrm(ks[0], (BATCH, SEQ, D_MODEL), 1.0),
        'x_sample': nrm(ks[1], (DEC_BATCH, DEC_SEQ, D_MODEL), 1.0),
        'c_prompt': nrm(ks[2], (BATCH, D_MODEL), 1.0),
        'c_sample': nrm(ks[3], (DEC_BATCH, D_MODEL), 1.0),
        'w_ada': nrm(ks[4], (DEPTH, D_MODEL, N_MOD * D_MODEL), 0.5 * D_MODEL ** -0.5),
        'b_ada': nrm(ks[5], (DEPTH, N_MOD * D_MODEL), 0.01),
        'norm1_g': 1.0 + nrm(ks[6], (DEPTH, D_MODEL), 0.02),
        'w_in': nrm(ks[7], (DEPTH, D_MODEL, IN_COLS), D_MODEL ** -0.5),
        'conv_w': nrm(ks[8], (DEPTH, CONV_W, D_RNN), CONV_W ** -0.5),
        'conv_b': nrm(ks[9], (DEPTH, D_RNN), 0.01),
        'rg_wa': nrm(ks[10], (DEPTH, 2, RG_BLOCKS, RG_BW, RG_BW), RG_BW ** -0.5),
        'rg_ba': nrm(ks[11], (DEPTH, 2, RG_BLOCKS, RG_BW), 0.01),
        'rg_wx': nrm(ks[12], (DEPTH, 2, RG_BLOCKS, RG_BW, RG_BW), RG_BW ** -0.5),
        'rg_bx': nrm(ks[13], (DEPTH, 2, RG_BLOCKS, RG_BW), 0.01),
        'rg_lambda': jnp.log(a0) - jnp.log1p(-a0),
        'w_br_rnn': nrm(ks[16], (DEPTH, D_RNN, D_MODEL), D_RNN ** -0.5),
        'w_br_attn': nrm(ks[17], (DEPTH, ATT_OUT, D_MODEL), ATT_OUT ** -0.5),
        'w_out': nrm(ks[18], (DEPTH, D_MODEL, D_MODEL), D_MODEL ** -0.5),
        'norm2_g': 1.0 + nrm(ks[19], (DEPTH, D_MODEL), 0.02),
        'w_ffn_in': nrm(ks[20], (DEPTH, D_MODEL, 2 * D_FF), D_MODEL ** -0.5),
        'w_ffn_out': nrm(ks[21], (DEPTH, D_FF, D_MODEL), D_FF ** -0.5),
        'final_g': 1.0 + nrm(ks[22], (D_MODEL,), 0.02),
    }


def reference(x_prompt, x_sample, c_prompt, c_sample, w_ada, b_ada, norm1_g, w_in, conv_w, conv_b,
              rg_wa, rg_ba, rg_wx, rg_bx, rg_lambda, w_br_rnn, w_br_attn, w_out, norm2_g,
              w_ffn_in, w_ffn_out, final_g):
    y_prompt = _encode(x_prompt, c_prompt, w_ada, b_ada, norm1_g, w_in, conv_w, conv_b, rg_wa, rg_ba, rg_wx,
                       rg_bx, rg_lambda, w_br_rnn, w_br_attn, w_out, norm2_g, w_ffn_in, w_ffn_out, final_g)
    y_sample = _encode(x_sample, c_sample, w_ada, b_ada, norm1_g, w_in, conv_w, conv_b, rg_wa, rg_ba, rg_wx,
                       rg_bx, rg_lambda, w_br_rnn, w_br_attn, w_out, norm2_g, w_ffn_in, w_ffn_out, final_g)
    return (y_prompt, y_sample)
```

```python
import numpy as np
import concourse.bass as bass
import concourse.mybir as mybir
from concourse.bass_utils import run_bass_kernel_spmd
from concourse.alu_op_type import AluOpType as ALU
from contextlib import ExitStack

F32 = mybir.dt.float32
BF16 = mybir.dt.bfloat16
AF = mybir.ActivationFunctionType

D = 1024
T = 8192
SEG = 4096
NT = T // 128
KC = D // 128
IN_COLS = 8704
D_FF = 2816
FC = D_FF // 128
EPS = 1e-6
GROUPS = ((128, 1), (512, 4), (2048, 16))
ROPE_THETA = 500000.0
import os as _os
EVAC = _os.environ.get("K_EVAC", "both")
P4MODE = _os.environ.get("K_P4MODE", "")
BISECT = int(_os.environ.get("K_BISECT", "0"))

V_BADA = 0
V_N1G = 48
V_N2G = 56
V_CONVW = 64
V_CONVB = 96
V_BA = 104
V_BX = 120
V_LAM = 136
NV = 152


class Prog:
    ENG = ("sync", "scalar", "vector", "gpsimd", "tensor")

    def __init__(self, nc, stack):
        self.nc = nc
        self.stack = stack
        self.e = {"sync": nc.sync, "scalar": nc.scalar, "vector": nc.vector,
                  "gpsimd": nc.gpsimd, "tensor": nc.tensor}
        self.sem = {n: stack.enter_context(nc.semaphore("s_" + n)) for n in self.ENG}
        self.cnt = {n: 0 for n in self.ENG}
        self.seen = {n: {} for n in self.ENG}
        self.dsems = []

    def dsem(self, name):
        s = self.stack.enter_context(self.nc.semaphore("d_" + name))
        d = {"sem": s, "cnt": 0, "name": "d_" + name}
        self.dsems.append(d)
        return d

    def wait(self, eng, *deps):
        for d in deps:
            if d is None:
                continue
            if isinstance(d, (list,)):
                self.wait(eng, *d)
                continue
            key, s, val = d
            if self.seen[eng].get(key, 0) < val:
                self.seen[eng][key] = val
                self.e[eng].wait_ge(s, val)

    def sig(self, eng, ins):
        self.cnt[eng] += 1
        ins.then_inc(self.sem[eng], 1)
        return (eng, self.sem[eng], self.cnt[eng])

    def do(self, eng, deps, method, *a, **kw):
        self.wait(eng, deps)
        return self.sig(eng, getattr(self.e[eng], method)(*a, **kw))

    def last(self, eng):
        return (eng, self.sem[eng], self.cnt[eng])

    def dma(self, q, ds, out, in_, *deps, **kw):
        self.wait(q, *deps)
        ins = self.e[q].dma_start(out=out, in_=in_, **kw)
        ds["cnt"] += 16
        ins.then_inc(ds["sem"], 16)
        return (ds["name"], ds["sem"], ds["cnt"])

    def dlast(self, ds):
        return (ds["name"], ds["sem"], ds["cnt"])

    def barrier(self):
        hs = [self.last(n) for n in self.ENG] + [self.dlast(d) for d in self.dsems]
        for n in self.ENG:
            self.wait(n, *[h for h in hs if h[0] != n and h[2] > 0])


def build_program(debug=False, upto=99):
    nc = bass.Bass("TRN2", target_bir_lowering=False)
    dbgset = set(debug) if debug else set()

    def din(name, shape, dt=F32):
        return nc.dram_tensor(name, list(shape), dt, kind="ExternalInput").ap()

    def dscr(name, shape, dt=F32):
        return nc.dram_tensor(name, list(shape), dt, kind=("ExternalOutput" if name in dbgset else "Internal")).ap()

    x_in = din("x", [T, D])
    cT_in = din("cT", [128, KC, 2])
    conn_in = din("conn", [128, 1])
    vecs_in = din("vecs", [128, NV])
    rope_in = din("rope", [128, NT, 16])
    fgrow_in = din("fgrow", [1, D])
    badarow_in = din("badarow", [1, 6 * D])
    w_ada = din("w_ada", [D, 6 * D])
    w_in = din("w_in", [D, IN_COLS])
    rg_wa = din("rg_wa", [2, 16, 64, 64])
    rg_wx = din("rg_wx", [2, 16, 64, 64])
    w_br_rnn = din("w_br_rnn", [D, D])
    w_br_attn = din("w_br_attn", [512, D])
    w_out = din("w_out", [D, D])
    w_ffn_in = din("w_ffn_in", [D, 2 * D_FF])
    w_ffn_out = din("w_ffn_out", [D_FF, D])
    y_out = nc.dram_tensor("y", [T, D], F32, kind="ExternalOutput").ap()

    xrT = dscr("xrT", [D, T])
    grT = dscr("grT", [D, T])
    sgT = dscr("sgT", [2 * D, T])
    qn = dscr("qn", [3, T, 512], BF16)
    kn = dscr("kn", [3, T, 512], BF16)
    vn = dscr("vn", [3, T, 512], BF16)
    rnnT = dscr("rnnT", [D, T], BF16)
    attT = dscr("attT", [512, T], BF16)
    x1s = dscr("x1s", [T, D])
    h2T = dscr("h2T", [D, T], BF16)
    hT_dbg = dscr("hT_dbg", [D, T], BF16) if "hT_dbg" in dbgset else None

    stack = ExitStack()
    with stack:
        P = Prog(nc, stack)

        def sb(name, shape, dt=F32, st=stack):
            return st.enter_context(nc.sbuf_tensor("sb_" + name, list(shape), dt))

        def ps(name, shape, dt=F32, st=stack):
            return st.enter_context(nc.psum_tensor("ps_" + name, list(shape), dt))

        vecs = sb("vecs", [128, NV])
        conn = sb("conn", [128, 1])
        idf = sb("idf", [128, 128])
        idb = sb("idb", [128, 128], BF16)
        scale1 = sb("scale1", [128, KC, 2])
        shift1 = sb("shift1", [128, KC, 2])
        scale2 = sb("scale2", [128, KC, 2])
        shift2 = sb("shift2", [128, KC, 2])
        spv = sb("spv", [128, 16])
        hspv = sb("hspv", [128, 16])
        hbias = sb("hbias", [128, 32])
        cpow = sb("cpow", [128, 2])
        gt2row = sb("gt2row", [128, 2, D])
        fgrow = sb("fgrow", [128, D])

        d_const = P.dsem("const")
        P.dma("sync", d_const, vecs[:], vecs_in[:, :])
        P.dma("sync", d_const, conn[:], conn_in[:, :])
        h_const = P.dma("sync", d_const, fgrow[:], fgrow_in[0:1, :].broadcast_to([128, D]))

        h_ms = P.do("gpsimd", [], "memset", idf[:], 1.0)
        h_idf = P.do("gpsimd", [h_ms], "affine_select", out=idf[:], in_=idf[:], pattern=[[-1, 128]],
                     compare_op=ALU.is_equal, fill=0.0, base=0, channel_multiplier=1)
        h_idb = P.do("gpsimd", [h_idf], "tensor_copy", out=idb[:], in_=idf[:])
        h_cpow = [P.do("gpsimd", [], "memset", cpow[:, 0:1], -0.5), P.do("gpsimd", [], "memset", cpow[:, 1:2], 0.5)]
        gt1stack = ExitStack()
        gt1row = sb("gt1row", [128, 2, D], st=gt1stack)
        gtrows = [gt1row, gt2row]

        with ExitStack() as s0:
            cT = sb("cT", [128, KC, 2], st=s0)
            scT = sb("scT", [128, KC, 2], st=s0)
            screp = sb("screp", [128, 2, KC, 128], st=s0)
            wbuf = [sb("wa", [128, KC, D], st=s0), sb("wa2", [128, KC, D], st=s0)]
            modT = sb("modT", [128, 4, KC, 2], st=s0)
            brow = sb("brow", [128, 2, D], st=s0)
            tmp16 = sb("tmp16", [128, 16], st=s0)
            pm = [ps("pm0", [128, 512], st=s0), ps("pm1", [128, 512], st=s0)]
            d_c = P.dsem("p0c")
            d_w = [P.dsem("p0w0"), P.dsem("p0w1")]

            h_c = P.dma("sync", d_c, cT[:], cT_in[:, :, :])
            h_sc = P.do("scalar", [h_c], "activation", out=scT[:], in_=cT[:], func=AF.Silu)
            h_rep = []
            for s in range(2):
                h_rep.append(P.do("vector", [h_sc], "tensor_copy", out=screp[:, s, :, :],
                                  in_=scT[:, :, s:s + 1].broadcast_to([128, KC, 128])))
            h_t = P.do("scalar", [h_const], "activation", out=tmp16[:], in_=vecs[:, V_LAM:V_LAM + 16],
                       func=AF.Exp, scale=-1.0)
            h_t = P.do("scalar", [h_t], "activation", out=tmp16[:], in_=tmp16[:], func=AF.Ln, bias=1.0)
            P.do("vector", [h_t], "tensor_scalar", out=spv[:], in0=tmp16[:], scalar1=-8.0, scalar2=None, op0=ALU.mult)
            h_spv = [P.do("vector", [h_t], "tensor_scalar", out=hspv[:], in0=tmp16[:], scalar1=-4.0, scalar2=None,
                          op0=ALU.mult),
                     P.do("vector", [h_const], "tensor_scalar", out=hbias[:], in0=vecs[:, V_BA:V_BA + 32], scalar1=0.5,
                          scalar2=None, op0=ALU.mult)]

            w_ada_v = w_ada.rearrange("(k p) n -> p k n", p=128)
            jobs = [(0, 0), (1, 1), (2, 3), (3, 4)]
            h_free = [None, None]
            h_pmfree = [None, None]
            npm = 0
            h_mod = []
            for ji, (mi, col) in enumerate(jobs):
                b = ji % 2
                h_w = P.dma("sync", d_w[b], wbuf[b][:], w_ada_v[:, :, col * D:(col + 1) * D], h_free[b])
                for j in range(KC):
                    pb = npm % 2
                    npm += 1
                    P.wait("tensor", h_w, h_sc, h_pmfree[pb])
                    for k in range(KC):
                        mm = nc.tensor.matmul(pm[pb][:, 0:2], lhsT=wbuf[b][:, k, j * 128:(j + 1) * 128],
                                              rhs=scT[:, k, :], start=(k == 0), stop=(k == KC - 1))
                    h_mm = P.sig("tensor", mm)
                    h_e = P.do("vector", [h_mm, h_const], "tensor_scalar", out=modT[:, mi, j, :], in0=pm[pb][:, 0:2],
                               scalar1=vecs[:, V_BADA + col * 8 + j:V_BADA + col * 8 + j + 1], scalar2=None,
                               op0=ALU.add)
                    h_pmfree[pb] = h_e
                    h_mod.append(h_e)
                h_free[b] = h_mm
            h_ss = []
            for (dst_sc, dst_sh, mi_sh, mi_sc, gcol) in ((scale1, shift1, 0, 1, V_N1G), (scale2, shift2, 2, 3, V_N2G)):
                h1 = P.do("vector", h_mod, "tensor_scalar", out=dst_sc[:], in0=modT[:, mi_sc, :, :], scalar1=1.0,
                          scalar2=None, op0=ALU.add)
                h2 = P.do("vector", [h1], "tensor_tensor", out=dst_sc[:], in0=dst_sc[:],
                          in1=vecs[:, gcol:gcol + KC].unsqueeze(2).broadcast_to([128, KC, 2]), op=ALU.mult)
                h3 = P.do("vector", h_mod, "tensor_copy", out=dst_sh[:], in_=modT[:, mi_sh, :, :])
                h_ss += [h2, h3]
            d_b = P.dsem("p0b")
            for wi, col in enumerate((2, 5)):
                h_b = P.dma("sync", d_b, brow[:, wi, :], badarow_in[0:1, col * D:(col + 1) * D].broadcast_to([128, D]))
            h_gt = []
            for wi, col in enumerate((2, 5)):
                b = wi % 2
                h_w = P.dma("sync", d_w[b], wbuf[b][:], w_ada_v[:, :, col * D:(col + 1) * D], h_free[b])
                for s in range(2):
                    for half in range(2):
                        pb = npm % 2
                        npm += 1
                        P.wait("tensor", h_w, h_rep, h_pmfree[pb])
                        for k in range(KC):
                            mm = nc.tensor.matmul(pm[pb][:, :], lhsT=screp[:, s, k, :],
                                                  rhs=wbuf[b][:, k, half * 512:(half + 1) * 512],
                                                  start=(k == 0), stop=(k == KC - 1))
                        h_mm = P.sig("tensor", mm)
                        h_e = P.do("vector", [h_mm, h_b], "tensor_tensor",
                                   out=gtrows[wi][:, s, half * 512:(half + 1) * 512], in0=pm[pb][:, :],
                                   in1=brow[:, wi, half * 512:(half + 1) * 512], op=ALU.add)
                        h_pmfree[pb] = h_e
                        h_gt.append(h_e)
                h_free[b] = h_mm
            P.barrier()

        if upto <= 0:
            gt1stack.close()
            return finish(nc, P)

        hstack = ExitStack()
        hT = sb("hT", [128, KC, T], BF16, st=hstack)

        def norm_tile(xsrc, ss_col, rs_col, xn_dst, junk, deps, h_junk):
            h1 = P.do("scalar", deps + [h_junk], "activation", out=junk[:], in_=xsrc, func=AF.Square, accum_out=ss_col)
            h2 = P.do("gpsimd", [h1], "tensor_scalar", out=rs_col, in0=ss_col, scalar1=1.0 / D, scalar2=EPS,
                      op0=ALU.mult, op1=ALU.add)
            h4 = P.do("gpsimd", [h2, h_cpow], "tensor_tensor", out=rs_col, in0=rs_col, in1=cpow[:, 0:1], op=ALU.pow)
            return h1, h4

        with ExitStack() as s1:
            NB = 6
            xt = [sb(f"xt{i}", [128, D], st=s1) for i in range(NB)]
            xn = [sb(f"xn{i}", [128, D], st=s1) for i in range(3)]
            junk = sb("junk", [128, D], BF16, st=s1)
            ss = sb("ss", [128, NT], st=s1)
            rs = sb("rs", [128, NT], st=s1)
            pt_ = [ps(f"ptr{i}", [128, KC, 128], st=s1) for i in range(2)]
            d_x = [P.dsem(f"p1x{i}") for i in range(NB)]
            h_xfree = [None] * NB
            h_xnfree = [None] * 3
            h_ptfree = [None] * 2
            h_junk = None
            h_xn_ = {}

            def N1(i):
                nonlocal h_junk
                b = i % NB
                xb_ = i % 3
                h_x = P.dma("sync", d_x[b], xt[b][:], x_in[i * 128:(i + 1) * 128, :], h_xfree[b])
                h_junk, h_rs = norm_tile(xt[b][:], ss[:, i:i + 1], rs[:, i:i + 1], None, junk, [h_x], h_junk)
                h_xn = P.do("vector", [h_rs, h_xnfree[xb_]], "tensor_scalar", out=xn[xb_][:], in0=xt[b][:],
                            scalar1=rs[:, i:i + 1], scalar2=None, op0=ALU.mult)
                h_xfree[b] = h_xn
                h_xn_[i] = h_xn

            def T1(i):
                nb = i % 2
                xb_ = i % 3
                seg = i // (NT // 2)
                P.wait("tensor", h_xn_.pop(i), h_idf, h_ptfree[nb])
                for k in range(KC):
                    tr = nc.tensor.transpose(pt_[nb][:, k, :], xn[xb_][:, k * 128:(k + 1) * 128], idf[:])
                h_tr = P.sig("tensor", tr)
                h_xnfree[xb_] = h_tr
                hs = []
                for k in range(KC):
                    if nb == 0:
                        hs.append(P.do("scalar", [h_tr, h_ss], "activation", out=hT[:, k, i * 128:(i + 1) * 128],
                                       in_=pt_[nb][:, k, :], func=AF.Identity,
                                       scale=scale1[:, k, seg:seg + 1], bias=shift1[:, k, seg:seg + 1]))
                    else:
                        hs.append(P.do("vector", [h_tr, h_ss], "tensor_scalar", out=hT[:, k, i * 128:(i + 1) * 128],
                                       in0=pt_[nb][:, k, :], scalar1=scale1[:, k, seg:seg + 1],
                                       scalar2=shift1[:, k, seg:seg + 1], op0=ALU.mult, op1=ALU.add))
                h_ptfree[nb] = hs

            N1(0)
            N1(1)
            for i in range(NT):
                T1(i)
                if i + 2 < NT:
                    N1(i + 2)
            P.barrier()
            if hT_dbg is not None:
                d_dbg = P.dsem("dbg")
                for k in range(KC):
                    P.dma("sync", d_dbg, hT_dbg[k * 128:(k + 1) * 128, :], hT[:, k, :])
                P.barrier()

        if upto <= 1:
            hstack.close()
            gt1stack.close()
            return finish(nc, P)

        w_in_v = w_in.rearrange("(k p) n -> p k n", p=128)
        with ExitStack() as s2:
            wsl = [sb(f"wsl{i}", [128, KC, 128], BF16, st=s2) for i in range(2)]
            stg = [sb(f"stg{i}", [128, 2048], st=s2) for i in range(2)]
            pz = [ps(f"pz{i}", [128, 512], st=s2) for i in range(2)]
            d_wsl = [P.dsem(f"p2w{i}") for i in range(2)]
            d_stg = [P.dsem(f"p2s{i}") for i in range(2)]
            jobs = []
            for j in range(8):
                jobs.append((xrT, j * 128, j * 128, "copy"))
            for j in range(8):
                jobs.append((grT, j * 128, 1024 + j * 128, "gelu"))
            for j in range(16):
                jobs.append((sgT, j * 128, 6656 + j * 128, "sigmoid"))
            h_wfree = [None, None]
            h_pzfree = [None, None]
            h_stgfree = [None, None]
            ntile = 0
            nstage = 0
            for ji, (dst, row0, col0, mode) in enumerate(jobs):
                b = ji % 2
                h_w = P.dma("gpsimd", d_wsl[b], wsl[b][:], w_in_v[:, :, col0:col0 + 128], h_wfree[b])
                evs = []
                for tt in range(16):
                    pb = ntile % 2
                    ntile += 1
                    sbi = nstage % 2
                    P.wait("tensor", h_w, h_pzfree[pb])
                    for k in range(KC):
                        mm = nc.tensor.matmul(pz[pb][:, :], lhsT=wsl[b][:, k, :], rhs=hT[:, k, tt * 512:(tt + 1) * 512],
                                              start=(k == 0), stop=(k == KC - 1))
                    h_mm = P.sig("tensor", mm)
                    o = stg[sbi][:, (tt % 4) * 512:(tt % 4 + 1) * 512]
                    if mode == "copy" and tt % 2 == 1:
                        h_e = P.do("vector", [h_mm, h_stgfree[sbi]], "tensor_copy", out=o, in_=pz[pb][:, :])
                    else:
                        fn = {"copy": AF.Copy, "gelu": AF.Gelu_apprx_tanh, "sigmoid": AF.Sigmoid}[mode]
                        h_e = P.do("scalar", [h_mm, h_stgfree[sbi]], "activation", out=o, in_=pz[pb][:, :], func=fn)
                    h_pzfree[pb] = h_e
                    evs.append(h_e)
                    if tt % 4 == 3:
                        h_st = P.dma("sync", d_stg[sbi], dst[row0:row0 + 128, (tt // 4) * 2048:(tt // 4 + 1) * 2048],
                                     stg[sbi][:], evs)
                        h_stgfree[sbi] = h_st
                        nstage += 1
                        evs = []
                h_wfree[b] = h_mm
            P.barrier()

        with ExitStack() as s2:
            wq = sb("wq", [128, KC, 1536], BF16, st=s2)
            ropet = sb("ropet", [128, NT, 16], st=s2)
            TB = 2
            stq = [sb(f"stq{i}", [128, TB, 2, 512], BF16, st=s2) for i in range(2)]
            stv = [sb(f"stv{i}", [128, TB, 512], BF16, st=s2) for i in range(2)]
            rtmp = [sb(f"rtmp{i}", [128, 4, 16, 8], st=s2) for i in range(2)]
            pq = [ps(f"pq{i}", [128, 3, 512], st=s2) for i in range(2)]
            d_wq = P.dsem("p2wq")
            d_rp = P.dsem("p2rp")
            d_sq = [P.dsem(f"p2sq{i}") for i in range(2)]
            d_sk = [P.dsem(f"p2sk{i}") for i in range(2)]
            d_sv = [P.dsem(f"p2sv{i}") for i in range(2)]
            h_rp = P.dma("sync", d_rp, ropet[:], rope_in[:, :, :])
            h_wqfree = None
            h_pqfree = [None, None]
            h_stfree = [[None, None], [None, None]]
            h_rtfree = [None, None]
            h_rffree = [None, None]
            rf = [sb(f"rf{i}", [128, 16, 16], st=s2) for i in range(2)]
            nt_ = 0
            for g in range(3):
                hw = []
                for c, base in enumerate((2048, 3584, 5120)):
                    hw.append(P.dma("gpsimd", d_wq, wq[:, :, c * 512:(c + 1) * 512],
                                    w_in_v[:, :, base + g * 512:base + (g + 1) * 512], h_wqfree))
                h_w = hw[-1]
                evq, evv = [], []
                for i in range(NT):
                    pb = nt_ % 2
                    sbi = (nt_ // TB) % 2
                    ti = i % TB
                    nt_ += 1
                    P.wait("tensor", h_w, h_pqfree[pb])
                    for c in range(3):
                        for k in range(KC):
                            mm = nc.tensor.matmul(pq[pb][:, c, :], lhsT=hT[:, k, i * 128:(i + 1) * 128],
                                                  rhs=wq[:, k, c * 512:(c + 1) * 512],
                                                  start=(k == 0), stop=(k == KC - 1))
                    h_mm = P.sig("tensor", mm)
                    h_v = P.do("scalar", [h_mm, h_stfree[sbi][1]], "activation", out=stv[sbi][:, ti, :],
                               in_=pq[pb][:, 2, :], func=AF.Copy)
                    src = pq[pb][:, 0:2, :].rearrange("p a (h e) -> p (a h) e", e=64)
                    dsto = stq[sbi][:, ti, :, :].rearrange("p a (h e) -> p (a h) e", e=64)
                    h_r = P.do("scalar", [h_mm, h_stfree[sbi][0]], "activation", out=stq[sbi][:, ti, :, :],
                               in_=pq[pb][:, 0:2, :], func=AF.Copy)
                    h_rf = P.do("scalar", [h_mm, h_rffree[pb]], "activation", out=rf[pb][:, :, :],
                                in_=src[:, :, 0:16], func=AF.Copy)
                    cosb = ropet[:, i, 0:8].unsqueeze(1).broadcast_to([128, 16, 8])
                    sinb = ropet[:, i, 8:16].unsqueeze(1).broadcast_to([128, 16, 8])
                    x1 = rf[pb][:, :, 0:8]
                    x2 = rf[pb][:, :, 8:16]
                    rt = rtmp[pb]
                    dep0 = [h_rf, h_rp, h_rtfree[pb]]
                    ha = P.do("vector", dep0, "tensor_tensor", out=rt[:, 0, :, :], in0=x1, in1=cosb, op=ALU.mult)
                    hb = P.do("vector", dep0, "tensor_tensor", out=rt[:, 1, :, :], in0=x2, in1=sinb, op=ALU.mult)
                    hc = P.do("vector", dep0, "tensor_tensor", out=rt[:, 2, :, :], in0=x2, in1=cosb, op=ALU.mult)
                    hd = P.do("vector", dep0, "tensor_tensor", out=rt[:, 3, :, :], in0=x1, in1=sinb, op=ALU.mult)
                    ho1 = P.do("vector", [ha, hb, h_r], "tensor_tensor", out=dsto[:, :, 0:8],
                               in0=rt[:, 0, :, :], in1=rt[:, 1, :, :], op=ALU.subtract)
                    ho2 = P.do("vector", [hc, hd, h_r], "tensor_tensor", out=dsto[:, :, 8:16],
                               in0=rt[:, 2, :, :], in1=rt[:, 3, :, :], op=ALU.add)
                    h_rtfree[pb] = [ho1, ho2]
                    h_rffree[pb] = [ha, hb, hc, hd]
                    h_pqfree[pb] = [h_v, h_r, h_rf]
                    evq += [h_r, ho1, ho2]
                    evv.append(h_v)
                    if ti == TB - 1:
                        i0 = i - (TB - 1)
                        rows = slice(i0 * 128, (i0 + TB) * 128)
                        h1 = P.dma("sync", d_sq[sbi], qn[g, rows, :].rearrange("(t p) c -> p t c", p=128),
                                   stq[sbi][:, :, 0, :], evq)
                        h2 = P.dma("sync", d_sk[sbi], kn[g, rows, :].rearrange("(t p) c -> p t c", p=128),
                                   stq[sbi][:, :, 1, :], evq)
                        h3 = P.dma("sync", d_sv[sbi], vn[g, rows, :].rearrange("(t p) c -> p t c", p=128),
                                   stv[sbi][:, :, :], evv)
                        h_stfree[sbi] = [[h1, h2], h3]
                        evq, evv = [], []
                h_wqfree = h_mm
            P.barrier()
        hstack.close()

        if upto <= 2:
            gt1stack.close()
            return finish(nc, P)

        TP = 1024
        NWS = 4
        NPC = T // TP
        with ExitStack() as s3:
            XC = sb("XC", [128, T], st=s3)
            XCB = sb("XCB", [128, T], BF16, st=s3)
            HF = sb("HF", [128, T], st=s3)
            ws = []
            for i in range(NWS):
                ws.append(dict(XR=sb(f"XR{i}", [128, TP + 3], st=s3), R=sb(f"R{i}", [128, TP], st=s3),
                               Pb=sb(f"Pb{i}", [128, TP], st=s3), I=sb(f"I{i}", [128, TP], st=s3),
                               GL=sb(f"GL{i}", [128, TP], st=s3), OUT=sb(f"OUT{i}", [128, TP], BF16, st=s3)))
            Wg = [sb(f"Wg{i}", [128, 4, 128], BF16, st=s3) for i in range(2)]
            carry = sb("carry", [128, 64], st=s3)
            pg = [[ps(f"pg{i}{t}", [128, 512], st=s3) for t in range(2)] for i in range(3)]
            pc = [ps(f"pc{i}", [128, 512], st=s3) for i in range(2)]
            Dk = [sb(f"Dk{i}", [128, 4, 128], st=s3) for i in range(2)]
            h_pcfree = [None, None]
            h_dkfree = [None, None]
            ccount = 0
            d_wg = [P.dsem(f"p3wg{i}") for i in range(2)]
            d_xr = [P.dsem(f"p3xr{i}") for i in range(NWS)]
            d_gl = [P.dsem(f"p3gl{i}") for i in range(NWS)]
            d_out = [P.dsem(f"p3o{i}") for i in range(NWS)]
            h_wgz = [P.do("gpsimd", [], "memset", Wg[i][:], 0.0) for i in range(2)]
            h_wgfree = [None, None]
            h_pgfree = [None] * 3
            free = {k: [None] * NWS for k in ("XR", "R", "Pb", "I", "GL", "OUT")}
            h_xcfree = [None] * NPC
            h_xcbfree = [None] * NPC
            h_hffree = [None] * NPC
            cnt = 0
            xcnt = 0
            gcount = 0

            def gates(j, b, dirn, p, w, wi, h_wg, h_xcb_p):
                nonlocal gcount
                c0 = p * TP
                h_rs, h_is = [], []
                for s_ in range(TP // 512):
                    pb = gcount % 3
                    gcount += 1
                    P.wait("tensor", h_xcb_p, h_wg, h_pgfree[pb])
                    cols = slice(c0 + s_ * 512, c0 + (s_ + 1) * 512)
                    nc.tensor.matmul(pg[pb][0][:, :], lhsT=Wg[b][:, 2 * dirn, :], rhs=XCB[:, cols], start=True, stop=True)
                    h_mm = P.sig("tensor", nc.tensor.matmul(pg[pb][1][:, :], lhsT=Wg[b][:, 2 * dirn + 1, :],
                                                            rhs=XCB[:, cols], start=True, stop=True))
                    sl = slice(s_ * 512, (s_ + 1) * 512)
                    h_r = P.do("scalar", [h_mm, free["R"][wi]], "activation", out=w["R"][:, sl], in_=pg[pb][0][:, :],
                               func=AF.Tanh, scale=0.5, bias=hbias[:, dirn * 8 + j:dirn * 8 + j + 1])
                    h_i = P.do("scalar", [h_mm, free["I"][wi]], "activation", out=w["I"][:, sl], in_=pg[pb][1][:, :],
                               func=AF.Tanh, scale=0.5, bias=hbias[:, 16 + dirn * 8 + j:16 + dirn * 8 + j + 1])
                    h_pgfree[pb] = [h_r, h_i]
                    h_rs.append(h_r)
                    h_is.append(h_i)
                return h_rs, h_is, h_mm

            def au_front(j, dirn, p, w, wi, h_rs, offload=False):
                sc = spv[:, dirn * 8 + j:dirn * 8 + j + 1]
                hsc = hspv[:, dirn * 8 + j:dirn * 8 + j + 1]
                if offload:
                    h_a = P.do("scalar", [h_rs, h_spv], "activation", out=w["R"][:, :], in_=w["R"][:, :], func=AF.Exp,
                               scale=hsc, bias=hsc)
                    return None, h_a
                h_p1 = P.do("scalar", [h_rs, h_spv, free["Pb"][wi]], "activation", out=w["Pb"][:, :], in_=w["R"][:, :],
                            func=AF.Exp, scale=sc, bias=sc)
                h_a = P.do("scalar", [h_p1, h_rs], "activation", out=w["R"][:, :], in_=w["R"][:, :], func=AF.Exp, scale=hsc,
                           bias=hsc)
                return h_p1, h_a

            def au_sqrt(w, h_p1):
                return P.do("scalar", [h_p1], "activation", out=w["Pb"][:, :], in_=w["Pb"][:, :], func=AF.Sqrt,
                            scale=-0.25, bias=0.25)

            def au_back(p, w, h_is, h_xc_p, h_p3):
                c0 = p * TP
                h_i1 = P.do("vector", [h_is, h_xc_p], "scalar_tensor_tensor", out=w["I"][:, :], in0=w["I"][:, :], scalar=1.0,
                            in1=XC[:, c0:c0 + TP], op0=ALU.add, op1=ALU.mult)
                h_i2 = P.do("vector", [h_i1, h_p3], "tensor_tensor", out=w["I"][:, :], in0=w["I"][:, :],
                            in1=w["Pb"][:, :], op=ALU.mult)
                return h_i1, h_i2

            def schedule(n):
                ev = [("F", 0), ("F", 1), ("S", 0), ("S", 1), ("B", 0)]
                k = 2
                while k < n:
                    ev += [("F", k), ("B", k - 1), ("F", k + 1), ("S", k), ("S", k + 1), ("B", k)]
                    k += 2
                ev.append(("B", n - 1))
                return ev

            def prep_chunk(j):
                b = j % 2
                for t, (src, dirn) in enumerate(((rg_wa, 0), (rg_wx, 0), (rg_wa, 1), (rg_wx, 1))):
                    for blk in range(2):
                        h_wg_ = P.dma("gpsimd", d_wg[b], Wg[b][64 * blk:64 * blk + 64, t, 64 * blk:64 * blk + 64],
                                      src[dirn, 2 * j + blk, :, :], h_wgz[b], h_wgfree[b])
                h_dk_ = [P.do("gpsimd", [h_idf, h_const, h_dkfree[b]], "tensor_scalar", out=Dk[b][:, k, :], in0=idf[:],
                              scalar1=vecs[:, V_CONVW + k * 8 + j:V_CONVW + k * 8 + j + 1], scalar2=None, op0=ALU.mult)
                         for k in range(4)]
                return h_wg_, h_dk_

            chunk_prep = {0: prep_chunk(0)}
            chunk_prep_keep = {}
            cstate = {}
            for j in range(KC):
                b = j % 2
                rows = slice(j * 128, (j + 1) * 128)
                h_wg, h_dk = chunk_prep.pop(j)
                chunk_prep_keep[j] = (h_wg, h_dk)
                cw = [vecs[:, V_CONVW + k * 8 + j:V_CONVW + k * 8 + j + 1] for k in range(4)]
                cb = vecs[:, V_CONVB + j:V_CONVB + j + 1]
                cstate.setdefault(j, dict(h_xc=[None] * NPC, h_xcb=[None] * NPC, info={}, done=set()))
                h_xc = cstate[j]["h_xc"]
                h_xcb = cstate[j]["h_xcb"]
                info = cstate[j]["info"]
                h_scan = [None] * NPC
                d1 = j % 2
                d2 = 1 - d1
                order1 = list(range(NPC)) if d1 == 0 else list(range(NPC - 1, -1, -1))
                order2 = list(range(NPC)) if d2 == 0 else list(range(NPC - 1, -1, -1))

                def C_conv(p, jj=None, evac_eng="vector", phase=None):
                    nonlocal xcnt, ccount
                    jj = j if jj is None else jj
                    cs_ = cstate.setdefault(jj, dict(h_xc=[None] * NPC, h_xcb=[None] * NPC, info={}, done=set()))
                    pend = cs_.setdefault("pend", {})
                    if phase == "ev":
                        if p not in pend:
                            return
                        mms = pend.pop(p)
                    else:
                        if p in cs_["done"]:
                            return
                        cs_["done"].add(p)
                        mms = None
                    b = jj % 2
                    rows = slice(jj * 128, (jj + 1) * 128)
                    cb = vecs[:, V_CONVB + jj:V_CONVB + jj + 1]
                    h_dk = (chunk_prep_keep[jj] if jj in chunk_prep_keep else chunk_prep[jj])[1]
                    h_xc, h_xcb, info = cs_["h_xc"], cs_["h_xcb"], cs_["info"]
                    c0 = p * TP
                    if mms is None:
                        wi = xcnt % NWS
                        xcnt += 1
                        w = ws[wi]
                        lo = max(c0 - 2, 0)
                        hi = min(c0 + TP + 1, T)
                        off = lo - (c0 - 2)
                        h_xr = P.dma("sync", d_xr[wi], w["XR"][:, off:off + (hi - lo)], xrT[rows, lo:hi], free["XR"][wi])
                        if p == 0:
                            h_fix = P.do("gpsimd", [free["XR"][wi]], "memset", w["XR"][:, 0:2], 0.0)
                        elif p == NPC - 1:
                            h_fix = P.do("gpsimd", [free["XR"][wi]], "memset", w["XR"][:, TP + 2:TP + 3], 0.0)
                        elif p == NPC // 2 - 1:
                            h_fix = P.do("gpsimd", [h_xr], "tensor_scalar", out=w["XR"][:, TP + 2:TP + 3],
                                         in0=w["XR"][:, TP + 2:TP + 3], scalar1=conn[:, 0:1], scalar2=None, op0=ALU.mult)
                        elif p == NPC // 2:
                            h_fix = P.do("gpsimd", [h_xr], "tensor_scalar", out=w["XR"][:, 0:2], in0=w["XR"][:, 0:2],
                                         scalar1=conn[:, 0:1], scalar2=None, op0=ALU.mult)
                        else:
                            h_fix = None
                        mms = []
                        for s_ in range(TP // 512):
                            cb_ = ccount % 2
                            ccount += 1
                            P.wait("tensor", h_xr, h_fix, h_dk, h_pcfree[cb_])
                            for k in range(4):
                                mm = nc.tensor.matmul(pc[cb_][:, :], lhsT=Dk[b][:, k, :],
                                                      rhs=w["XR"][:, k + s_ * 512:k + s_ * 512 + 512], start=(k == 0),
                                                      stop=(k == 3))
                            h_cm = P.sig("tensor", mm)
                            mms.append((cb_, h_cm))
                        free["XR"][wi] = h_cm
                        h_dkfree[b] = h_cm
                        info[p] = dict()
                        if phase == "mm":
                            pend[p] = mms
                            return
                    hs_c = []
                    hs_b = []
                    for s_, (cb_, h_cm) in enumerate(mms):
                        if evac_eng == "scalar":
                            h_e = P.do("scalar", [h_cm, h_xcfree[p], h_const], "activation",
                                       out=XC[:, c0 + s_ * 512:c0 + (s_ + 1) * 512], in_=pc[cb_][:, :], func=AF.Identity,
                                       bias=cb)
                            h_e2 = P.do("scalar", [h_cm, h_xcbfree[p], h_const], "activation",
                                        out=XCB[:, c0 + s_ * 512:c0 + (s_ + 1) * 512], in_=pc[cb_][:, :], func=AF.Identity,
                                        bias=cb)
                        else:
                            h_e = P.do("vector", [h_cm, h_xcfree[p], h_const], "tensor_scalar",
                                       out=XC[:, c0 + s_ * 512:c0 + (s_ + 1) * 512], in0=pc[cb_][:, :], scalar1=cb,
                                       scalar2=None, op0=ALU.add)
                            h_e2 = P.do("vector", [h_cm, h_xcbfree[p], h_const], "tensor_scalar",
                                        out=XCB[:, c0 + s_ * 512:c0 + (s_ + 1) * 512], in0=pc[cb_][:, :], scalar1=cb,
                                        scalar2=None, op0=ALU.add)
                        h_pcfree[cb_] = [h_e, h_e2]
                        hs_c.append(h_e)
                        hs_b.append(h_e2)
                    h_xc[p] = hs_c
                    h_xcb[p] = hs_b

                def F1(k):
                    nonlocal cnt
                    p = order1[k]
                    wi = cnt % NWS
                    cnt += 1
                    w = ws[wi]
                    info[p].update(wi=wi, w=w)
                    h_rs, h_is, h_lmm = gates(j, b, d1, p, w, wi, h_wg, h_xcb[p])
                    h_p1, h_a = au_front(j, d1, p, w, wi, h_rs)
                    info[p].update(h_is=h_is, h_p1=h_p1, h_a=h_a)

                def S1(k):
                    p = order1[k]
                    info[p]["h_p3"] = au_sqrt(info[p]["w"], info[p]["h_p1"])

                def B1(k):
                    p = order1[k]
                    d_ = info.pop(p)
                    wi, w = d_["wi"], d_["w"]
                    c0 = p * TP
                    h_i1, h_i2 = au_back(p, w, d_["h_is"], h_xc[p], d_["h_p3"])
                    if k == 0:
                        init = 0.0
                        h_init = None
                    else:
                        pp = order1[k - 1]
                        src = HF[:, c0 - 1:c0] if d1 == 0 else HF[:, c0 + TP:c0 + TP + 1]
                        if (d1 == 0 and p == NPC // 2) or (d1 == 1 and p == NPC // 2 - 1):
                            cc = carry[:, j * 8:j * 8 + 1]
                            h_init = P.do("vector", [h_scan[pp]], "tensor_scalar", out=cc, in0=src, scalar1=conn[:, 0:1],
                                          scalar2=None, op0=ALU.mult)
                            init = cc
                        else:
                            init = src
                            h_init = h_scan[pp]
                    hfv = HF[:, c0:c0 + TP]
                    if d1 == 0:
                        o_, a_, u_ = hfv, w["R"][:, :], w["I"][:, :]
                    else:
                        o_, a_, u_ = hfv[:, ::-1], w["R"][:, ::-1], w["I"][:, ::-1]
                    h_scan[p] = P.do("vector", [d_["h_a"], h_i2, h_init, h_hffree[p]], "tensor_tensor_scan",
                                     out=o_, data0=a_, data1=u_, initial=init, op0=ALU.mult, op1=ALU.add)
                    free["R"][wi] = h_scan[p]
                    free["I"][wi] = h_scan[p]
                    free["Pb"][wi] = h_i2

                C_conv(order1[0])
                C_conv(order1[1])
                for (kind, k) in schedule(NPC):
                    if kind == "F":
                        F1(k)
                        if k + 2 < NPC:
                            C_conv(order1[k + 2])
                    else:
                        {"S": S1, "B": B1}[kind](k)

                if j + 1 < KC:
                    chunk_prep[j + 1] = prep_chunk(j + 1)
                bst = dict(h_bprev=None, wi_prev=None, h_lmm=None)

                def F2(k):
                    nonlocal cnt
                    p = order2[k]
                    wi = cnt % NWS
                    cnt += 1
                    w = ws[wi]
                    c0 = p * TP
                    h_gl = P.dma("sync", d_gl[wi], w["GL"][:, :], grT[rows, c0:c0 + TP], free["GL"][wi])
                    h_rs, h_is, h_lmm = gates(j, b, d2, p, w, wi, h_wg, h_xcb[p])
                    h_xcbfree[p] = h_lmm
                    bst["h_lmm"] = h_lmm
                    h_p1, h_a = au_front(j, d2, p, w, wi, h_rs, offload=True)
                    info[p] = dict(wi=wi, w=w, h_is=h_is, h_p1=h_p1, h_a=h_a, h_gl=h_gl)

                def S2(k):
                    p = order2[k]
                    d_ = info[p]
                    if d_["h_p1"] is None:
                        w_ = d_["w"]
                        d_["h_p1"] = P.do("vector", [d_["h_a"], free["Pb"][d_["wi"]]], "tensor_tensor", out=w_["Pb"][:, :],
                                          in0=w_["R"][:, :], in1=w_["R"][:, :], op=ALU.mult)
                    info[p]["h_p3"] = au_sqrt(info[p]["w"], info[p]["h_p1"])

                def B2(k):
                    p = order2[k]
                    d_ = info.pop(p)
                    wi, w = d_["wi"], d_["w"]
                    c0 = p * TP
                    h_i1, h_i2 = au_back(p, w, d_["h_is"], h_xc[p], d_["h_p3"])
                    h_xcfree[p] = h_i1
                    if k == 0:
                        init = 0.0
                        h_init = None
                    else:
                        wp = bst["wi_prev"]
                        cc = carry[:, j * 8 + 1 + (k % 7):j * 8 + 2 + (k % 7)]
                        src = ws[wp]["Pb"][:, TP - 1:TP] if d2 == 0 else ws[wp]["Pb"][:, 0:1]
                        if (d2 == 0 and p == NPC // 2) or (d2 == 1 and p == NPC // 2 - 1):
                            h_init = P.do("vector", [bst["h_bprev"], h_i2], "tensor_scalar", out=cc, in0=src,
                                          scalar1=conn[:, 0:1], scalar2=None, op0=ALU.mult)
                        else:
                            h_init = P.do("vector", [bst["h_bprev"], h_i2], "tensor_copy", out=cc, in_=src)
                        init = cc
                        free["Pb"][wp] = [free["Pb"][wp], h_init]
                    if d2 == 0:
                        o_, a_, u_ = w["Pb"][:, :], w["R"][:, :], w["I"][:, :]
                    else:
                        o_, a_, u_ = w["Pb"][:, ::-1], w["R"][:, ::-1], w["I"][:, ::-1]
                    h_bs = P.do("vector", [d_["h_a"], h_i2, h_init], "tensor_tensor_scan", out=o_, data0=a_, data1=u_,
                                initial=init, op0=ALU.mult, op1=ALU.add)
                    h_rec = P.do("gpsimd", [h_bs, h_scan[p]], "tensor_tensor", out=w["I"][:, :], in0=HF[:, c0:c0 + TP],
                                 in1=w["Pb"][:, :], op=ALU.add)
                    h_o = P.do("vector", [h_rec, d_["h_gl"], free["OUT"][wi]], "tensor_tensor", out=w["OUT"][:, :],
                               in0=w["I"][:, :], in1=w["GL"][:, :], op=ALU.mult)
                    h_st = P.dma("gpsimd", d_out[wi], rnnT[rows, c0:c0 + TP], w["OUT"][:, :], h_o)
                    free["OUT"][wi] = h_st
                    free["GL"][wi] = h_o
                    free["R"][wi] = h_bs
                    free["I"][wi] = h_o
                    bst["h_bprev"] = h_bs
                    bst["wi_prev"] = wi
                    free["Pb"][wi] = h_rec
                    h_hffree[p] = h_rec

                for (kind, k) in schedule(NPC):
                    {"F": F2, "S": S2, "B": B2}[kind](k)
                    if kind == "B" and k == 1 and j + 1 < KC:
                        nfirst_ = list(range(NPC)) if (j + 1) % 2 == 0 else list(range(NPC - 1, -1, -1))
                        C_conv(nfirst_[0], j + 1, phase="mm")
                    if kind == "S" and k == NPC - 1 and j + 1 < KC:
                        nfirst = list(range(NPC)) if (j + 1) % 2 == 0 else list(range(NPC - 1, -1, -1))
                        C_conv(nfirst[0], j + 1, evac_eng="scalar", phase="ev")
                        C_conv(nfirst[1], j + 1, evac_eng="scalar")
                h_wgfree[b] = bst["h_lmm"]
            P.barrier()

        if upto <= 3:
            gt1stack.close()
            return finish(nc, P)

        with ExitStack() as s4:
            acc = sb("acc", [128, 2, T], st=s4)
            QT = sb("QT", [128, 10240], BF16, st=s4)
            KT = sb("KT", [128, T], BF16, st=s4)
            VA = sb("VA", [128, NT, 2, 128], BF16, st=s4)
            tok = [dict(q=sb(f"tq{i}", [128, 16, 128], BF16, st=s4), k=sb(f"tk{i}", [128, 16, 128], BF16, st=s4),
                        v=sb(f"tv{i}", [128, 16, 128], BF16, st=s4)) for i in range(2)]
            mf = sb("mf", [128, 2, 128], st=s4)
            mstd = sb("mstd", [128, 2, 128], BF16, st=s4)
            mmid = sb("mmid", [128, 2, 128], BF16, st=s4)
            cm1 = sb("cm1", [128, 1], st=s4)
            pt = [sb(f"pt{i}", [128, 2, 2, 128], BF16, st=s4) for i in range(2)]
            ntmp = sb("ntmp", [128, 2048], st=s4)
            nout = [sb(f"nout{i}", [128, 2048], BF16, st=s4) for i in range(2)]
            ptr4 = [ps(f"ptr4{i}", [128, 2, 4, 128], BF16, st=s4) for i in range(2)]
            pss_ = [ps(f"pss{i}", [128, 2, 512], st=s4) for i in range(2)]
            pss = [t_[:, :, 0:256].rearrange("p h (t q) -> p h t q", q=128) for t_ in pss_]
            pso = [ps(f"pso{i}", [128, 4, 128], st=s4) for i in range(2)]
            d_tok = [P.dsem(f"p4t{i}") for i in range(2)]
            d_no = [P.dsem(f"p4n{i}") for i in range(2)]

            h_m = P.do("gpsimd", [], "memset", mf[:], 0.0)
            hm = [P.do("gpsimd", [h_m], "affine_select", out=mf[:, 0, :], in_=mf[:, 0, :], pattern=[[-1, 128]],
                       compare_op=ALU.is_ge, fill=-30000.0, base=0, channel_multiplier=1),
                  P.do("gpsimd", [h_m], "affine_select", out=mf[:, 1, :], in_=mf[:, 1, :], pattern=[[1, 128]],
                       compare_op=ALU.is_ge, fill=-30000.0, base=0, channel_multiplier=-1)]
            h_ms1 = P.do("gpsimd", hm, "tensor_copy", out=mstd[:], in_=mf[:])
            h_cm = P.do("gpsimd", [h_const], "tensor_scalar", out=cm1[:], in0=conn[:], scalar1=-1.0, scalar2=30000.0,
                        op0=ALU.add, op1=ALU.mult)
            hm2 = [P.do("gpsimd", hm + [h_cm, h_ms1], "tensor_scalar", out=mf[:, 0, 64:128], in0=mf[:, 0, 64:128],
                        scalar1=cm1[:, 0:1], scalar2=None, op0=ALU.add),
                   P.do("gpsimd", hm + [h_cm, h_ms1], "tensor_scalar", out=mf[:, 1, 0:64], in0=mf[:, 1, 0:64],
                        scalar1=cm1[:, 0:1], scalar2=None, op0=ALU.add)]
            h_ms2 = P.do("gpsimd", hm2 + [h_ms1], "tensor_copy", out=mmid[:], in_=mf[:])
            h_masks = [h_ms1, h_ms2, h_idb]
            h_va1 = [P.do("gpsimd", [], "memset", VA[:, :, 0, 64:128], 1.0),
                     P.do("gpsimd", [], "memset", VA[:, :, 1, 0:64], 1.0)]

            pgs = [(p, g) for p in range(4) for g in range(3)]

            def geom(g):
                d = GROUPS[g][1]
                n = NT // d
                RL = T // d
                return d, n, RL, RL + 128

            qt_rd = [None] * 20
            kt_rd = [None] * 16
            va_rd = [None] * 16
            ready = {}
            st4 = dict(tokfree=[None, None], trfree=[None, None], pssfree=[None, None], ptfree=[None, None],
                       psofree=[None, None], accfree=None, ntmpfree=None, noutfree=[None, None], lastacc=None,
                       npiece=0, nbatch=0, nblk=0, nnorm=0)

            def prep_gen(idx):
                p, g = pgs[idx]
                d, n, RL, QS = geom(g)
                qv = qn[g].rearrange("(m r) c -> r m c", r=d)
                kv = kn[g].rearrange("(m r) c -> r m c", r=d)
                vv = vn[g].rearrange("(m r) c -> r m c", r=d)
                rd = dict(h_qt=[None] * NT, h_kt=[None] * NT, h_vt=[None] * NT, h_qz=None)
                ready[idx] = rd
                for c in range(4):
                    wi = st4["npiece"] % 2
                    st4["npiece"] += 1
                    tk = tok[wi]
                    for r in range(d):
                        t_lo = max(16 * c, r * n)
                        t_hi = min(16 * c + 16, (r + 1) * n)
                        if t_lo >= t_hi:
                            continue
                        lt0 = t_lo - r * n
                        cntt = t_hi - t_lo
                        to = t_lo - 16 * c
                        for (srcv, dstt) in ((qv, tk["q"]), (kv, tk["k"]), (vv, tk["v"])):
                            h_ld = P.dma("sync", d_tok[wi], dstt[:, to:to + cntt, :],
                                         srcv[r, lt0 * 128:(lt0 + cntt) * 128, p * 128:(p + 1) * 128]
                                         .rearrange("(t q) c -> q t c", q=128), st4["tokfree"][wi])
                    hfree = []
                    for b4 in range(4):
                        pbq = st4["nbatch"] % 2
                        st4["nbatch"] += 1
                        ti0 = 16 * c + 4 * b4
                        r = ti0 // n
                        lt = ti0 % n
                        P.wait("tensor", h_ld, h_idb, st4["trfree"][pbq])
                        for t in range(4):
                            nc.tensor.transpose(ptr4[pbq][:, 0, t, :], tk["q"][:, b4 * 4 + t, :], idb[:])
                        for t in range(4):
                            tr = nc.tensor.transpose(ptr4[pbq][:, 1, t, :], tk["k"][:, b4 * 4 + t, :], idb[:])
                        h_tr = P.sig("tensor", tr)
                        qc = r * QS + 64 + lt * 128
                        kc = r * RL + lt * 128
                        dq = [qt_rd[qc // 512], qt_rd[(qc + 511) // 512]]
                        dk = [kt_rd[kc // 512]]
                        if pbq == 0:
                            h_eq = P.do("scalar", [h_tr] + dq, "activation", out=QT[:, qc:qc + 512],
                                        in_=ptr4[pbq][:, 0, :, :], func=AF.Copy)
                            h_ek = P.do("scalar", [h_tr] + dk, "activation", out=KT[:, kc:kc + 512],
                                        in_=ptr4[pbq][:, 1, :, :], func=AF.Copy)
                        else:
                            h_eq = P.do("vector", [h_tr] + dq, "tensor_copy", out=QT[:, qc:qc + 512],
                                        in_=ptr4[pbq][:, 0, :, :])
                            h_ek = P.do("vector", [h_tr] + dk, "tensor_copy", out=KT[:, kc:kc + 512],
                                        in_=ptr4[pbq][:, 1, :, :])
                        st4["trfree"][pbq] = [h_eq, h_ek]
                        for t in range(4):
                            rd["h_qt"][ti0 + t] = h_eq
                            rd["h_kt"][ti0 + t] = h_ek
                        hfree.append(h_tr)
                        yield
                    dv = [va_rd[4 * c + x] for x in range(4)]
                    h_v0 = P.do("gpsimd", [h_ld, h_va1] + dv, "tensor_copy", out=VA[:, 16 * c:16 * c + 16, 0, 0:64],
                                in_=tk["v"][:, :, 0:64])
                    h_v1 = P.do("gpsimd", [h_ld, h_va1] + dv, "tensor_copy", out=VA[:, 16 * c:16 * c + 16, 1, 64:128],
                                in_=tk["v"][:, :, 64:128])
                    for t in range(16):
                        rd["h_vt"][16 * c + t] = [h_v0, h_v1]
                    st4["tokfree"][wi] = hfree + [h_v0, h_v1]
                    yield

            def prep_reqs(idx_next, idx_cur):
                _, g2 = pgs[idx_next]
                d2, n2, RL2, QS2 = geom(g2)
                _, g1 = pgs[idx_cur]
                d1, n1, RL1, QS1 = geom(g1)
                lastq = [-1] * 20
                lastk = [-1] * 16
                bi = 0
                for r in range(d1):
                    for i in range(-1, n1):
                        qcol = r * QS1 + 128 * (i + 1)
                        for ch in (qcol // 512, (qcol + 127) // 512):
                            lastq[ch] = bi
                        for tt_ in (i, i + 1):
                            if 0 <= tt_ <= n1 - 1:
                                lastk[(r * n1 + tt_) // 4] = bi
                        bi += 1
                reqs = []
                for c in range(4):
                    for b4 in range(4):
                        ti0 = 16 * c + 4 * b4
                        r = ti0 // n2
                        lt = ti0 % n2
                        qc = r * QS2 + 64 + lt * 128
                        kc = r * RL2 + lt * 128
                        reqs.append(max(lastq[qc // 512], lastq[(qc + 511) // 512], lastk[kc // 512]))
                    reqs.append(max(lastk[4 * c + x] for x in range(4)))
                return reqs

            def pads(idx):
                p, g = pgs[idx]
                d, n, RL, QS = geom(g)
                qv_ = QT[:, 0:d * QS].rearrange("p (r c) -> p r c", c=QS)
                deps = [x for x in qt_rd]
                ready[idx]["h_qz"] = [P.do("gpsimd", deps, "memset", qv_[:, :, 0:64], 0.0),
                                      P.do("gpsimd", deps, "memset", qv_[:, :, QS - 64:QS], 0.0)]

            def run_blocks(idx, nxt, nxt_reqs):
                p, g = pgs[idx]
                d, n, RL, QS = geom(g)
                rd = ready[idx]
                blocks = [(r, i) for r in range(d) for i in range(-1, n)]
                nb_ = len(blocks)
                state = {}
                step = [0]

                def emit_S(bi):
                    r, i = blocks[bi]
                    sbi = (st4["nblk"] + bi) % 2
                    hasA = i >= 0
                    hasB = i + 1 <= n - 1
                    qcol = r * QS + 128 * (i + 1)
                    deps = [st4["pssfree"][sbi], h_masks]
                    tiles = [tt_ for tt_ in (i, i + 1) if 0 <= tt_ <= n - 1]
                    for tt_ in tiles:
                        deps += [rd["h_qt"][r * n + tt_], rd["h_kt"][r * n + tt_]]
                    if i == -1 or i == n - 1:
                        deps.append(rd["h_qz"])
                    P.wait("tensor", deps)
                    M = mmid if i == n // 2 - 1 else mstd
                    mm = None
                    for t_ in ((0,) if hasA else ()) + ((1,) if hasB else ()):
                        kcol = r * RL + 128 * (i + t_)
                        for hh in range(2):
                            rows_ = slice(64 * hh, 64 * hh + 64)
                            nc.tensor.matmul(pss[sbi][:, hh, t_, :], lhsT=KT[rows_, kcol:kcol + 128],
                                             rhs=QT[rows_, qcol:qcol + 128], start=True, stop=False)
                        for hh in range(2):
                            mm = nc.tensor.matmul(pss[sbi][:, hh, t_, :], lhsT=idb[:], rhs=M[:, t_, :], start=False, stop=True)
                    h_s = P.sig("tensor", mm)
                    for ch in (qcol // 512, (qcol + 127) // 512):
                        qt_rd[ch] = h_s
                    for tt_ in tiles:
                        kt_rd[(r * n + tt_) // 4] = h_s
                    if hasA and hasB:
                        sv = lambda a: a[:, :, :, :]
                    elif hasB:
                        sv = lambda a: a[:, :, 1, :]
                    else:
                        sv = lambda a: a[:, :, 0, :]
                    h_e = P.do("scalar", [h_s, st4["ptfree"][sbi]], "activation", out=sv(pt[sbi]), in_=sv(pss[sbi]),
                               func=AF.Exp, scale=0.125)
                    st4["pssfree"][sbi] = h_e
                    state[bi] = h_e

                def emit_PV(bi):
                    r, i = blocks[bi]
                    sbi = (st4["nblk"] + bi) % 2
                    hasA = i >= 0
                    hasB = i + 1 <= n - 1
                    h_e = state.pop(bi)
                    deps = [h_e, st4["psofree"][sbi]]
                    tiles = [tt_ for tt_ in (i, i + 1) if 0 <= tt_ <= n - 1]
                    for tt_ in tiles:
                        deps.append(rd["h_vt"][r * n + tt_])
                    P.wait("tensor", deps)
                    mm = None
                    for hh in range(2):
                        if hasA:
                            mm = nc.tensor.matmul(pso[sbi][:, hh, :], lhsT=VA[:, r * n + i, hh, :],
                                                  rhs=pt[sbi][:, hh, 0, :], start=True, stop=not hasB)
                        if hasB:
                            mm = nc.tensor.matmul(pso[sbi][:, hh, :], lhsT=VA[:, r * n + i + 1, hh, :],
                                                  rhs=pt[sbi][:, hh, 1, :], start=not hasA, stop=True)
                    h_pv = P.sig("tensor", mm)
                    st4["ptfree"][sbi] = h_pv
                    for tt_ in tiles:
                        va_rd[(r * n + tt_) // 4] = h_pv
                    lo = 64 if i == -1 else 0
                    hi = 64 if i == n - 1 else 128
                    m_lo = 128 * i + 64 + lo
                    m_hi = 128 * i + 64 + hi
                    accv = acc[:, :, :].rearrange("p h (m r) -> p h r m", r=d)[:, :, r, m_lo:m_hi]
                    if g == 0:
                        h_u = P.do("vector", [h_pv, st4["accfree"]], "tensor_copy", out=accv, in_=pso[sbi][:, 0:2, lo:hi])
                    else:
                        h_u = P.do("vector", [h_pv, st4["lastacc_prev"]], "tensor_tensor", out=accv, in0=accv,
                                   in1=pso[sbi][:, 0:2, lo:hi], op=ALU.add)
                    st4["psofree"][sbi] = h_u
                    st4["lastacc"] = h_u

                for bi in range(nb_ + 1):
                    if bi < nb_:
                        emit_S(bi)
                    if bi >= 1:
                        emit_PV(bi - 1)
                    if nxt is not None and P4MODE != "nointerleave":
                        budget = 1
                        while budget > 0 and step[0] < len(nxt_reqs) and nxt_reqs[step[0]] <= bi - 1:
                            next(nxt)
                            step[0] += 1
                            budget -= 1
                if nxt is not None:
                    for _ in nxt:
                        pass
                st4["nblk"] += nb_
                st4["lastacc_prev"] = st4["lastacc"]

            st4["lastacc_prev"] = None
            g0 = prep_gen(0)
            for _ in g0:
                pass
            pads(0)
            for idx in range(len(pgs)):
                p, g = pgs[idx]
                if idx + 1 < len(pgs):
                    nxt = prep_gen(idx + 1)
                    reqs = prep_reqs(idx + 1, idx)
                else:
                    nxt, reqs = None, None
                run_blocks(idx, nxt, reqs)
                if idx + 1 < len(pgs):
                    pads(idx + 1)
                if g == 2:
                    hn = []
                    la = st4["lastacc"]
                    for c in range(4):
                        cols = slice(c * 2048, (c + 1) * 2048)
                        oi = st4["nnorm"] % 2
                        st4["nnorm"] += 1
                        h1 = P.do("scalar", [la, st4["ntmpfree"]], "activation", out=ntmp[0:64, :], in_=acc[64:128, 0, cols],
                                  func=AF.Ln)
                        h1 = P.do("scalar", [h1], "activation", out=ntmp[0:64, :], in_=ntmp[0:64, :], func=AF.Exp, scale=-1.0)
                        h2 = P.do("scalar", [la, st4["ntmpfree"]], "activation", out=ntmp[64:128, :], in_=acc[0:64, 1, cols],
                                  func=AF.Ln)
                        h2 = P.do("scalar", [h2], "activation", out=ntmp[64:128, :], in_=ntmp[64:128, :], func=AF.Exp, scale=-1.0)
                        h3 = P.do("vector", [h1, st4["noutfree"][oi]], "tensor_tensor", out=nout[oi][0:64, :],
                                  in0=acc[0:64, 0, cols], in1=ntmp[0:64, :], op=ALU.mult)
                        h4 = P.do("vector", [h2, st4["noutfree"][oi]], "tensor_tensor", out=nout[oi][64:128, :],
                                  in0=acc[64:128, 1, cols], in1=ntmp[64:128, :], op=ALU.mult)
                        st4["ntmpfree"] = [h3, h4]
                        st4["noutfree"][oi] = P.dma("sync", d_no[oi], attT[p * 128:(p + 1) * 128, cols], nout[oi][:, :], h3, h4)
                        hn += [h3, h4]
                    st4["accfree"] = hn
            P.barrier()

        if upto <= 4:
            gt1stack.close()
            return finish(nc, P)

        CH = 512
        NCH = T // CH
        rnn_v = rnnT.rearrange("(k p) t -> p k t", p=128)
        att_v = attT.rearrange("(k p) t -> p k t", p=128)
        sg_v = sgT.rearrange("(a k p) t -> p k a t", p=128, a=2)
        h2_v = h2T.rearrange("(k p) t -> p k t", p=128)
        with ExitStack() as s5:
            Wr = sb("Wr", [128, KC, D], BF16, st=s5)
            Wa = sb("Wa", [128, 4, D], BF16, st=s5)
            Wo = sb("Wo", [128, KC, D], BF16, st=s5)
            ra = [dict(rnn=sb(f"c_rnn{i}", [128, KC, CH], BF16, st=s5), att=sb(f"c_att{i}", [128, 4, CH], BF16, st=s5))
                  for i in range(2)]
            xb = [sb(f"c_x{i}", [128, 4, D], st=s5) for i in range(2)]
            gring = sb("c_g", [128, KC, 2, CH], st=s5)
            merged = sb("merged", [128, KC, CH], BF16, st=s5)
            mt1 = [sb(f"mt1{i}", [128, CH], st=s5) for i in range(2)]
            mt2 = [sb(f"mt2{i}", [128, CH], st=s5) for i in range(2)]
            mt3 = [sb(f"mt3{i}", [128, CH], st=s5) for i in range(2)]
            xn5 = [sb(f"xn5{i}", [128, D], st=s5) for i in range(2)]
            h2st = [sb(f"h2st{i}", [128, KC, CH], BF16, st=s5) for i in range(2)]
            junk5 = sb("junk5", [128, D], BF16, st=s5)
            ss5 = sb("ss5", [128, NT], st=s5)
            rs5 = sb("rs5", [128, NT], st=s5)
            pa = [ps(f"pa{i}", [128, 512], st=s5) for i in range(2)]
            pbb = [ps(f"pbb{i}", [128, 512], st=s5) for i in range(2)]
            pmx = [ps(f"pmx{i}", [128, 512], st=s5) for i in range(2)]
            ptr5 = ps("ptr5", [128, KC, 128], st=s5)
            d_w5 = P.dsem("p5w")
            d_ra = [P.dsem(f"p5l{i}") for i in range(2)]
            d_xl = [P.dsem(f"p5xl{i}") for i in range(2)]
            d_g = [P.dsem(f"p5g{i}") for i in range(KC)]
            d_h2 = [P.dsem(f"p5h{i}") for i in range(2)]
            d_x1 = [P.dsem(f"p5x{i}") for i in range(2)]
            P.dma("gpsimd", d_w5, Wr[:], w_br_rnn.rearrange("(k p) n -> p k n", p=128))
            P.dma("gpsimd", d_w5, Wa[:], w_br_attn.rearrange("(k p) n -> p k n", p=128))
            h_w5 = P.dma("gpsimd", d_w5, Wo[:], w_out.rearrange("(k p) n -> p k n", p=128))
            st5 = dict(rafree=[None, None], xfree=[None, None], gfree=[None] * KC, pafree=[None, None],
                       mt12free=[None, None], pmxfree=[None, None], mt3free=[None, None], mergedfree=None,
                       ptr5free=None, xn5free=[None, None], h2stfree=[None, None], junk=None, na=0, nm=0)
            h_g = [None] * KC
            h_ra = [None, None]
            h_xl = [None, None]
            hx1 = {}
            hev = {}

            def load_ra(c):
                wi = c % 2
                cols = slice(c * CH, (c + 1) * CH)
                P.dma("sync", d_ra[wi], ra[wi]["rnn"][:], rnn_v[:, :, cols], st5["rafree"][wi])
                h_ra[wi] = P.dma("sync", d_ra[wi], ra[wi]["att"][:], att_v[:, :, cols], st5["rafree"][wi])

            def load_x(c):
                wi = c % 2
                cols = slice(c * CH, (c + 1) * CH)
                h_xl[wi] = P.dma("sync", d_xl[wi], xb[wi][:], x_in[cols, :].rearrange("(t p) c -> p t c", p=128),
                                 st5["xfree"][wi])

            def load_g(c, oc):
                cols = slice(c * CH, (c + 1) * CH)
                h_g[oc] = P.dma("sync", d_g[oc], gring[:, oc, :, :], sg_v[:, oc, :, cols], st5["gfree"][oc])

            def A_step(c, oc):
                wi = c % 2
                ab = st5["na"] % 2
                st5["na"] += 1
                ocs = slice(oc * 128, (oc + 1) * 128)
                P.wait("tensor", h_ra[wi], h_w5, st5["pafree"][ab])
                for k in range(KC):
                    nc.tensor.matmul(pa[ab][:, :], lhsT=Wr[:, k, ocs], rhs=ra[wi]["rnn"][:, k, :], start=(k == 0),
                                     stop=(k == KC - 1))
                for k in range(4):
                    mm = nc.tensor.matmul(pbb[ab][:, :], lhsT=Wa[:, k, ocs], rhs=ra[wi]["att"][:, k, :], start=(k == 0),
                                          stop=(k == 3))
                h_mm = P.sig("tensor", mm)
                h1 = P.do("vector", [h_mm, h_g[oc], st5["mt12free"][ab]], "tensor_tensor", out=mt1[ab][:, :], in0=pa[ab][:, :],
                          in1=gring[:, oc, 0, :], op=ALU.mult)
                h2 = P.do("vector", [h_mm, h_g[oc], st5["mt12free"][ab]], "tensor_tensor", out=mt2[ab][:, :], in0=pbb[ab][:, :],
                          in1=gring[:, oc, 1, :], op=ALU.mult)
                st5["pafree"][ab] = [h1, h2]
                st5["gfree"][oc] = [h1, h2]
                h3 = P.do("gpsimd", [h1, h2, st5["mergedfree"]], "tensor_tensor", out=merged[:, oc, :], in0=mt1[ab][:, :],
                          in1=mt2[ab][:, :], op=ALU.add)
                st5["mt12free"][ab] = h3
                if c + 1 < NCH:
                    load_g(c + 1, oc)
                return h_mm, h3

            def B_phase(c, hm_):
                wi = c % 2
                seg = c // (NCH // 2)
                hx1[c] = [[None, None] for _ in range(4)]
                for t in range(4):
                    for half in range(2):
                        mb = st5["nm"] % 2
                        st5["nm"] += 1
                        hs_ = slice(half * 512, (half + 1) * 512)
                        P.wait("tensor", hm_, st5["pmxfree"][mb])
                        for k in range(KC):
                            mm = nc.tensor.matmul(pmx[mb][:, :], lhsT=merged[:, k, t * 128:(t + 1) * 128], rhs=Wo[:, k, hs_],
                                                  start=(k == 0), stop=(k == KC - 1))
                        h_mm = P.sig("tensor", mm)
                        h1 = P.do("vector", [h_mm, st5["mt3free"][mb], h_gt], "tensor_tensor", out=mt3[mb][:, :],
                                  in0=pmx[mb][:, :], in1=gt1row[:, seg, hs_], op=ALU.mult)
                        st5["pmxfree"][mb] = h1
                        h2 = P.do("gpsimd", [h1, h_xl[wi]], "tensor_tensor", out=xb[wi][:, t, hs_], in0=xb[wi][:, t, hs_],
                                  in1=mt3[mb][:, :], op=ALU.add)
                        st5["mt3free"][mb] = h2
                        hx1[c][t][half] = h2
                st5["mergedfree"] = h_mm

            hxn = {}

            def C_norm(c, t):
                wi = c % 2
                i = c * 4 + t
                xb_ = t % 2
                st5["junk"], h_rs = norm_tile(xb[wi][:, t, :], ss5[:, i:i + 1], rs5[:, i:i + 1], None, junk5, hx1[c][t],
                                              st5["junk"])
                hxn[(c, t)] = P.do("vector", [h_rs, st5["xn5free"][xb_]], "tensor_scalar", out=xn5[xb_][:, :],
                                   in0=xb[wi][:, t, :], scalar1=rs5[:, i:i + 1], scalar2=None, op0=ALU.mult)

            def C_trans(c, t):
                wi = c % 2
                seg = c // (NCH // 2)
                xb_ = t % 2
                hst = h2st[wi]
                P.wait("tensor", hxn.pop((c, t)), h_idf, st5["ptr5free"])
                for k in range(KC):
                    tr = nc.tensor.transpose(ptr5[:, k, :], xn5[xb_][:, k * 128:(k + 1) * 128], idf[:])
                h_tr = P.sig("tensor", tr)
                st5["xn5free"][xb_] = h_tr
                hs2 = []
                for k in range(KC):
                    hs2.append(P.do("scalar", [h_tr, h_ss, st5["h2stfree"][wi]], "activation",
                                    out=hst[:, k, t * 128:(t + 1) * 128], in_=ptr5[:, k, :], func=AF.Identity,
                                    scale=scale2[:, k, seg:seg + 1], bias=shift2[:, k, seg:seg + 1]))
                st5["ptr5free"] = hs2
                hev.setdefault(c, [])
                hev[c] += hs2
                if t == 3:
                    cols = slice(c * CH, (c + 1) * CH)
                    st5["h2stfree"][wi] = P.dma("scalar", d_h2[wi], h2_v[:, :, cols], hst[:, :, :], hev[c])
                    st5["xfree"][wi] = P.dma("scalar", d_x1[wi], x1s[cols, :].rearrange("(t p) c -> p t c", p=128),
                                             xb[wi][:, :, :], hev[c])
                    if c + 2 < NCH:
                        load_x(c + 2)

            load_ra(0)
            load_x(0)
            load_x(1)
            for oc in range(KC):
                load_g(0, oc)
            load_ra(1)
            for c in range(NCH + 1):
                if c < NCH:
                    hm_ = []
                    if c >= 1:
                        C_norm(c - 1, 0)
                    for oc in range(KC):
                        h_mm, h3 = A_step(c, oc)
                        hm_.append(h3)
                        if c >= 1 and oc % 2 == 1:
                            C_trans(c - 1, oc // 2)
                            if oc // 2 + 1 < 4:
                                C_norm(c - 1, oc // 2 + 1)
                    st5["rafree"][c % 2] = h_mm
                    if c + 2 < NCH:
                        load_ra(c + 2)
                    B_phase(c, hm_)
                else:
                    for t in range(4):
                        C_norm(c - 1, t)
                        C_trans(c - 1, t)
            P.barrier()

        if upto <= 5:
            gt1stack.close()
            return finish(nc, P)

        gt1stack.close()
        CF = 512
        NCF = T // CF
        with ExitStack() as s6:
            W1 = sb("W1", [128, KC, 2 * D_FF], BF16, st=s6)
            W2 = sb("W2", [128, FC, D], BF16, st=s6)
            h2c_ = [sb(f"h2c{i}", [128, KC, CF], BF16, st=s6) for i in range(2)]
            x1t = [sb(f"x1t{i}", [128, D], st=s6) for i in range(2)]
            hid = sb("hid", [128, FC, CF], BF16, st=s6)
            sgt = [sb(f"sgt{i}", [128, CF], st=s6) for i in range(2)]
            ft = [sb(f"ft{i}", [128, 512], st=s6) for i in range(2)]
            junk6 = sb("junk6", [128, D], BF16, st=s6)
            ss6 = sb("ss6", [128, NT], st=s6)
            rs6 = sb("rs6", [128, NT], st=s6)
            pgg = [ps(f"pgg{i}", [128, 512], st=s6) for i in range(2)]
            puu = [ps(f"puu{i}", [128, 512], st=s6) for i in range(2)]
            poo = [ps(f"poo{i}", [128, 512], st=s6) for i in range(2)]
            d_w1 = P.dsem("p6w1")
            d_w2 = P.dsem("p6w2")
            d_l6 = [P.dsem(f"p6l{i}") for i in range(2)]
            d_xt = [P.dsem(f"p6x{i}") for i in range(2)]
            w1v = w_ffn_in.rearrange("(k p) n -> p k n", p=128)
            for k in range(KC):
                h_w1 = P.dma("gpsimd", d_w1, W1[:, k, :], w1v[:, k, :])
            w2v = w_ffn_out.rearrange("(f p) n -> p f n", p=128)
            for f0 in range(0, FC, 2):
                h_w2 = P.dma("gpsimd", d_w2, W2[:, f0:f0 + 2, :], w2v[:, f0:f0 + 2, :])
            h_h2free = [None, None]
            h_xtfree = [None] * 2
            h_pgfree6 = [None, None]
            h_sgtfree = [None, None]
            h_poofree = [None, None]
            h_ftfree = [None, None]
            h_hidfree = None
            h_junk6 = None
            nf = 0
            no = 0
            nx = 0
            for c in range(NCF):
                seg = c // (NCF // 2)
                cols = slice(c * CF, (c + 1) * CF)
                h2c = h2c_[c % 2]
                if c == 0:
                    h_lh_next = P.dma("sync", d_l6[0], h2c_[0][:], h2_v[:, :, cols], h_h2free[0])
                h_lh = h_lh_next
                if c + 1 < NCF:
                    h_lh_next = P.dma("sync", d_l6[(c + 1) % 2], h2c_[(c + 1) % 2][:],
                                      h2_v[:, :, slice((c + 1) * CF, (c + 2) * CF)], h_h2free[(c + 1) % 2])
                hh_ = []
                for f in range(FC):
                    gb = nf % 2
                    nf += 1
                    P.wait("tensor", h_lh, h_w1, h_pgfree6[gb])
                    for k in range(KC):
                        nc.tensor.matmul(pgg[gb][:, :], lhsT=W1[:, k, f * 128:(f + 1) * 128], rhs=h2c[:, k, :],
                                         start=(k == 0), stop=(k == KC - 1))
                    for k in range(KC):
                        mm = nc.tensor.matmul(puu[gb][:, :], lhsT=W1[:, k, D_FF + f * 128:D_FF + (f + 1) * 128],
                                              rhs=h2c[:, k, :], start=(k == 0), stop=(k == KC - 1))
                    h_mm = P.sig("tensor", mm)
                    h1 = P.do("scalar", [h_mm, h_sgtfree[gb]], "activation", out=sgt[gb][:, :], in_=pgg[gb][:, :], func=AF.Silu)
                    h2 = P.do("vector", [h1, h_mm, h_hidfree], "tensor_tensor", out=hid[:, f, :], in0=puu[gb][:, :],
                              in1=sgt[gb][:, :], op=ALU.mult)
                    h_pgfree6[gb] = [h1, h2]
                    h_sgtfree[gb] = h2
                    hh_.append(h2)
                h_h2free[c % 2] = h_mm
                for t in range(4):
                    i = c * 4 + t
                    xi = nx % 2
                    nx += 1
                    rows_ = slice(i * 128, (i + 1) * 128)
                    h_lx = P.dma("sync", d_xt[xi], x1t[xi][:, :], x1s[rows_, :], h_xtfree[xi])
                    hx2 = []
                    for half in range(2):
                        ob = no % 2
                        no += 1
                        hs_ = slice(half * 512, (half + 1) * 512)
                        P.wait("tensor", hh_, h_w2, h_poofree[ob])
                        for f in range(FC):
                            mm = nc.tensor.matmul(poo[ob][:, :], lhsT=hid[:, f, t * 128:(t + 1) * 128], rhs=W2[:, f, hs_],
                                                  start=(f == 0), stop=(f == FC - 1))
                        h_mm = P.sig("tensor", mm)
                        h1 = P.do("vector", [h_mm, h_ftfree[ob], h_gt], "tensor_tensor", out=ft[ob][:, :], in0=poo[ob][:, :],
                                  in1=gt2row[:, seg, hs_], op=ALU.mult)
                        h_poofree[ob] = h1
                        h2 = P.do("gpsimd", [h1, h_lx], "tensor_tensor", out=x1t[xi][:, hs_], in0=x1t[xi][:, hs_],
                                  in1=ft[ob][:, :], op=ALU.add)
                        h_ftfree[ob] = h2
                        hx2.append(h2)
                    h_junk6, h_rs = norm_tile(x1t[xi][:, :], ss6[:, i:i + 1], rs6[:, i:i + 1], None, junk6, hx2, h_junk6)
                    hy = P.do("vector", [h_rs, h_const], "scalar_tensor_tensor", out=x1t[xi][:, :], in0=x1t[xi][:, :],
                              scalar=rs6[:, i:i + 1], in1=fgrow[:, :], op0=ALU.mult, op1=ALU.mult)
                    h_xtfree[xi] = P.dma("sync", d_xt[xi], y_out[rows_, :], x1t[xi][:, :], hy)
                h_hidfree = h_mm
            P.barrier()

        return finish(nc, P)


def finish(nc, P):
    P.barrier()
    return nc


def _core_inputs(core, inp):
    if core < 4:
        x = np.ascontiguousarray(inp["x_prompt"][core])
        c2 = np.stack([inp["c_prompt"][core], inp["c_prompt"][core]], 0)
        connv = 1.0
        pos = np.arange(T)
    else:
        a, b = 2 * (core - 4), 2 * (core - 4) + 1
        x = np.ascontiguousarray(np.concatenate([inp["x_sample"][a], inp["x_sample"][b]], 0))
        c2 = np.stack([inp["c_sample"][a], inp["c_sample"][b]], 0)
        connv = 0.0
        pos = np.concatenate([np.arange(SEG), np.arange(SEG)])
    return x, c2, connv, pos


def _fm(v):
    return np.ascontiguousarray(np.asarray(v, np.float32).reshape(-1, 128).T)


def _shared_inputs(inp):
    vecs = np.zeros((128, NV), np.float32)
    vecs[:, V_BADA:V_BADA + 48] = _fm(inp["b_ada"][0])
    vecs[:, V_N1G:V_N1G + 8] = _fm(inp["norm1_g"][0])
    vecs[:, V_N2G:V_N2G + 8] = _fm(inp["norm2_g"][0])
    for k in range(4):
        vecs[:, V_CONVW + k * 8:V_CONVW + k * 8 + 8] = _fm(inp["conv_w"][0, k])
    vecs[:, V_CONVB:V_CONVB + 8] = _fm(inp["conv_b"][0])
    for d in range(2):
        vecs[:, V_BA + d * 8:V_BA + d * 8 + 8] = _fm(inp["rg_ba"][0, d].reshape(-1))
        vecs[:, V_BX + d * 8:V_BX + d * 8 + 8] = _fm(inp["rg_bx"][0, d].reshape(-1))
        vecs[:, V_LAM + d * 8:V_LAM + d * 8 + 8] = _fm(inp["rg_lambda"][0, d])
    sh = {
        "vecs": vecs,
        "fgrow": np.ascontiguousarray(inp["final_g"].reshape(1, D).astype(np.float32)),
        "badarow": np.ascontiguousarray(inp["b_ada"][0].reshape(1, 6 * D).astype(np.float32)),
        "w_ada": np.ascontiguousarray(inp["w_ada"][0]),
        "w_in": np.ascontiguousarray(inp["w_in"][0]),
        "rg_wa": np.ascontiguousarray(inp["rg_wa"][0]),
        "rg_wx": np.ascontiguousarray(inp["rg_wx"][0]),
        "w_br_rnn": np.ascontiguousarray(inp["w_br_rnn"][0]),
        "w_br_attn": np.ascontiguousarray(inp["w_br_attn"][0]),
        "w_out": np.ascontiguousarray(inp["w_out"][0]),
        "w_ffn_in": np.ascontiguousarray(inp["w_ffn_in"][0]),
        "w_ffn_out": np.ascontiguousarray(inp["w_ffn_out"][0]),
    }
    return sh


def _rope_table(pos):
    inv = (ROPE_THETA ** (-(np.arange(0, 16, 2, dtype=np.float32) / np.float32(16)))).astype(np.float32)
    ang = pos.astype(np.float32)[:, None] * inv[None, :]
    tab = np.concatenate([np.cos(ang), np.sin(ang)], -1).astype(np.float32)
    return np.ascontiguousarray(tab.reshape(NT, 128, 16).transpose(1, 0, 2))


def make_in_maps(inp):
    inp = {k: np.asarray(v) for k, v in inp.items()}
    sh = _shared_inputs(inp)
    maps = []
    for core in range(8):
        x, c2, connv, pos = _core_inputs(core, inp)
        m = dict(sh)
        m["x"] = x
        m["cT"] = np.ascontiguousarray(c2.astype(np.float32).reshape(2, KC, 128).transpose(2, 1, 0))
        m["conn"] = np.full((128, 1), connv, np.float32)
        m["rope"] = _rope_table(pos)
        maps.append(m)
    return maps


def kernel(**inputs):
    nc = build_program()
    maps = make_in_maps(inputs)
    res = run_bass_kernel_spmd(nc, maps, core_ids=list(range(8)))
    ys = [np.asarray(r["y"], np.float32) for r in res.results]
    y_prompt = np.stack(ys[0:4], 0)
    y_sample = np.stack([ys[4 + i // 2][(i % 2) * SEG:(i % 2 + 1) * SEG] for i in range(8)], 0)
    return (y_prompt, y_sample)
```

```python
import numpy as np
import concourse.bass as bass
import concourse.mybir as mybir
from concourse.bass_utils import run_bass_kernel_spmd
from concourse.alu_op_type import AluOpType as ALU
from contextlib import ExitStack

F32 = mybir.dt.float32
BF16 = mybir.dt.bfloat16
AF = mybir.ActivationFunctionType

D = 1024
T = 8192
SEG = 4096
NT = T // 128
KC = D // 128
IN_COLS = 8704
D_FF = 2816
FC = D_FF // 128
EPS = 1e-6
GROUPS = ((128, 1), (512, 4), (2048, 16))
ROPE_THETA = 500000.0
import os as _os
EVAC = _os.environ.get("K_EVAC", "both")
P4MODE = _os.environ.get("K_P4MODE", "")
BISECT = int(_os.environ.get("K_BISECT", "0"))

V_BADA = 0
V_N1G = 48
V_N2G = 56
V_CONVW = 64
V_CONVB = 96
V_BA = 104
V_BX = 120
V_LAM = 136
NV = 152


class Prog:
    ENG = ("sync", "scalar", "vector", "gpsimd", "tensor")

    def __init__(self, nc, stack):
        self.nc = nc
        self.stack = stack
        self.e = {"sync": nc.sync, "scalar": nc.scalar, "vector": nc.vector,
                  "gpsimd": nc.gpsimd, "tensor": nc.tensor}
        self.sem = {n: stack.enter_context(nc.semaphore("s_" + n)) for n in self.ENG}
        self.cnt = {n: 0 for n in self.ENG}
        self.seen = {n: {} for n in self.ENG}
        self.dsems = []

    def dsem(self, name):
        s = self.stack.enter_context(self.nc.semaphore("d_" + name))
        d = {"sem": s, "cnt": 0, "name": "d_" + name}
        self.dsems.append(d)
        return d

    def wait(self, eng, *deps):
        for d in deps:
            if d is None:
                continue
            if isinstance(d, (list,)):
                self.wait(eng, *d)
                continue
            key, s, val = d
            if self.seen[eng].get(key, 0) < val:
                self.seen[eng][key] = val
                self.e[eng].wait_ge(s, val)

    def sig(self, eng, ins):
        self.cnt[eng] += 1
        ins.then_inc(self.sem[eng], 1)
        return (eng, self.sem[eng], self.cnt[eng])

    def do(self, eng, deps, method, *a, **kw):
        self.wait(eng, deps)
        return self.sig(eng, getattr(self.e[eng], method)(*a, **kw))

    def last(self, eng):
        return (eng, self.sem[eng], self.cnt[eng])

    def dma(self, q, ds, out, in_, *deps, **kw):
        self.wait(q, *deps)
        ins = self.e[q].dma_start(out=out, in_=in_, **kw)
        ds["cnt"] += 16
        ins.then_inc(ds["sem"], 16)
        return (ds["name"], ds["sem"], ds["cnt"])

    def dlast(self, ds):
        return (ds["name"], ds["sem"], ds["cnt"])

    def barrier(self):
        hs = [self.last(n) for n in self.ENG] + [self.dlast(d) for d in self.dsems]
        for n in self.ENG:
            self.wait(n, *[h for h in hs if h[0] != n and h[2] > 0])


def build_program(debug=False, upto=99):
    nc = bass.Bass("TRN2", target_bir_lowering=False)
    dbgset = set(debug) if debug else set()

    def din(name, shape, dt=F32):
        return nc.dram_tensor(name, list(shape), dt, kind="ExternalInput").ap()

    def dscr(name, shape, dt=F32):
        return nc.dram_tensor(name, list(shape), dt, kind=("ExternalOutput" if name in dbgset else "Internal")).ap()

    x_in = din("x", [T, D])
    cT_in = din("cT", [128, KC, 2])
    conn_in = din("conn", [128, 1])
    vecs_in = din("vecs", [128, NV])
    rope_in = din("rope", [128, NT, 16])
    fgrow_in = din("fgrow", [1, D])
    badarow_in = din("badarow", [1, 6 * D])
    w_ada = din("w_ada", [D, 6 * D])
    w_in = din("w_in", [D, IN_COLS])
    rg_wa = din("rg_wa", [2, 16, 64, 64])
    rg_wx = din("rg_wx", [2, 16, 64, 64])
    w_br_rnn = din("w_br_rnn", [D, D])
    w_br_attn = din("w_br_attn", [512, D])
    w_out = din("w_out", [D, D])
    w_ffn_in = din("w_ffn_in", [D, 2 * D_FF])
    w_ffn_out = din("w_ffn_out", [D_FF, D])
    y_out = nc.dram_tensor("y", [T, D], F32, kind="ExternalOutput").ap()

    xrT = dscr("xrT", [D, T])
    grT = dscr("grT", [D, T])
    sgT = dscr("sgT", [2 * D, T])
    qn = dscr("qn", [3, T, 512], BF16)
    kn = dscr("kn", [3, T, 512], BF16)
    vn = dscr("vn", [3, T, 512], BF16)
    rnnT = dscr("rnnT", [D, T], BF16)
    attT = dscr("attT", [512, T], BF16)
    x1s = dscr("x1s", [T, D])
    h2T = dscr("h2T", [D, T], BF16)
    hT_dbg = dscr("hT_dbg", [D, T], BF16) if "hT_dbg" in dbgset else None

    stack = ExitStack()
    with stack:
        P = Prog(nc, stack)

        def sb(name, shape, dt=F32, st=stack):
            return st.enter_context(nc.sbuf_tensor("sb_" + name, list(shape), dt))

        def ps(name, shape, dt=F32, st=stack):
            return st.enter_context(nc.psum_tensor("ps_" + name, list(shape), dt))

        vecs = sb("vecs", [128, NV])
        conn = sb("conn", [128, 1])
        idf = sb("idf", [128, 128])
        idb = sb("idb", [128, 128], BF16)
        scale1 = sb("scale1", [128, KC, 2])
        shift1 = sb("shift1", [128, KC, 2])
        scale2 = sb("scale2", [128, KC, 2])
        shift2 = sb("shift2", [128, KC, 2])
        spv = sb("spv", [128, 16])
        hspv = sb("hspv", [128, 16])
        hbias = sb("hbias", [128, 32])
        cpow = sb("cpow", [128, 2])
        gt2row = sb("gt2row", [128, 2, D])
        fgrow = sb("fgrow", [128, D])

        d_const = P.dsem("const")
        P.dma("sync", d_const, vecs[:], vecs_in[:, :])
        P.dma("sync", d_const, conn[:], conn_in[:, :])
        h_const = P.dma("sync", d_const, fgrow[:], fgrow_in[0:1, :].broadcast_to([128, D]))

        h_ms = P.do("gpsimd", [], "memset", idf[:], 1.0)
        h_idf = P.do("gpsimd", [h_ms], "affine_select", out=idf[:], in_=idf[:], pattern=[[-1, 128]],
                     compare_op=ALU.is_equal, fill=0.0, base=0, channel_multiplier=1)
        h_idb = P.do("gpsimd", [h_idf], "tensor_copy", out=idb[:], in_=idf[:])
        h_cpow = [P.do("gpsimd", [], "memset", cpow[:, 0:1], -0.5), P.do("gpsimd", [], "memset", cpow[:, 1:2], 0.5)]
        gt1stack = ExitStack()
        gt1row = sb("gt1row", [128, 2, D], st=gt1stack)
        gtrows = [gt1row, gt2row]

        with ExitStack() as s0:
            cT = sb("cT", [128, KC, 2], st=s0)
            scT = sb("scT", [128, KC, 2], st=s0)
            screp = sb("screp", [128, 2, KC, 128], st=s0)
            wbuf = [sb("wa", [128, KC, D], st=s0), sb("wa2", [128, KC, D], st=s0)]
            modT = sb("modT", [128, 4, KC, 2], st=s0)
            brow = sb("brow", [128, 2, D], st=s0)
            tmp16 = sb("tmp16", [128, 16], st=s0)
            pm = [ps("pm0", [128, 512], st=s0), ps("pm1", [128, 512], st=s0)]
            d_c = P.dsem("p0c")
            d_w = [P.dsem("p0w0"), P.dsem("p0w1")]

            h_c = P.dma("sync", d_c, cT[:], cT_in[:, :, :])
            h_sc = P.do("scalar", [h_c], "activation", out=scT[:], in_=cT[:], func=AF.Silu)
            h_rep = []
            for s in range(2):
                h_rep.append(P.do("vector", [h_sc], "tensor_copy", out=screp[:, s, :, :],
                                  in_=scT[:, :, s:s + 1].broadcast_to([128, KC, 128])))
            h_t = P.do("scalar", [h_const], "activation", out=tmp16[:], in_=vecs[:, V_LAM:V_LAM + 16],
                       func=AF.Exp, scale=-1.0)
            h_t = P.do("scalar", [h_t], "activation", out=tmp16[:], in_=tmp16[:], func=AF.Ln, bias=1.0)
            P.do("vector", [h_t], "tensor_scalar", out=spv[:], in0=tmp16[:], scalar1=-8.0, scalar2=None, op0=ALU.mult)
            h_spv = [P.do("vector", [h_t], "tensor_scalar", out=hspv[:], in0=tmp16[:], scalar1=-4.0, scalar2=None,
                          op0=ALU.mult),
                     P.do("vector", [h_const], "tensor_scalar", out=hbias[:], in0=vecs[:, V_BA:V_BA + 32], scalar1=0.5,
                          scalar2=None, op0=ALU.mult)]

            w_ada_v = w_ada.rearrange("(k p) n -> p k n", p=128)
            jobs = [(0, 0), (1, 1), (2, 3), (3, 4)]
            h_free = [None, None]
            h_pmfree = [None, None]
            npm = 0
            h_mod = []
            for ji, (mi, col) in enumerate(jobs):
                b = ji % 2
                h_w = P.dma("sync", d_w[b], wbuf[b][:], w_ada_v[:, :, col * D:(col + 1) * D], h_free[b])
                for j in range(KC):
                    pb = npm % 2
                    npm += 1
                    P.wait("tensor", h_w, h_sc, h_pmfree[pb])
                    for k in range(KC):
                        mm = nc.tensor.matmul(pm[pb][:, 0:2], lhsT=wbuf[b][:, k, j * 128:(j + 1) * 128],
                                              rhs=scT[:, k, :], start=(k == 0), stop=(k == KC - 1))
                    h_mm = P.sig("tensor", mm)
                    h_e = P.do("vector", [h_mm, h_const], "tensor_scalar", out=modT[:, mi, j, :], in0=pm[pb][:, 0:2],
                               scalar1=vecs[:, V_BADA + col * 8 + j:V_BADA + col * 8 + j + 1], scalar2=None,
                               op0=ALU.add)
                    h_pmfree[pb] = h_e
                    h_mod.append(h_e)
                h_free[b] = h_mm
            h_ss = []
            for (dst_sc, dst_sh, mi_sh, mi_sc, gcol) in ((scale1, shift1, 0, 1, V_N1G), (scale2, shift2, 2, 3, V_N2G)):
                h1 = P.do("vector", h_mod, "tensor_scalar", out=dst_sc[:], in0=modT[:, mi_sc, :, :], scalar1=1.0,
                          scalar2=None, op0=ALU.add)
                h2 = P.do("vector", [h1], "tensor_tensor", out=dst_sc[:], in0=dst_sc[:],
                          in1=vecs[:, gcol:gcol + KC].unsqueeze(2).broadcast_to([128, KC, 2]), op=ALU.mult)
                h3 = P.do("vector", h_mod, "tensor_copy", out=dst_sh[:], in_=modT[:, mi_sh, :, :])
                h_ss += [h2, h3]
            d_b = P.dsem("p0b")
            for wi, col in enumerate((2, 5)):
                h_b = P.dma("sync", d_b, brow[:, wi, :], badarow_in[0:1, col * D:(col + 1) * D].broadcast_to([128, D]))
            h_gt = []
            for wi, col in enumerate((2, 5)):
                b = wi % 2
                h_w = P.dma("sync", d_w[b], wbuf[b][:], w_ada_v[:, :, col * D:(col + 1) * D], h_free[b])
                for s in range(2):
                    for half in range(2):
                        pb = npm % 2
                        npm += 1
                        P.wait("tensor", h_w, h_rep, h_pmfree[pb])
                        for k in range(KC):
                            mm = nc.tensor.matmul(pm[pb][:, :], lhsT=screp[:, s, k, :],
                                                  rhs=wbuf[b][:, k, half * 512:(half + 1) * 512],
                                                  start=(k == 0), stop=(k == KC - 1))
                        h_mm = P.sig("tensor", mm)
                        h_e = P.do("vector", [h_mm, h_b], "tensor_tensor",
                                   out=gtrows[wi][:, s, half * 512:(half + 1) * 512], in0=pm[pb][:, :],
                                   in1=brow[:, wi, half * 512:(half + 1) * 512], op=ALU.add)
                        h_pmfree[pb] = h_e
                        h_gt.append(h_e)
                h_free[b] = h_mm
            P.barrier()

        if upto <= 0:
            gt1stack.close()
            return finish(nc, P)

        hstack = ExitStack()
        hT = sb("hT", [128, KC, T], BF16, st=hstack)

        def norm_tile(xsrc, ss_col, rs_col, xn_dst, junk, deps, h_junk):
            h1 = P.do("scalar", deps + [h_junk], "activation", out=junk[:], in_=xsrc, func=AF.Square, accum_out=ss_col)
            h2 = P.do("gpsimd", [h1], "tensor_scalar", out=rs_col, in0=ss_col, scalar1=1.0 / D, scalar2=EPS,
                      op0=ALU.mult, op1=ALU.add)
            h4 = P.do("gpsimd", [h2, h_cpow], "tensor_tensor", out=rs_col, in0=rs_col, in1=cpow[:, 0:1], op=ALU.pow)
            return h1, h4

        with ExitStack() as s1:
            NB = 6
            xt = [sb(f"xt{i}", [128, D], st=s1) for i in range(NB)]
            xn = [sb(f"xn{i}", [128, D], st=s1) for i in range(3)]
            junk = sb("junk", [128, D], BF16, st=s1)
            ss = sb("ss", [128, NT], st=s1)
            rs = sb("rs", [128, NT], st=s1)
            pt_ = [ps(f"ptr{i}", [128, KC, 128], st=s1) for i in range(2)]
            d_x = [P.dsem(f"p1x{i}") for i in range(NB)]
            h_xfree = [None] * NB
            h_xnfree = [None] * 3
            h_ptfree = [None] * 2
            h_junk = None
            h_xn_ = {}

            def N1(i):
                nonlocal h_junk
                b = i % NB
                xb_ = i % 3
                h_x = P.dma("sync", d_x[b], xt[b][:], x_in[i * 128:(i + 1) * 128, :], h_xfree[b])
                h_junk, h_rs = norm_tile(xt[b][:], ss[:, i:i + 1], rs[:, i:i + 1], None, junk, [h_x], h_junk)
                h_xn = P.do("vector", [h_rs, h_xnfree[xb_]], "tensor_scalar", out=xn[xb_][:], in0=xt[b][:],
                            scalar1=rs[:, i:i + 1], scalar2=None, op0=ALU.mult)
                h_xfree[b] = h_xn
                h_xn_[i] = h_xn

            def T1(i):
                nb = i % 2
                xb_ = i % 3
                seg = i // (NT // 2)
                P.wait("tensor", h_xn_.pop(i), h_idf, h_ptfree[nb])
                for k in range(KC):
                    tr = nc.tensor.transpose(pt_[nb][:, k, :], xn[xb_][:, k * 128:(k + 1) * 128], idf[:])
                h_tr = P.sig("tensor", tr)
                h_xnfree[xb_] = h_tr
                hs = []
                for k in range(KC):
                    if nb == 0:
                        hs.append(P.do("scalar", [h_tr, h_ss], "activation", out=hT[:, k, i * 128:(i + 1) * 128],
                                       in_=pt_[nb][:, k, :], func=AF.Identity,
                                       scale=scale1[:, k, seg:seg + 1], bias=shift1[:, k, seg:seg + 1]))
                    else:
                        hs.append(P.do("vector", [h_tr, h_ss], "tensor_scalar", out=hT[:, k, i * 128:(i + 1) * 128],
                                       in0=pt_[nb][:, k, :], scalar1=scale1[:, k, seg:seg + 1],
                                       scalar2=shift1[:, k, seg:seg + 1], op0=ALU.mult, op1=ALU.add))
                h_ptfree[nb] = hs

            N1(0)
            N1(1)
            for i in range(NT):
                T1(i)
                if i + 2 < NT:
                    N1(i + 2)
            P.barrier()
            if hT_dbg is not None:
                d_dbg = P.dsem("dbg")
                for k in range(KC):
                    P.dma("sync", d_dbg, hT_dbg[k * 128:(k + 1) * 128, :], hT[:, k, :])
                P.barrier()

        if upto <= 1:
            hstack.close()
            gt1stack.close()
            return finish(nc, P)

        w_in_v = w_in.rearrange("(k p) n -> p k n", p=128)
        with ExitStack() as s2:
            wsl = [sb(f"wsl{i}", [128, KC, 128], BF16, st=s2) for i in range(2)]
            stg = [sb(f"stg{i}", [128, 2048], st=s2) for i in range(2)]
            pz = [ps(f"pz{i}", [128, 512], st=s2) for i in range(2)]
            d_wsl = [P.dsem(f"p2w{i}") for i in range(2)]
            d_stg = [P.dsem(f"p2s{i}") for i in range(2)]
            jobs = []
            for j in range(8):
                jobs.append((xrT, j * 128, j * 128, "copy"))
            for j in range(8):
                jobs.append((grT, j * 128, 1024 + j * 128, "gelu"))
            for j in range(16):
                jobs.append((sgT, j * 128, 6656 + j * 128, "sigmoid"))
            h_wfree = [None, None]
            h_pzfree = [None, None]
            h_stgfree = [None, None]
            ntile = 0
            nstage = 0
            for ji, (dst, row0, col0, mode) in enumerate(jobs):
                b = ji % 2
                h_w = P.dma("gpsimd", d_wsl[b], wsl[b][:], w_in_v[:, :, col0:col0 + 128], h_wfree[b])
                evs = []
                for tt in range(16):
                    pb = ntile % 2
                    ntile += 1
                    sbi = nstage % 2
                    P.wait("tensor", h_w, h_pzfree[pb])
                    for k in range(KC):
                        mm = nc.tensor.matmul(pz[pb][:, :], lhsT=wsl[b][:, k, :], rhs=hT[:, k, tt * 512:(tt + 1) * 512],
                                              start=(k == 0), stop=(k == KC - 1))
                    h_mm = P.sig("tensor", mm)
                    o = stg[sbi][:, (tt % 4) * 512:(tt % 4 + 1) * 512]
                    if mode == "copy" and tt % 2 == 1:
                        h_e = P.do("vector", [h_mm, h_stgfree[sbi]], "tensor_copy", out=o, in_=pz[pb][:, :])
                    else:
                        fn = {"copy": AF.Copy, "gelu": AF.Gelu_apprx_tanh, "sigmoid": AF.Sigmoid}[mode]
                        h_e = P.do("scalar", [h_mm, h_stgfree[sbi]], "activation", out=o, in_=pz[pb][:, :], func=fn)
                    h_pzfree[pb] = h_e
                    evs.append(h_e)
                    if tt % 4 == 3:
                        h_st = P.dma("sync", d_stg[sbi], dst[row0:row0 + 128, (tt // 4) * 2048:(tt // 4 + 1) * 2048],
                                     stg[sbi][:], evs)
                        h_stgfree[sbi] = h_st
                        nstage += 1
                        evs = []
                h_wfree[b] = h_mm
            P.barrier()

        with ExitStack() as s2:
            wq = sb("wq", [128, KC, 1536], BF16, st=s2)
            ropet = sb("ropet", [128, NT, 16], st=s2)
            TB = 2
            stq = [sb(f"stq{i}", [128, TB, 2, 512], BF16, st=s2) for i in range(2)]
            stv = [sb(f"stv{i}", [128, TB, 512], BF16, st=s2) for i in range(2)]
            rtmp = [sb(f"rtmp{i}", [128, 4, 16, 8], st=s2) for i in range(2)]
            pq = [ps(f"pq{i}", [128, 3, 512], st=s2) for i in range(2)]
            d_wq = P.dsem("p2wq")
            d_rp = P.dsem("p2rp")
            d_sq = [P.dsem(f"p2sq{i}") for i in range(2)]
            d_sk = [P.dsem(f"p2sk{i}") for i in range(2)]
            d_sv = [P.dsem(f"p2sv{i}") for i in range(2)]
            h_rp = P.dma("sync", d_rp, ropet[:], rope_in[:, :, :])
            h_wqfree = None
            h_pqfree = [None, None]
            h_stfree = [[None, None], [None, None]]
            h_rtfree = [None, None]
            h_rffree = [None, None]
            rf = [sb(f"rf{i}", [128, 16, 16], st=s2) for i in range(2)]
            nt_ = 0
            for g in range(3):
                hw = []
                for c, base in enumerate((2048, 3584, 5120)):
                    hw.append(P.dma("gpsimd", d_wq, wq[:, :, c * 512:(c + 1) * 512],
                                    w_in_v[:, :, base + g * 512:base + (g + 1) * 512], h_wqfree))
                h_w = hw[-1]
                evq, evv = [], []
                for i in range(NT):
                    pb = nt_ % 2
                    sbi = (nt_ // TB) % 2
                    ti = i % TB
                    nt_ += 1
                    P.wait("tensor", h_w, h_pqfree[pb])
                    for c in range(3):
                        for k in range(KC):
                            mm = nc.tensor.matmul(pq[pb][:, c, :], lhsT=hT[:, k, i * 128:(i + 1) * 128],
                                                  rhs=wq[:, k, c * 512:(c + 1) * 512],
                                                  start=(k == 0), stop=(k == KC - 1))
                    h_mm = P.sig("tensor", mm)
                    h_v = P.do("scalar", [h_mm, h_stfree[sbi][1]], "activation", out=stv[sbi][:, ti, :],
                               in_=pq[pb][:, 2, :], func=AF.Copy)
                    src = pq[pb][:, 0:2, :].rearrange("p a (h e) -> p (a h) e", e=64)
                    dsto = stq[sbi][:, ti, :, :].rearrange("p a (h e) -> p (a h) e", e=64)
                    h_r = P.do("scalar", [h_mm, h_stfree[sbi][0]], "activation", out=stq[sbi][:, ti, :, :],
                               in_=pq[pb][:, 0:2, :], func=AF.Copy)
                    h_rf = P.do("scalar", [h_mm, h_rffree[pb]], "activation", out=rf[pb][:, :, :],
                                in_=src[:, :, 0:16], func=AF.Copy)
                    cosb = ropet[:, i, 0:8].unsqueeze(1).broadcast_to([128, 16, 8])
                    sinb = ropet[:, i, 8:16].unsqueeze(1).broadcast_to([128, 16, 8])
                    x1 = rf[pb][:, :, 0:8]
                    x2 = rf[pb][:, :, 8:16]
                    rt = rtmp[pb]
                    dep0 = [h_rf, h_rp, h_rtfree[pb]]
                    ha = P.do("vector", dep0, "tensor_tensor", out=rt[:, 0, :, :], in0=x1, in1=cosb, op=ALU.mult)
                    hb = P.do("vector", dep0, "tensor_tensor", out=rt[:, 1, :, :], in0=x2, in1=sinb, op=ALU.mult)
                    hc = P.do("vector", dep0, "tensor_tensor", out=rt[:, 2, :, :], in0=x2, in1=cosb, op=ALU.mult)
                    hd = P.do("vector", dep0, "tensor_tensor", out=rt[:, 3, :, :], in0=x1, in1=sinb, op=ALU.mult)
                    ho1 = P.do("vector", [ha, hb, h_r], "tensor_tensor", out=dsto[:, :, 0:8],
                               in0=rt[:, 0, :, :], in1=rt[:, 1, :, :], op=ALU.subtract)
                    ho2 = P.do("vector", [hc, hd, h_r], "tensor_tensor", out=dsto[:, :, 8:16],
                               in0=rt[:, 2, :, :], in1=rt[:, 3, :, :], op=ALU.add)
                    h_rtfree[pb] = [ho1, ho2]
                    h_rffree[pb] = [ha, hb, hc, hd]
                    h_pqfree[pb] = [h_v, h_r, h_rf]
                    evq += [h_r, ho1, ho2]
                    evv.append(h_v)
                    if ti == TB - 1:
                        i0 = i - (TB - 1)
                        rows = slice(i0 * 128, (i0 + TB) * 128)
                        h1 = P.dma("sync", d_sq[sbi], qn[g, rows, :].rearrange("(t p) c -> p t c", p=128),
                                   stq[sbi][:, :, 0, :], evq)
                        h2 = P.dma("sync", d_sk[sbi], kn[g, rows, :].rearrange("(t p) c -> p t c", p=128),
                                   stq[sbi][:, :, 1, :], evq)
                        h3 = P.dma("sync", d_sv[sbi], vn[g, rows, :].rearrange("(t p) c -> p t c", p=128),
                                   stv[sbi][:, :, :], evv)
                        h_stfree[sbi] = [[h1, h2], h3]
                        evq, evv = [], []
                h_wqfree = h_mm
            P.barrier()
        hstack.close()

        if upto <= 2:
            gt1stack.close()
            return finish(nc, P)

        TP = 1024
        NWS = 4
        NPC = T // TP
        with ExitStack() as s3:
            XC = sb("XC", [128, T], st=s3)
            XCB = sb("XCB", [128, T], BF16, st=s3)
            HF = sb("HF", [128, T], st=s3)
            ws = []
            for i in range(NWS):
                ws.append(dict(XR=sb(f"XR{i}", [128, TP + 3], st=s3), R=sb(f"R{i}", [128, TP], st=s3),
                               Pb=sb(f"Pb{i}", [128, TP], st=s3), I=sb(f"I{i}", [128, TP], st=s3),
                               GL=sb(f"GL{i}", [128, TP], st=s3), OUT=sb(f"OUT{i}", [128, TP], BF16, st=s3)))
            Wg = [sb(f"Wg{i}", [128, 4, 128], BF16, st=s3) for i in range(2)]
            carry = sb("carry", [128, 64], st=s3)
            pg = [[ps(f"pg{i}{t}", [128, 512], st=s3) for t in range(2)] for i in range(3)]
            pc = [ps(f"pc{i}", [128, 512], st=s3) for i in range(2)]
            Dk = [sb(f"Dk{i}", [128, 4, 128], st=s3) for i in range(2)]
            h_pcfree = [None, None]
            h_dkfree = [None, None]
            ccount = 0
            d_wg = [P.dsem(f"p3wg{i}") for i in range(2)]
            d_xr = [P.dsem(f"p3xr{i}") for i in range(NWS)]
            d_gl = [P.dsem(f"p3gl{i}") for i in range(NWS)]
            d_out = [P.dsem(f"p3o{i}") for i in range(NWS)]
            h_wgz = [P.do("gpsimd", [], "memset", Wg[i][:], 0.0) for i in range(2)]
            h_wgfree = [None, None]
            h_pgfree = [None] * 3
            free = {k: [None] * NWS for k in ("XR", "R", "Pb", "I", "GL", "OUT")}
            h_xcfree = [None] * NPC
            h_xcbfree = [None] * NPC
            h_hffree = [None] * NPC
            cnt = 0
            xcnt = 0
            gcount = 0

            def gates(j, b, dirn, p, w, wi, h_wg, h_xcb_p):
                nonlocal gcount
                c0 = p * TP
                h_rs, h_is = [], []
                for s_ in range(TP // 512):
                    pb = gcount % 3
                    gcount += 1
                    P.wait("tensor", h_xcb_p, h_wg, h_pgfree[pb])
                    cols = slice(c0 + s_ * 512, c0 + (s_ + 1) * 512)
                    nc.tensor.matmul(pg[pb][0][:, :], lhsT=Wg[b][:, 2 * dirn, :], rhs=XCB[:, cols], start=True, stop=True)
                    h_mm = P.sig("tensor", nc.tensor.matmul(pg[pb][1][:, :], lhsT=Wg[b][:, 2 * dirn + 1, :],
                                                            rhs=XCB[:, cols], start=True, stop=True))
                    sl = slice(s_ * 512, (s_ + 1) * 512)
                    h_r = P.do("scalar", [h_mm, free["R"][wi]], "activation", out=w["R"][:, sl], in_=pg[pb][0][:, :],
                               func=AF.Tanh, scale=0.5, bias=hbias[:, dirn * 8 + j:dirn * 8 + j + 1])
                    h_i = P.do("scalar", [h_mm, free["I"][wi]], "activation", out=w["I"][:, sl], in_=pg[pb][1][:, :],
                               func=AF.Tanh, scale=0.5, bias=hbias[:, 16 + dirn * 8 + j:16 + dirn * 8 + j + 1])
                    h_pgfree[pb] = [h_r, h_i]
                    h_rs.append(h_r)
                    h_is.append(h_i)
                return h_rs, h_is, h_mm

            def au_front(j, dirn, p, w, wi, h_rs):
                sc = spv[:, dirn * 8 + j:dirn * 8 + j + 1]
                hsc = hspv[:, dirn * 8 + j:dirn * 8 + j + 1]
                h_p1 = P.do("scalar", [h_rs, h_spv, free["Pb"][wi]], "activation", out=w["Pb"][:, :], in_=w["R"][:, :],
                            func=AF.Exp, scale=sc, bias=sc)
                h_a = P.do("scalar", [h_p1, h_rs], "activation", out=w["R"][:, :], in_=w["R"][:, :], func=AF.Exp, scale=hsc,
                           bias=hsc)
                return h_p1, h_a

            def au_sqrt(w, h_p1):
                return P.do("scalar", [h_p1], "activation", out=w["Pb"][:, :], in_=w["Pb"][:, :], func=AF.Sqrt,
                            scale=-0.25, bias=0.25)

            def au_back(p, w, h_is, h_xc_p, h_p3):
                c0 = p * TP
                h_i1 = P.do("vector", [h_is, h_xc_p], "scalar_tensor_tensor", out=w["I"][:, :], in0=w["I"][:, :], scalar=1.0,
                            in1=XC[:, c0:c0 + TP], op0=ALU.add, op1=ALU.mult)
                h_i2 = P.do("vector", [h_i1, h_p3], "tensor_tensor", out=w["I"][:, :], in0=w["I"][:, :],
                            in1=w["Pb"][:, :], op=ALU.mult)
                return h_i1, h_i2

            def schedule(n):
                ev = [("F", 0), ("F", 1), ("S", 0), ("S", 1), ("B", 0)]
                k = 2
                while k < n:
                    ev += [("F", k), ("B", k - 1), ("F", k + 1), ("S", k), ("S", k + 1), ("B", k)]
                    k += 2
                ev.append(("B", n - 1))
                return ev

            def prep_chunk(j):
                b = j % 2
                for t, (src, dirn) in enumerate(((rg_wa, 0), (rg_wx, 0), (rg_wa, 1), (rg_wx, 1))):
                    for blk in range(2):
                        h_wg_ = P.dma("gpsimd", d_wg[b], Wg[b][64 * blk:64 * blk + 64, t, 64 * blk:64 * blk + 64],
                                      src[dirn, 2 * j + blk, :, :], h_wgz[b], h_wgfree[b])
                h_dk_ = [P.do("gpsimd", [h_idf, h_const, h_dkfree[b]], "tensor_scalar", out=Dk[b][:, k, :], in0=idf[:],
                              scalar1=vecs[:, V_CONVW + k * 8 + j:V_CONVW + k * 8 + j + 1], scalar2=None, op0=ALU.mult)
                         for k in range(4)]
                return h_wg_, h_dk_

            chunk_prep = {0: prep_chunk(0)}
            chunk_prep_keep = {}
            cstate = {}
            for j in range(KC):
                b = j % 2
                rows = slice(j * 128, (j + 1) * 128)
                h_wg, h_dk = chunk_prep.pop(j)
                chunk_prep_keep[j] = (h_wg, h_dk)
                cw = [vecs[:, V_CONVW + k * 8 + j:V_CONVW + k * 8 + j + 1] for k in range(4)]
                cb = vecs[:, V_CONVB + j:V_CONVB + j + 1]
                cstate.setdefault(j, dict(h_xc=[None] * NPC, h_xcb=[None] * NPC, info={}, done=set()))
                h_xc = cstate[j]["h_xc"]
                h_xcb = cstate[j]["h_xcb"]
                info = cstate[j]["info"]
                h_scan = [None] * NPC
                d1 = j % 2
                d2 = 1 - d1
                order1 = list(range(NPC)) if d1 == 0 else list(range(NPC - 1, -1, -1))
                order2 = list(range(NPC)) if d2 == 0 else list(range(NPC - 1, -1, -1))

                def C_conv(p, jj=None, evac_eng="vector", phase=None):
                    nonlocal xcnt, ccount
                    jj = j if jj is None else jj
                    cs_ = cstate.setdefault(jj, dict(h_xc=[None] * NPC, h_xcb=[None] * NPC, info={}, done=set()))
                    pend = cs_.setdefault("pend", {})
                    if phase == "ev":
                        if p not in pend:
                            return
                        mms = pend.pop(p)
                    else:
                        if p in cs_["done"]:
                            return
                        cs_["done"].add(p)
                        mms = None
                    b = jj % 2
                    rows = slice(jj * 128, (jj + 1) * 128)
                    cb = vecs[:, V_CONVB + jj:V_CONVB + jj + 1]
                    h_dk = (chunk_prep_keep[jj] if jj in chunk_prep_keep else chunk_prep[jj])[1]
                    h_xc, h_xcb, info = cs_["h_xc"], cs_["h_xcb"], cs_["info"]
                    c0 = p * TP
                    if mms is None:
                        wi = xcnt % NWS
                        xcnt += 1
                        w = ws[wi]
                        lo = max(c0 - 2, 0)
                        hi = min(c0 + TP + 1, T)
                        off = lo - (c0 - 2)
                        h_xr = P.dma("sync", d_xr[wi], w["XR"][:, off:off + (hi - lo)], xrT[rows, lo:hi], free["XR"][wi])
                        if p == 0:
                            h_fix = P.do("gpsimd", [free["XR"][wi]], "memset", w["XR"][:, 0:2], 0.0)
                        elif p == NPC - 1:
                            h_fix = P.do("gpsimd", [free["XR"][wi]], "memset", w["XR"][:, TP + 2:TP + 3], 0.0)
                        elif p == NPC // 2 - 1:
                            h_fix = P.do("gpsimd", [h_xr], "tensor_scalar", out=w["XR"][:, TP + 2:TP + 3],
                                         in0=w["XR"][:, TP + 2:TP + 3], scalar1=conn[:, 0:1], scalar2=None, op0=ALU.mult)
                        elif p == NPC // 2:
                            h_fix = P.do("gpsimd", [h_xr], "tensor_scalar", out=w["XR"][:, 0:2], in0=w["XR"][:, 0:2],
                                         scalar1=conn[:, 0:1], scalar2=None, op0=ALU.mult)
                        else:
                            h_fix = None
                        mms = []
                        for s_ in range(TP // 512):
                            cb_ = ccount % 2
                            ccount += 1
                            P.wait("tensor", h_xr, h_fix, h_dk, h_pcfree[cb_])
                            for k in range(4):
                                mm = nc.tensor.matmul(pc[cb_][:, :], lhsT=Dk[b][:, k, :],
                                                      rhs=w["XR"][:, k + s_ * 512:k + s_ * 512 + 512], start=(k == 0),
                                                      stop=(k == 3))
                            h_cm = P.sig("tensor", mm)
                            mms.append((cb_, h_cm))
                        free["XR"][wi] = h_cm
                        h_dkfree[b] = h_cm
                        info[p] = dict()
                        if phase == "mm":
                            pend[p] = mms
                            return
                    hs_c = []
                    hs_b = []
                    for s_, (cb_, h_cm) in enumerate(mms):
                        if evac_eng == "scalar":
                            h_e = P.do("scalar", [h_cm, h_xcfree[p], h_const], "activation",
                                       out=XC[:, c0 + s_ * 512:c0 + (s_ + 1) * 512], in_=pc[cb_][:, :], func=AF.Identity,
                                       bias=cb)
                            h_e2 = P.do("scalar", [h_cm, h_xcbfree[p], h_const], "activation",
                                        out=XCB[:, c0 + s_ * 512:c0 + (s_ + 1) * 512], in_=pc[cb_][:, :], func=AF.Identity,
                                        bias=cb)
                        else:
                            h_e = P.do("vector", [h_cm, h_xcfree[p], h_const], "tensor_scalar",
                                       out=XC[:, c0 + s_ * 512:c0 + (s_ + 1) * 512], in0=pc[cb_][:, :], scalar1=cb,
                                       scalar2=None, op0=ALU.add)
                            h_e2 = P.do("vector", [h_cm, h_xcbfree[p], h_const], "tensor_scalar",
                                        out=XCB[:, c0 + s_ * 512:c0 + (s_ + 1) * 512], in0=pc[cb_][:, :], scalar1=cb,
                                        scalar2=None, op0=ALU.add)
                        h_pcfree[cb_] = [h_e, h_e2]
                        hs_c.append(h_e)
                        hs_b.append(h_e2)
                    h_xc[p] = hs_c
                    h_xcb[p] = hs_b

                def F1(k):
                    nonlocal cnt
                    p = order1[k]
                    wi = cnt % NWS
                    cnt += 1
                    w = ws[wi]
                    info[p].update(wi=wi, w=w)
                    h_rs, h_is, h_lmm = gates(j, b, d1, p, w, wi, h_wg, h_xcb[p])
                    h_p1, h_a = au_front(j, d1, p, w, wi, h_rs)
                    info[p].update(h_is=h_is, h_p1=h_p1, h_a=h_a)

                def S1(k):
                    p = order1[k]
                    info[p]["h_p3"] = au_sqrt(info[p]["w"], info[p]["h_p1"])

                def B1(k):
                    p = order1[k]
                    d_ = info.pop(p)
                    wi, w = d_["wi"], d_["w"]
                    c0 = p * TP
                    h_i1, h_i2 = au_back(p, w, d_["h_is"], h_xc[p], d_["h_p3"])
                    if k == 0:
                        init = 0.0
                        h_init = None
                    else:
                        pp = order1[k - 1]
                        src = HF[:, c0 - 1:c0] if d1 == 0 else HF[:, c0 + TP:c0 + TP + 1]
                        if (d1 == 0 and p == NPC // 2) or (d1 == 1 and p == NPC // 2 - 1):
                            cc = carry[:, j * 8:j * 8 + 1]
                            h_init = P.do("vector", [h_scan[pp]], "tensor_scalar", out=cc, in0=src, scalar1=conn[:, 0:1],
                                          scalar2=None, op0=ALU.mult)
                            init = cc
                        else:
                            init = src
                            h_init = h_scan[pp]
                    hfv = HF[:, c0:c0 + TP]
                    if d1 == 0:
                        o_, a_, u_ = hfv, w["R"][:, :], w["I"][:, :]
                    else:
                        o_, a_, u_ = hfv[:, ::-1], w["R"][:, ::-1], w["I"][:, ::-1]
                    h_scan[p] = P.do("vector", [d_["h_a"], h_i2, h_init, h_hffree[p]], "tensor_tensor_scan",
                                     out=o_, data0=a_, data1=u_, initial=init, op0=ALU.mult, op1=ALU.add)
                    free["R"][wi] = h_scan[p]
                    free["I"][wi] = h_scan[p]
                    free["Pb"][wi] = h_i2

                C_conv(order1[0])
                C_conv(order1[1])
                for (kind, k) in schedule(NPC):
                    if kind == "F":
                        F1(k)
                        if k + 2 < NPC:
                            C_conv(order1[k + 2])
                    else:
                        {"S": S1, "B": B1}[kind](k)

                if j + 1 < KC:
                    chunk_prep[j + 1] = prep_chunk(j + 1)
                bst = dict(h_bprev=None, wi_prev=None, h_lmm=None)

                def F2(k):
                    nonlocal cnt
                    p = order2[k]
                    wi = cnt % NWS
                    cnt += 1
                    w = ws[wi]
                    c0 = p * TP
                    h_gl = P.dma("sync", d_gl[wi], w["GL"][:, :], grT[rows, c0:c0 + TP], free["GL"][wi])
                    h_rs, h_is, h_lmm = gates(j, b, d2, p, w, wi, h_wg, h_xcb[p])
                    h_xcbfree[p] = h_lmm
                    bst["h_lmm"] = h_lmm
                    h_p1, h_a = au_front(j, d2, p, w, wi, h_rs)
                    info[p] = dict(wi=wi, w=w, h_is=h_is, h_p1=h_p1, h_a=h_a, h_gl=h_gl)

                def S2(k):
                    p = order2[k]
                    info[p]["h_p3"] = au_sqrt(info[p]["w"], info[p]["h_p1"])

                def B2(k):
                    p = order2[k]
                    d_ = info.pop(p)
                    wi, w = d_["wi"], d_["w"]
                    c0 = p * TP
                    h_i1, h_i2 = au_back(p, w, d_["h_is"], h_xc[p], d_["h_p3"])
                    h_xcfree[p] = h_i1
                    if k == 0:
                        init = 0.0
                        h_init = None
                    else:
                        wp = bst["wi_prev"]
                        cc = carry[:, j * 8 + 1 + (k % 7):j * 8 + 2 + (k % 7)]
                        src = ws[wp]["Pb"][:, TP - 1:TP] if d2 == 0 else ws[wp]["Pb"][:, 0:1]
                        if (d2 == 0 and p == NPC // 2) or (d2 == 1 and p == NPC // 2 - 1):
                            h_init = P.do("vector", [bst["h_bprev"], h_i2], "tensor_scalar", out=cc, in0=src,
                                          scalar1=conn[:, 0:1], scalar2=None, op0=ALU.mult)
                        else:
                            h_init = P.do("vector", [bst["h_bprev"], h_i2], "tensor_copy", out=cc, in_=src)
                        init = cc
                        free["Pb"][wp] = [free["Pb"][wp], h_init]
                    if d2 == 0:
                        o_, a_, u_ = w["Pb"][:, :], w["R"][:, :], w["I"][:, :]
                    else:
                        o_, a_, u_ = w["Pb"][:, ::-1], w["R"][:, ::-1], w["I"][:, ::-1]
                    h_bs = P.do("vector", [d_["h_a"], h_i2, h_init], "tensor_tensor_scan", out=o_, data0=a_, data1=u_,
                                initial=init, op0=ALU.mult, op1=ALU.add)
                    h_rec = P.do("vector", [h_bs, h_scan[p]], "tensor_tensor", out=w["I"][:, :], in0=HF[:, c0:c0 + TP],
                                 in1=w["Pb"][:, :], op=ALU.add)
                    h_o = P.do("vector", [h_rec, d_["h_gl"], free["OUT"][wi]], "tensor_tensor", out=w["OUT"][:, :],
                               in0=w["I"][:, :], in1=w["GL"][:, :], op=ALU.mult)
                    h_st = P.dma("gpsimd", d_out[wi], rnnT[rows, c0:c0 + TP], w["OUT"][:, :], h_o)
                    free["OUT"][wi] = h_st
                    free["GL"][wi] = h_o
                    free["R"][wi] = h_bs
                    free["I"][wi] = h_o
                    bst["h_bprev"] = h_bs
                    bst["wi_prev"] = wi
                    free["Pb"][wi] = h_rec
                    h_hffree[p] = h_rec

                for (kind, k) in schedule(NPC):
                    {"F": F2, "S": S2, "B": B2}[kind](k)
                    if kind == "B" and k == 1 and j + 1 < KC:
                        nfirst_ = list(range(NPC)) if (j + 1) % 2 == 0 else list(range(NPC - 1, -1, -1))
                        C_conv(nfirst_[0], j + 1, phase="mm")
                    if kind == "S" and k == NPC - 1 and j + 1 < KC:
                        nfirst = list(range(NPC)) if (j + 1) % 2 == 0 else list(range(NPC - 1, -1, -1))
                        C_conv(nfirst[0], j + 1, evac_eng="scalar", phase="ev")
                        C_conv(nfirst[1], j + 1, evac_eng="scalar")
                h_wgfree[b] = bst["h_lmm"]
            P.barrier()

        if upto <= 3:
            gt1stack.close()
            return finish(nc, P)

        with ExitStack() as s4:
            acc = sb("acc", [128, 2, T], st=s4)
            QT = sb("QT", [128, 10240], BF16, st=s4)
            KT = sb("KT", [128, T], BF16, st=s4)
            VA = sb("VA", [128, NT, 2, 128], BF16, st=s4)
            tok = [dict(q=sb(f"tq{i}", [128, 16, 128], BF16, st=s4), k=sb(f"tk{i}", [128, 16, 128], BF16, st=s4),
                        v=sb(f"tv{i}", [128, 16, 128], BF16, st=s4)) for i in range(2)]
            mf = sb("mf", [128, 2, 128], st=s4)
            mstd = sb("mstd", [128, 2, 128], BF16, st=s4)
            mmid = sb("mmid", [128, 2, 128], BF16, st=s4)
            cm1 = sb("cm1", [128, 1], st=s4)
            pt = [sb(f"pt{i}", [128, 2, 2, 128], BF16, st=s4) for i in range(2)]
            ntmp = sb("ntmp", [128, 2048], st=s4)
            nout = [sb(f"nout{i}", [128, 2048], BF16, st=s4) for i in range(2)]
            ptr4 = [ps(f"ptr4{i}", [128, 2, 4, 128], BF16, st=s4) for i in range(2)]
            pss_ = [ps(f"pss{i}", [128, 2, 512], st=s4) for i in range(2)]
            pss = [t_[:, :, 0:256].rearrange("p h (t q) -> p h t q", q=128) for t_ in pss_]
            pso = [ps(f"pso{i}", [128, 4, 128], st=s4) for i in range(2)]
            d_tok = [P.dsem(f"p4t{i}") for i in range(2)]
            d_no = [P.dsem(f"p4n{i}") for i in range(2)]

            h_m = P.do("gpsimd", [], "memset", mf[:], 0.0)
            hm = [P.do("gpsimd", [h_m], "affine_select", out=mf[:, 0, :], in_=mf[:, 0, :], pattern=[[-1, 128]],
                       compare_op=ALU.is_ge, fill=-30000.0, base=0, channel_multiplier=1),
                  P.do("gpsimd", [h_m], "affine_select", out=mf[:, 1, :], in_=mf[:, 1, :], pattern=[[1, 128]],
                       compare_op=ALU.is_ge, fill=-30000.0, base=0, channel_multiplier=-1)]
            h_ms1 = P.do("gpsimd", hm, "tensor_copy", out=mstd[:], in_=mf[:])
            h_cm = P.do("gpsimd", [h_const], "tensor_scalar", out=cm1[:], in0=conn[:], scalar1=-1.0, scalar2=30000.0,
                        op0=ALU.add, op1=ALU.mult)
            hm2 = [P.do("gpsimd", hm + [h_cm, h_ms1], "tensor_scalar", out=mf[:, 0, 64:128], in0=mf[:, 0, 64:128],
                        scalar1=cm1[:, 0:1], scalar2=None, op0=ALU.add),
                   P.do("gpsimd", hm + [h_cm, h_ms1], "tensor_scalar", out=mf[:, 1, 0:64], in0=mf[:, 1, 0:64],
                        scalar1=cm1[:, 0:1], scalar2=None, op0=ALU.add)]
            h_ms2 = P.do("gpsimd", hm2 + [h_ms1], "tensor_copy", out=mmid[:], in_=mf[:])
            h_masks = [h_ms1, h_ms2, h_idb]
            h_va1 = [P.do("gpsimd", [], "memset", VA[:, :, 0, 64:128], 1.0),
                     P.do("gpsimd", [], "memset", VA[:, :, 1, 0:64], 1.0)]

            pgs = [(p, g) for p in range(4) for g in range(3)]

            def geom(g):
                d = GROUPS[g][1]
                n = NT // d
                RL = T // d
                return d, n, RL, RL + 128

            qt_rd = [None] * 20
            kt_rd = [None] * 16
            va_rd = [None] * 16
            ready = {}
            st4 = dict(tokfree=[None, None], trfree=[None, None], pssfree=[None, None], ptfree=[None, None],
                       psofree=[None, None], accfree=None, ntmpfree=None, noutfree=[None, None], lastacc=None,
                       npiece=0, nbatch=0, nblk=0, nnorm=0)

            def prep_gen(idx):
                p, g = pgs[idx]
                d, n, RL, QS = geom(g)
                qv = qn[g].rearrange("(m r) c -> r m c", r=d)
                kv = kn[g].rearrange("(m r) c -> r m c", r=d)
                vv = vn[g].rearrange("(m r) c -> r m c", r=d)
                rd = dict(h_qt=[None] * NT, h_kt=[None] * NT, h_vt=[None] * NT, h_qz=None)
                ready[idx] = rd
                for c in range(4):
                    wi = st4["npiece"] % 2
                    st4["npiece"] += 1
                    tk = tok[wi]
                    for r in range(d):
                        t_lo = max(16 * c, r * n)
                        t_hi = min(16 * c + 16, (r + 1) * n)
                        if t_lo >= t_hi:
                            continue
                        lt0 = t_lo - r * n
                        cntt = t_hi - t_lo
                        to = t_lo - 16 * c
                        for (srcv, dstt) in ((qv, tk["q"]), (kv, tk["k"]), (vv, tk["v"])):
                            h_ld = P.dma("sync", d_tok[wi], dstt[:, to:to + cntt, :],
                                         srcv[r, lt0 * 128:(lt0 + cntt) * 128, p * 128:(p + 1) * 128]
                                         .rearrange("(t q) c -> q t c", q=128), st4["tokfree"][wi])
                    hfree = []
                    for b4 in range(4):
                        pbq = st4["nbatch"] % 2
                        st4["nbatch"] += 1
                        ti0 = 16 * c + 4 * b4
                        r = ti0 // n
                        lt = ti0 % n
                        P.wait("tensor", h_ld, h_idb, st4["trfree"][pbq])
                        for t in range(4):
                            nc.tensor.transpose(ptr4[pbq][:, 0, t, :], tk["q"][:, b4 * 4 + t, :], idb[:])
                        for t in range(4):
                            tr = nc.tensor.transpose(ptr4[pbq][:, 1, t, :], tk["k"][:, b4 * 4 + t, :], idb[:])
                        h_tr = P.sig("tensor", tr)
                        qc = r * QS + 64 + lt * 128
                        kc = r * RL + lt * 128
                        dq = [qt_rd[qc // 512], qt_rd[(qc + 511) // 512]]
                        dk = [kt_rd[kc // 512]]
                        if pbq == 0:
                            h_eq = P.do("scalar", [h_tr] + dq, "activation", out=QT[:, qc:qc + 512],
                                        in_=ptr4[pbq][:, 0, :, :], func=AF.Copy)
                            h_ek = P.do("scalar", [h_tr] + dk, "activation", out=KT[:, kc:kc + 512],
                                        in_=ptr4[pbq][:, 1, :, :], func=AF.Copy)
                        else:
                            h_eq = P.do("vector", [h_tr] + dq, "tensor_copy", out=QT[:, qc:qc + 512],
                                        in_=ptr4[pbq][:, 0, :, :])
                            h_ek = P.do("vector", [h_tr] + dk, "tensor_copy", out=KT[:, kc:kc + 512],
                                        in_=ptr4[pbq][:, 1, :, :])
                        st4["trfree"][pbq] = [h_eq, h_ek]
                        for t in range(4):
                            rd["h_qt"][ti0 + t] = h_eq
                            rd["h_kt"][ti0 + t] = h_ek
                        hfree.append(h_tr)
                        yield
                    dv = [va_rd[4 * c + x] for x in range(4)]
                    h_v0 = P.do("gpsimd", [h_ld, h_va1] + dv, "tensor_copy", out=VA[:, 16 * c:16 * c + 16, 0, 0:64],
                                in_=tk["v"][:, :, 0:64])
                    h_v1 = P.do("gpsimd", [h_ld, h_va1] + dv, "tensor_copy", out=VA[:, 16 * c:16 * c + 16, 1, 64:128],
                                in_=tk["v"][:, :, 64:128])
                    for t in range(16):
                        rd["h_vt"][16 * c + t] = [h_v0, h_v1]
                    st4["tokfree"][wi] = hfree + [h_v0, h_v1]
                    yield

            def prep_reqs(idx_next, idx_cur):
                _, g2 = pgs[idx_next]
                d2, n2, RL2, QS2 = geom(g2)
                _, g1 = pgs[idx_cur]
                d1, n1, RL1, QS1 = geom(g1)
                lastq = [-1] * 20
                lastk = [-1] * 16
                bi = 0
                for r in range(d1):
                    for i in range(-1, n1):
                        qcol = r * QS1 + 128 * (i + 1)
                        for ch in (qcol // 512, (qcol + 127) // 512):
                            lastq[ch] = bi
                        for tt_ in (i, i + 1):
                            if 0 <= tt_ <= n1 - 1:
                                lastk[(r * n1 + tt_) // 4] = bi
                        bi += 1
                reqs = []
                for c in range(4):
                    for b4 in range(4):
                        ti0 = 16 * c + 4 * b4
                        r = ti0 // n2
                        lt = ti0 % n2
                        qc = r * QS2 + 64 + lt * 128
                        kc = r * RL2 + lt * 128
                        reqs.append(max(lastq[qc // 512], lastq[(qc + 511) // 512], lastk[kc // 512]))
                    reqs.append(max(lastk[4 * c + x] for x in range(4)))
                return reqs

            def pads(idx):
                p, g = pgs[idx]
                d, n, RL, QS = geom(g)
                qv_ = QT[:, 0:d * QS].rearrange("p (r c) -> p r c", c=QS)
                deps = [x for x in qt_rd]
                ready[idx]["h_qz"] = [P.do("gpsimd", deps, "memset", qv_[:, :, 0:64], 0.0),
                                      P.do("gpsimd", deps, "memset", qv_[:, :, QS - 64:QS], 0.0)]

            def run_blocks(idx, nxt, nxt_reqs):
                p, g = pgs[idx]
                d, n, RL, QS = geom(g)
                rd = ready[idx]
                blocks = [(r, i) for r in range(d) for i in range(-1, n)]
                nb_ = len(blocks)
                state = {}
                step = [0]

                def emit_S(bi):
                    r, i = blocks[bi]
                    sbi = (st4["nblk"] + bi) % 2
                    hasA = i >= 0
                    hasB = i + 1 <= n - 1
                    qcol = r * QS + 128 * (i + 1)
                    deps = [st4["pssfree"][sbi], h_masks]
                    tiles = [tt_ for tt_ in (i, i + 1) if 0 <= tt_ <= n - 1]
                    for tt_ in tiles:
                        deps += [rd["h_qt"][r * n + tt_], rd["h_kt"][r * n + tt_]]
                    if i == -1 or i == n - 1:
                        deps.append(rd["h_qz"])
                    P.wait("tensor", deps)
                    M = mmid if i == n // 2 - 1 else mstd
                    mm = None
                    for t_ in ((0,) if hasA else ()) + ((1,) if hasB else ()):
                        kcol = r * RL + 128 * (i + t_)
                        for hh in range(2):
                            rows_ = slice(64 * hh, 64 * hh + 64)
                            nc.tensor.matmul(pss[sbi][:, hh, t_, :], lhsT=KT[rows_, kcol:kcol + 128],
                                             rhs=QT[rows_, qcol:qcol + 128], start=True, stop=False)
                        for hh in range(2):
                            mm = nc.tensor.matmul(pss[sbi][:, hh, t_, :], lhsT=idb[:], rhs=M[:, t_, :], start=False, stop=True)
                    h_s = P.sig("tensor", mm)
                    for ch in (qcol // 512, (qcol + 127) // 512):
                        qt_rd[ch] = h_s
                    for tt_ in tiles:
                        kt_rd[(r * n + tt_) // 4] = h_s
                    if hasA and hasB:
                        sv = lambda a: a[:, :, :, :]
                    elif hasB:
                        sv = lambda a: a[:, :, 1, :]
                    else:
                        sv = lambda a: a[:, :, 0, :]
                    h_e = P.do("scalar", [h_s, st4["ptfree"][sbi]], "activation", out=sv(pt[sbi]), in_=sv(pss[sbi]),
                               func=AF.Exp, scale=0.125)
                    st4["pssfree"][sbi] = h_e
                    state[bi] = h_e

                def emit_PV(bi):
                    r, i = blocks[bi]
                    sbi = (st4["nblk"] + bi) % 2
                    hasA = i >= 0
                    hasB = i + 1 <= n - 1
                    h_e = state.pop(bi)
                    deps = [h_e, st4["psofree"][sbi]]
                    tiles = [tt_ for tt_ in (i, i + 1) if 0 <= tt_ <= n - 1]
                    for tt_ in tiles:
                        deps.append(rd["h_vt"][r * n + tt_])
                    P.wait("tensor", deps)
                    mm = None
                    for hh in range(2):
                        if hasA:
                            mm = nc.tensor.matmul(pso[sbi][:, hh, :], lhsT=VA[:, r * n + i, hh, :],
                                                  rhs=pt[sbi][:, hh, 0, :], start=True, stop=not hasB)
                        if hasB:
                            mm = nc.tensor.matmul(pso[sbi][:, hh, :], lhsT=VA[:, r * n + i + 1, hh, :],
                                                  rhs=pt[sbi][:, hh, 1, :], start=not hasA, stop=True)
                    h_pv = P.sig("tensor", mm)
                    st4["ptfree"][sbi] = h_pv
                    for tt_ in tiles:
                        va_rd[(r * n + tt_) // 4] = h_pv
                    lo = 64 if i == -1 else 0
                    hi = 64 if i == n - 1 else 128
                    m_lo = 128 * i + 64 + lo
                    m_hi = 128 * i + 64 + hi
                    accv = acc[:, :, :].rearrange("p h (m r) -> p h r m", r=d)[:, :, r, m_lo:m_hi]
                    if g == 0:
                        h_u = P.do("vector", [h_pv, st4["accfree"]], "tensor_copy", out=accv, in_=pso[sbi][:, 0:2, lo:hi])
                    else:
                        h_u = P.do("vector", [h_pv, st4["lastacc_prev"]], "tensor_tensor", out=accv, in0=accv,
                                   in1=pso[sbi][:, 0:2, lo:hi], op=ALU.add)
                    st4["psofree"][sbi] = h_u
                    st4["lastacc"] = h_u

                for bi in range(nb_ + 1):
                    if bi < nb_:
                        emit_S(bi)
                    if bi >= 1:
                        emit_PV(bi - 1)
                    if nxt is not None and P4MODE != "nointerleave":
                        budget = 1
                        while budget > 0 and step[0] < len(nxt_reqs) and nxt_reqs[step[0]] <= bi - 1:
                            next(nxt)
                            step[0] += 1
                            budget -= 1
                if nxt is not None:
                    for _ in nxt:
                        pass
                st4["nblk"] += nb_
                st4["lastacc_prev"] = st4["lastacc"]

            st4["lastacc_prev"] = None
            g0 = prep_gen(0)
            for _ in g0:
                pass
            pads(0)
            for idx in range(len(pgs)):
                p, g = pgs[idx]
                if idx + 1 < len(pgs):
                    nxt = prep_gen(idx + 1)
                    reqs = prep_reqs(idx + 1, idx)
                else:
                    nxt, reqs = None, None
                run_blocks(idx, nxt, reqs)
                if idx + 1 < len(pgs):
                    pads(idx + 1)
                if g == 2:
                    hn = []
                    la = st4["lastacc"]
                    for c in range(4):
                        cols = slice(c * 2048, (c + 1) * 2048)
                        oi = st4["nnorm"] % 2
                        st4["nnorm"] += 1
                        h1 = P.do("scalar", [la, st4["ntmpfree"]], "activation", out=ntmp[0:64, :], in_=acc[64:128, 0, cols],
                                  func=AF.Ln)
                        h1 = P.do("scalar", [h1], "activation", out=ntmp[0:64, :], in_=ntmp[0:64, :], func=AF.Exp, scale=-1.0)
                        h2 = P.do("scalar", [la, st4["ntmpfree"]], "activation", out=ntmp[64:128, :], in_=acc[0:64, 1, cols],
                                  func=AF.Ln)
                        h2 = P.do("scalar", [h2], "activation", out=ntmp[64:128, :], in_=ntmp[64:128, :], func=AF.Exp, scale=-1.0)
                        h3 = P.do("vector", [h1, st4["noutfree"][oi]], "tensor_tensor", out=nout[oi][0:64, :],
                                  in0=acc[0:64, 0, cols], in1=ntmp[0:64, :], op=ALU.mult)
                        h4 = P.do("vector", [h2, st4["noutfree"][oi]], "tensor_tensor", out=nout[oi][64:128, :],
                                  in0=acc[64:128, 1, cols], in1=ntmp[64:128, :], op=ALU.mult)
                        st4["ntmpfree"] = [h3, h4]
                        st4["noutfree"][oi] = P.dma("sync", d_no[oi], attT[p * 128:(p + 1) * 128, cols], nout[oi][:, :], h3, h4)
                        hn += [h3, h4]
                    st4["accfree"] = hn
            P.barrier()

        if upto <= 4:
            gt1stack.close()
            return finish(nc, P)

        CH = 512
        NCH = T // CH
        rnn_v = rnnT.rearrange("(k p) t -> p k t", p=128)
        att_v = attT.rearrange("(k p) t -> p k t", p=128)
        sg_v = sgT.rearrange("(a k p) t -> p k a t", p=128, a=2)
        h2_v = h2T.rearrange("(k p) t -> p k t", p=128)
        with ExitStack() as s5:
            Wr = sb("Wr", [128, KC, D], BF16, st=s5)
            Wa = sb("Wa", [128, 4, D], BF16, st=s5)
            Wo = sb("Wo", [128, KC, D], BF16, st=s5)
            ra = [dict(rnn=sb(f"c_rnn{i}", [128, KC, CH], BF16, st=s5), att=sb(f"c_att{i}", [128, 4, CH], BF16, st=s5))
                  for i in range(2)]
            xb = [sb(f"c_x{i}", [128, 4, D], st=s5) for i in range(2)]
            gring = sb("c_g", [128, KC, 2, CH], st=s5)
            merged = sb("merged", [128, KC, CH], BF16, st=s5)
            mt1 = [sb(f"mt1{i}", [128, CH], st=s5) for i in range(2)]
            mt2 = [sb(f"mt2{i}", [128, CH], st=s5) for i in range(2)]
            mt3 = [sb(f"mt3{i}", [128, CH], st=s5) for i in range(2)]
            xn5 = [sb(f"xn5{i}", [128, D], st=s5) for i in range(2)]
            h2st = [sb(f"h2st{i}", [128, KC, CH], BF16, st=s5) for i in range(2)]
            junk5 = sb("junk5", [128, D], BF16, st=s5)
            ss5 = sb("ss5", [128, NT], st=s5)
            rs5 = sb("rs5", [128, NT], st=s5)
            pa = [ps(f"pa{i}", [128, 512], st=s5) for i in range(2)]
            pbb = [ps(f"pbb{i}", [128, 512], st=s5) for i in range(2)]
            pmx = [ps(f"pmx{i}", [128, 512], st=s5) for i in range(2)]
            ptr5 = ps("ptr5", [128, KC, 128], st=s5)
            d_w5 = P.dsem("p5w")
            d_ra = [P.dsem(f"p5l{i}") for i in range(2)]
            d_xl = [P.dsem(f"p5xl{i}") for i in range(2)]
            d_g = [P.dsem(f"p5g{i}") for i in range(KC)]
            d_h2 = [P.dsem(f"p5h{i}") for i in range(2)]
            d_x1 = [P.dsem(f"p5x{i}") for i in range(2)]
            P.dma("gpsimd", d_w5, Wr[:], w_br_rnn.rearrange("(k p) n -> p k n", p=128))
            P.dma("gpsimd", d_w5, Wa[:], w_br_attn.rearrange("(k p) n -> p k n", p=128))
            h_w5 = P.dma("gpsimd", d_w5, Wo[:], w_out.rearrange("(k p) n -> p k n", p=128))
            st5 = dict(rafree=[None, None], xfree=[None, None], gfree=[None] * KC, pafree=[None, None],
                       mt12free=[None, None], pmxfree=[None, None], mt3free=[None, None], mergedfree=None,
                       ptr5free=None, xn5free=[None, None], h2stfree=[None, None], junk=None, na=0, nm=0)
            h_g = [None] * KC
            h_ra = [None, None]
            h_xl = [None, None]
            hx1 = {}
            hev = {}

            def load_ra(c):
                wi = c % 2
                cols = slice(c * CH, (c + 1) * CH)
                P.dma("sync", d_ra[wi], ra[wi]["rnn"][:], rnn_v[:, :, cols], st5["rafree"][wi])
                h_ra[wi] = P.dma("sync", d_ra[wi], ra[wi]["att"][:], att_v[:, :, cols], st5["rafree"][wi])

            def load_x(c):
                wi = c % 2
                cols = slice(c * CH, (c + 1) * CH)
                h_xl[wi] = P.dma("sync", d_xl[wi], xb[wi][:], x_in[cols, :].rearrange("(t p) c -> p t c", p=128),
                                 st5["xfree"][wi])

            def load_g(c, oc):
                cols = slice(c * CH, (c + 1) * CH)
                h_g[oc] = P.dma("sync", d_g[oc], gring[:, oc, :, :], sg_v[:, oc, :, cols], st5["gfree"][oc])

            def A_step(c, oc):
                wi = c % 2
                ab = st5["na"] % 2
                st5["na"] += 1
                ocs = slice(oc * 128, (oc + 1) * 128)
                P.wait("tensor", h_ra[wi], h_w5, st5["pafree"][ab])
                for k in range(KC):
                    nc.tensor.matmul(pa[ab][:, :], lhsT=Wr[:, k, ocs], rhs=ra[wi]["rnn"][:, k, :], start=(k == 0),
                                     stop=(k == KC - 1))
                for k in range(4):
                    mm = nc.tensor.matmul(pbb[ab][:, :], lhsT=Wa[:, k, ocs], rhs=ra[wi]["att"][:, k, :], start=(k == 0),
                                          stop=(k == 3))
                h_mm = P.sig("tensor", mm)
                h1 = P.do("vector", [h_mm, h_g[oc], st5["mt12free"][ab]], "tensor_tensor", out=mt1[ab][:, :], in0=pa[ab][:, :],
                          in1=gring[:, oc, 0, :], op=ALU.mult)
                h2 = P.do("vector", [h_mm, h_g[oc], st5["mt12free"][ab]], "tensor_tensor", out=mt2[ab][:, :], in0=pbb[ab][:, :],
                          in1=gring[:, oc, 1, :], op=ALU.mult)
                st5["pafree"][ab] = [h1, h2]
                st5["gfree"][oc] = [h1, h2]
                h3 = P.do("gpsimd", [h1, h2, st5["mergedfree"]], "tensor_tensor", out=merged[:, oc, :], in0=mt1[ab][:, :],
                          in1=mt2[ab][:, :], op=ALU.add)
                st5["mt12free"][ab] = h3
                if c + 1 < NCH:
                    load_g(c + 1, oc)
                return h_mm, h3

            def B_phase(c, hm_):
                wi = c % 2
                seg = c // (NCH // 2)
                hx1[c] = [[None, None] for _ in range(4)]
                for t in range(4):
                    for half in range(2):
                        mb = st5["nm"] % 2
                        st5["nm"] += 1
                        hs_ = slice(half * 512, (half + 1) * 512)
                        P.wait("tensor", hm_, st5["pmxfree"][mb])
                        for k in range(KC):
                            mm = nc.tensor.matmul(pmx[mb][:, :], lhsT=merged[:, k, t * 128:(t + 1) * 128], rhs=Wo[:, k, hs_],
                                                  start=(k == 0), stop=(k == KC - 1))
                        h_mm = P.sig("tensor", mm)
                        h1 = P.do("vector", [h_mm, st5["mt3free"][mb], h_gt], "tensor_tensor", out=mt3[mb][:, :],
                                  in0=pmx[mb][:, :], in1=gt1row[:, seg, hs_], op=ALU.mult)
                        st5["pmxfree"][mb] = h1
                        h2 = P.do("gpsimd", [h1, h_xl[wi]], "tensor_tensor", out=xb[wi][:, t, hs_], in0=xb[wi][:, t, hs_],
                                  in1=mt3[mb][:, :], op=ALU.add)
                        st5["mt3free"][mb] = h2
                        hx1[c][t][half] = h2
                st5["mergedfree"] = h_mm

            hxn = {}

            def C_norm(c, t):
                wi = c % 2
                i = c * 4 + t
                xb_ = t % 2
                st5["junk"], h_rs = norm_tile(xb[wi][:, t, :], ss5[:, i:i + 1], rs5[:, i:i + 1], None, junk5, hx1[c][t],
                                              st5["junk"])
                hxn[(c, t)] = P.do("vector", [h_rs, st5["xn5free"][xb_]], "tensor_scalar", out=xn5[xb_][:, :],
                                   in0=xb[wi][:, t, :], scalar1=rs5[:, i:i + 1], scalar2=None, op0=ALU.mult)

            def C_trans(c, t):
                wi = c % 2
                seg = c // (NCH // 2)
                xb_ = t % 2
                hst = h2st[wi]
                P.wait("tensor", hxn.pop((c, t)), h_idf, st5["ptr5free"])
                for k in range(KC):
                    tr = nc.tensor.transpose(ptr5[:, k, :], xn5[xb_][:, k * 128:(k + 1) * 128], idf[:])
                h_tr = P.sig("tensor", tr)
                st5["xn5free"][xb_] = h_tr
                hs2 = []
                for k in range(KC):
                    hs2.append(P.do("scalar", [h_tr, h_ss, st5["h2stfree"][wi]], "activation",
                                    out=hst[:, k, t * 128:(t + 1) * 128], in_=ptr5[:, k, :], func=AF.Identity,
                                    scale=scale2[:, k, seg:seg + 1], bias=shift2[:, k, seg:seg + 1]))
                st5["ptr5free"] = hs2
                hev.setdefault(c, [])
                hev[c] += hs2
                if t == 3:
                    cols = slice(c * CH, (c + 1) * CH)
                    st5["h2stfree"][wi] = P.dma("scalar", d_h2[wi], h2_v[:, :, cols], hst[:, :, :], hev[c])
                    st5["xfree"][wi] = P.dma("scalar", d_x1[wi], x1s[cols, :].rearrange("(t p) c -> p t c", p=128),
                                             xb[wi][:, :, :], hev[c])
                    if c + 2 < NCH:
                        load_x(c + 2)

            load_ra(0)
            load_x(0)
            load_x(1)
            for oc in range(KC):
                load_g(0, oc)
            load_ra(1)
            for c in range(NCH + 1):
                if c < NCH:
                    hm_ = []
                    if c >= 1:
                        C_norm(c - 1, 0)
                    for oc in range(KC):
                        h_mm, h3 = A_step(c, oc)
                        hm_.append(h3)
                        if c >= 1 and oc % 2 == 1:
                            C_trans(c - 1, oc // 2)
                            if oc // 2 + 1 < 4:
                                C_norm(c - 1, oc // 2 + 1)
                    st5["rafree"][c % 2] = h_mm
                    if c + 2 < NCH:
                        load_ra(c + 2)
                    B_phase(c, hm_)
                else:
                    for t in range(4):
                        C_norm(c - 1, t)
                        C_trans(c - 1, t)
            P.barrier()

        if upto <= 5:
            gt1stack.close()
            return finish(nc, P)

        gt1stack.close()
        CF = 512
        NCF = T // CF
        with ExitStack() as s6:
            W1 = sb("W1", [128, KC, 2 * D_FF], BF16, st=s6)
            W2 = sb("W2", [128, FC, D], BF16, st=s6)
            h2c_ = [sb(f"h2c{i}", [128, KC, CF], BF16, st=s6) for i in range(2)]
            x1t = [sb(f"x1t{i}", [128, D], st=s6) for i in range(2)]
            hid = sb("hid", [128, FC, CF], BF16, st=s6)
            sgt = [sb(f"sgt{i}", [128, CF], st=s6) for i in range(2)]
            ft = [sb(f"ft{i}", [128, 512], st=s6) for i in range(2)]
            junk6 = sb("junk6", [128, D], BF16, st=s6)
            ss6 = sb("ss6", [128, NT], st=s6)
            rs6 = sb("rs6", [128, NT], st=s6)
            pgg = [ps(f"pgg{i}", [128, 512], st=s6) for i in range(2)]
            puu = [ps(f"puu{i}", [128, 512], st=s6) for i in range(2)]
            poo = [ps(f"poo{i}", [128, 512], st=s6) for i in range(2)]
            d_w1 = P.dsem("p6w1")
            d_w2 = P.dsem("p6w2")
            d_l6 = [P.dsem(f"p6l{i}") for i in range(2)]
            d_xt = [P.dsem(f"p6x{i}") for i in range(2)]
            w1v = w_ffn_in.rearrange("(k p) n -> p k n", p=128)
            for k in range(KC):
                h_w1 = P.dma("gpsimd", d_w1, W1[:, k, :], w1v[:, k, :])
            w2v = w_ffn_out.rearrange("(f p) n -> p f n", p=128)
            for f0 in range(0, FC, 2):
                h_w2 = P.dma("gpsimd", d_w2, W2[:, f0:f0 + 2, :], w2v[:, f0:f0 + 2, :])
            h_h2free = [None, None]
            h_xtfree = [None] * 2
            h_pgfree6 = [None, None]
            h_sgtfree = [None, None]
            h_poofree = [None, None]
            h_ftfree = [None, None]
            h_hidfree = None
            h_junk6 = None
            nf = 0
            no = 0
            nx = 0
            for c in range(NCF):
                seg = c // (NCF // 2)
                cols = slice(c * CF, (c + 1) * CF)
                h2c = h2c_[c % 2]
                if c == 0:
                    h_lh_next = P.dma("sync", d_l6[0], h2c_[0][:], h2_v[:, :, cols], h_h2free[0])
                h_lh = h_lh_next
                if c + 1 < NCF:
                    h_lh_next = P.dma("sync", d_l6[(c + 1) % 2], h2c_[(c + 1) % 2][:],
                                      h2_v[:, :, slice((c + 1) * CF, (c + 2) * CF)], h_h2free[(c + 1) % 2])
                hh_ = []
                for f in range(FC):
                    gb = nf % 2
                    nf += 1
                    P.wait("tensor", h_lh, h_w1, h_pgfree6[gb])
                    for k in range(KC):
                        nc.tensor.matmul(pgg[gb][:, :], lhsT=W1[:, k, f * 128:(f + 1) * 128], rhs=h2c[:, k, :],
                                         start=(k == 0), stop=(k == KC - 1))
                    for k in range(KC):
                        mm = nc.tensor.matmul(puu[gb][:, :], lhsT=W1[:, k, D_FF + f * 128:D_FF + (f + 1) * 128],
                                              rhs=h2c[:, k, :], start=(k == 0), stop=(k == KC - 1))
                    h_mm = P.sig("tensor", mm)
                    h1 = P.do("scalar", [h_mm, h_sgtfree[gb]], "activation", out=sgt[gb][:, :], in_=pgg[gb][:, :], func=AF.Silu)
                    h2 = P.do("vector", [h1, h_mm, h_hidfree], "tensor_tensor", out=hid[:, f, :], in0=puu[gb][:, :],
                              in1=sgt[gb][:, :], op=ALU.mult)
                    h_pgfree6[gb] = [h1, h2]
                    h_sgtfree[gb] = h2
                    hh_.append(h2)
                h_h2free[c % 2] = h_mm
                for t in range(4):
                    i = c * 4 + t
                    xi = nx % 2
                    nx += 1
                    rows_ = slice(i * 128, (i + 1) * 128)
                    h_lx = P.dma("sync", d_xt[xi], x1t[xi][:, :], x1s[rows_, :], h_xtfree[xi])
                    hx2 = []
                    for half in range(2):
                        ob = no % 2
                        no += 1
                        hs_ = slice(half * 512, (half + 1) * 512)
                        P.wait("tensor", hh_, h_w2, h_poofree[ob])
                        for f in range(FC):
                            mm = nc.tensor.matmul(poo[ob][:, :], lhsT=hid[:, f, t * 128:(t + 1) * 128], rhs=W2[:, f, hs_],
                                                  start=(f == 0), stop=(f == FC - 1))
                        h_mm = P.sig("tensor", mm)
                        h1 = P.do("vector", [h_mm, h_ftfree[ob], h_gt], "tensor_tensor", out=ft[ob][:, :], in0=poo[ob][:, :],
                                  in1=gt2row[:, seg, hs_], op=ALU.mult)
                        h_poofree[ob] = h1
                        h2 = P.do("gpsimd", [h1, h_lx], "tensor_tensor", out=x1t[xi][:, hs_], in0=x1t[xi][:, hs_],
                                  in1=ft[ob][:, :], op=ALU.add)
                        h_ftfree[ob] = h2
                        hx2.append(h2)
                    h_junk6, h_rs = norm_tile(x1t[xi][:, :], ss6[:, i:i + 1], rs6[:, i:i + 1], None, junk6, hx2, h_junk6)
                    hy = P.do("vector", [h_rs, h_const], "scalar_tensor_tensor", out=x1t[xi][:, :], in0=x1t[xi][:, :],
                              scalar=rs6[:, i:i + 1], in1=fgrow[:, :], op0=ALU.mult, op1=ALU.mult)
                    h_xtfree[xi] = P.dma("sync", d_xt[xi], y_out[rows_, :], x1t[xi][:, :], hy)
                h_hidfree = h_mm
            P.barrier()

        return finish(nc, P)


def finish(nc, P):
    P.barrier()
    return nc


def _core_inputs(core, inp):
    if core < 4:
        x = np.ascontiguousarray(inp["x_prompt"][core])
        c2 = np.stack([inp["c_prompt"][core], inp["c_prompt"][core]], 0)
        connv = 1.0
        pos = np.arange(T)
    else:
        a, b = 2 * (core - 4), 2 * (core - 4) + 1
        x = np.ascontiguousarray(np.concatenate([inp["x_sample"][a], inp["x_sample"][b]], 0))
        c2 = np.stack([inp["c_sample"][a], inp["c_sample"][b]], 0)
        connv = 0.0
        pos = np.concatenate([np.arange(SEG), np.arange(SEG)])
    return x, c2, connv, pos


def _fm(v):
    return np.ascontiguousarray(np.asarray(v, np.float32).reshape(-1, 128).T)


def _shared_inputs(inp):
    vecs = np.zeros((128, NV), np.float32)
    vecs[:, V_BADA:V_BADA + 48] = _fm(inp["b_ada"][0])
    vecs[:, V_N1G:V_N1G + 8] = _fm(inp["norm1_g"][0])
    vecs[:, V_N2G:V_N2G + 8] = _fm(inp["norm2_g"][0])
    for k in range(4):
        vecs[:, V_CONVW + k * 8:V_CONVW + k * 8 + 8] = _fm(inp["conv_w"][0, k])
    vecs[:, V_CONVB:V_CONVB + 8] = _fm(inp["conv_b"][0])
    for d in range(2):
        vecs[:, V_BA + d * 8:V_BA + d * 8 + 8] = _fm(inp["rg_ba"][0, d].reshape(-1))
        vecs[:, V_BX + d * 8:V_BX + d * 8 + 8] = _fm(inp["rg_bx"][0, d].reshape(-1))
        vecs[:, V_LAM + d * 8:V_LAM + d * 8 + 8] = _fm(inp["rg_lambda"][0, d])
    sh = {
        "vecs": vecs,
        "fgrow": np.ascontiguousarray(inp["final_g"].reshape(1, D).astype(np.float32)),
        "badarow": np.ascontiguousarray(inp["b_ada"][0].reshape(1, 6 * D).astype(np.float32)),
        "w_ada": np.ascontiguousarray(inp["w_ada"][0]),
        "w_in": np.ascontiguousarray(inp["w_in"][0]),
        "rg_wa": np.ascontiguousarray(inp["rg_wa"][0]),
        "rg_wx": np.ascontiguousarray(inp["rg_wx"][0]),
        "w_br_rnn": np.ascontiguousarray(inp["w_br_rnn"][0]),
        "w_br_attn": np.ascontiguousarray(inp["w_br_attn"][0]),
        "w_out": np.ascontiguousarray(inp["w_out"][0]),
        "w_ffn_in": np.ascontiguousarray(inp["w_ffn_in"][0]),
        "w_ffn_out": np.ascontiguousarray(inp["w_ffn_out"][0]),
    }
    return sh


def _rope_table(pos):
    inv = (ROPE_THETA ** (-(np.arange(0, 16, 2, dtype=np.float32) / np.float32(16)))).astype(np.float32)
    ang = pos.astype(np.float32)[:, None] * inv[None, :]
    tab = np.concatenate([np.cos(ang), np.sin(ang)], -1).astype(np.float32)
    return np.ascontiguousarray(tab.reshape(NT, 128, 16).transpose(1, 0, 2))


def make_in_maps(inp):
    inp = {k: np.asarray(v) for k, v in inp.items()}
    sh = _shared_inputs(inp)
    maps = []
    for core in range(8):
        x, c2, connv, pos = _core_inputs(core, inp)
        m = dict(sh)
        m["x"] = x
        m["cT"] = np.ascontiguousarray(c2.astype(np.float32).reshape(2, KC, 128).transpose(2, 1, 0))
        m["conn"] = np.full((128, 1), connv, np.float32)
        m["rope"] = _rope_table(pos)
        maps.append(m)
    return maps


def kernel(**inputs):
    nc = build_program()
    maps = make_in_maps(inputs)
    res = run_bass_kernel_spmd(nc, maps, core_ids=list(range(8)))
    ys = [np.asarray(r["y"], np.float32) for r in res.results]
    y_prompt = np.stack(ys[0:4], 0)
    y_sample = np.stack([ys[4 + i // 2][(i % 2) * SEG:(i % 2 + 1) * SEG] for i in range(8)], 0)
    return (y_prompt, y_sample)
```

```python
import numpy as np
import concourse.bass as bass
import concourse.mybir as mybir
from concourse.bass_utils import run_bass_kernel_spmd
from concourse.alu_op_type import AluOpType as ALU
from contextlib import ExitStack

F32 = mybir.dt.float32
BF16 = mybir.dt.bfloat16
AF = mybir.ActivationFunctionType

D = 1024
T = 8192
SEG = 4096
NT = T // 128
KC = D // 128
IN_COLS = 8704
D_FF = 2816
FC = D_FF // 128
EPS = 1e-6
GROUPS = ((128, 1), (512, 4), (2048, 16))
ROPE_THETA = 500000.0
import os as _os
EVAC = _os.environ.get("K_EVAC", "both")
P4MODE = _os.environ.get("K_P4MODE", "")
BISECT = int(_os.environ.get("K_BISECT", "0"))

V_BADA = 0
V_N1G = 48
V_N2G = 56
V_CONVW = 64
V_CONVB = 96
V_BA = 104
V_BX = 120
V_LAM = 136
NV = 152


class Prog:
    ENG = ("sync", "scalar", "vector", "gpsimd", "tensor")

    def __init__(self, nc, stack):
        self.nc = nc
        self.stack = stack
        self.e = {"sync": nc.sync, "scalar": nc.scalar, "vector": nc.vector,
                  "gpsimd": nc.gpsimd, "tensor": nc.tensor}
        self.sem = {n: stack.enter_context(nc.semaphore("s_" + n)) for n in self.ENG}
        self.cnt = {n: 0 for n in self.ENG}
        self.seen = {n: {} for n in self.ENG}
        self.dsems = []

    def dsem(self, name):
        s = self.stack.enter_context(self.nc.semaphore("d_" + name))
        d = {"sem": s, "cnt": 0, "name": "d_" + name}
        self.dsems.append(d)
        return d

    def wait(self, eng, *deps):
        for d in deps:
            if d is None:
                continue
            if isinstance(d, (list,)):
                self.wait(eng, *d)
                continue
            key, s, val = d
            if self.seen[eng].get(key, 0) < val:
                self.seen[eng][key] = val
                self.e[eng].wait_ge(s, val)

    def sig(self, eng, ins):
        self.cnt[eng] += 1
        ins.then_inc(self.sem[eng], 1)
        return (eng, self.sem[eng], self.cnt[eng])

    def do(self, eng, deps, method, *a, **kw):
        self.wait(eng, deps)
        return self.sig(eng, getattr(self.e[eng], method)(*a, **kw))

    def last(self, eng):
        return (eng, self.sem[eng], self.cnt[eng])

    def dma(self, q, ds, out, in_, *deps, **kw):
        self.wait(q, *deps)
        ins = self.e[q].dma_start(out=out, in_=in_, **kw)
        ds["cnt"] += 16
        ins.then_inc(ds["sem"], 16)
        return (ds["name"], ds["sem"], ds["cnt"])

    def dlast(self, ds):
        return (ds["name"], ds["sem"], ds["cnt"])

    def barrier(self):
        hs = [self.last(n) for n in self.ENG] + [self.dlast(d) for d in self.dsems]
        for n in self.ENG:
            self.wait(n, *[h for h in hs if h[0] != n and h[2] > 0])


def build_program(debug=False, upto=99):
    nc = bass.Bass("TRN2", target_bir_lowering=False)
    dbgset = set(debug) if debug else set()

    def din(name, shape, dt=F32):
        return nc.dram_tensor(name, list(shape), dt, kind="ExternalInput").ap()

    def dscr(name, shape, dt=F32):
        return nc.dram_tensor(name, list(shape), dt, kind=("ExternalOutput" if name in dbgset else "Internal")).ap()

    x_in = din("x", [T, D])
    cT_in = din("cT", [128, KC, 2])
    conn_in = din("conn", [128, 1])
    vecs_in = din("vecs", [128, NV])
    rope_in = din("rope", [128, NT, 16])
    fgrow_in = din("fgrow", [1, D])
    badarow_in = din("badarow", [1, 6 * D])
    w_ada = din("w_ada", [D, 6 * D])
    w_in = din("w_in", [D, IN_COLS])
    rg_wa = din("rg_wa", [2, 16, 64, 64])
    rg_wx = din("rg_wx", [2, 16, 64, 64])
    w_br_rnn = din("w_br_rnn", [D, D])
    w_br_attn = din("w_br_attn", [512, D])
    w_out = din("w_out", [D, D])
    w_ffn_in = din("w_ffn_in", [D, 2 * D_FF])
    w_ffn_out = din("w_ffn_out", [D_FF, D])
    y_out = nc.dram_tensor("y", [T, D], F32, kind="ExternalOutput").ap()

    xrT = dscr("xrT", [D, T])
    grT = dscr("grT", [D, T])
    sgT = dscr("sgT", [2 * D, T])
    qn = dscr("qn", [3, T, 512], BF16)
    kn = dscr("kn", [3, T, 512], BF16)
    vn = dscr("vn", [3, T, 512], BF16)
    rnnT = dscr("rnnT", [D, T], BF16)
    attT = dscr("attT", [512, T], BF16)
    x1s = dscr("x1s", [T, D])
    h2T = dscr("h2T", [D, T], BF16)
    hT_dbg = dscr("hT_dbg", [D, T], BF16) if "hT_dbg" in dbgset else None

    stack = ExitStack()
    with stack:
        P = Prog(nc, stack)

        def sb(name, shape, dt=F32, st=stack):
            return st.enter_context(nc.sbuf_tensor("sb_" + name, list(shape), dt))

        def ps(name, shape, dt=F32, st=stack):
            return st.enter_context(nc.psum_tensor("ps_" + name, list(shape), dt))

        vecs = sb("vecs", [128, NV])
        conn = sb("conn", [128, 1])
        idf = sb("idf", [128, 128])
        idb = sb("idb", [128, 128], BF16)
        scale1 = sb("scale1", [128, KC, 2])
        shift1 = sb("shift1", [128, KC, 2])
        scale2 = sb("scale2", [128, KC, 2])
        shift2 = sb("shift2", [128, KC, 2])
        spv = sb("spv", [128, 16])
        hspv = sb("hspv", [128, 16])
        hbias = sb("hbias", [128, 32])
        cpow = sb("cpow", [128, 2])
        gt2row = sb("gt2row", [128, 2, D])
        fgrow = sb("fgrow", [128, D])

        d_const = P.dsem("const")
        P.dma("sync", d_const, vecs[:], vecs_in[:, :])
        P.dma("sync", d_const, conn[:], conn_in[:, :])
        h_const = P.dma("sync", d_const, fgrow[:], fgrow_in[0:1, :].broadcast_to([128, D]))

        h_ms = P.do("gpsimd", [], "memset", idf[:], 1.0)
        h_idf = P.do("gpsimd", [h_ms], "affine_select", out=idf[:], in_=idf[:], pattern=[[-1, 128]],
                     compare_op=ALU.is_equal, fill=0.0, base=0, channel_multiplier=1)
        h_idb = P.do("gpsimd", [h_idf], "tensor_copy", out=idb[:], in_=idf[:])
        h_cpow = [P.do("gpsimd", [], "memset", cpow[:, 0:1], -0.5), P.do("gpsimd", [], "memset", cpow[:, 1:2], 0.5)]
        gt1stack = ExitStack()
        gt1row = sb("gt1row", [128, 2, D], st=gt1stack)
        gtrows = [gt1row, gt2row]

        with ExitStack() as s0:
            cT = sb("cT", [128, KC, 2], st=s0)
            scT = sb("scT", [128, KC, 2], st=s0)
            screp = sb("screp", [128, 2, KC, 128], st=s0)
            wbuf = [sb("wa", [128, KC, D], st=s0), sb("wa2", [128, KC, D], st=s0)]
            modT = sb("modT", [128, 4, KC, 2], st=s0)
            brow = sb("brow", [128, 2, D], st=s0)
            tmp16 = sb("tmp16", [128, 16], st=s0)
            pm = [ps("pm0", [128, 512], st=s0), ps("pm1", [128, 512], st=s0)]
            d_c = P.dsem("p0c")
            d_w = [P.dsem("p0w0"), P.dsem("p0w1")]

            h_c = P.dma("sync", d_c, cT[:], cT_in[:, :, :])
            h_sc = P.do("scalar", [h_c], "activation", out=scT[:], in_=cT[:], func=AF.Silu)
            h_rep = []
            for s in range(2):
                h_rep.append(P.do("vector", [h_sc], "tensor_copy", out=screp[:, s, :, :],
                                  in_=scT[:, :, s:s + 1].broadcast_to([128, KC, 128])))
            h_t = P.do("scalar", [h_const], "activation", out=tmp16[:], in_=vecs[:, V_LAM:V_LAM + 16],
                       func=AF.Exp, scale=-1.0)
            h_t = P.do("scalar", [h_t], "activation", out=tmp16[:], in_=tmp16[:], func=AF.Ln, bias=1.0)
            P.do("vector", [h_t], "tensor_scalar", out=spv[:], in0=tmp16[:], scalar1=-8.0, scalar2=None, op0=ALU.mult)
            h_spv = [P.do("vector", [h_t], "tensor_scalar", out=hspv[:], in0=tmp16[:], scalar1=-4.0, scalar2=None,
                          op0=ALU.mult),
                     P.do("vector", [h_const], "tensor_scalar", out=hbias[:], in0=vecs[:, V_BA:V_BA + 32], scalar1=0.5,
                          scalar2=None, op0=ALU.mult)]

            w_ada_v = w_ada.rearrange("(k p) n -> p k n", p=128)
            jobs = [(0, 0), (1, 1), (2, 3), (3, 4)]
            h_free = [None, None]
            h_pmfree = [None, None]
            npm = 0
            h_mod = []
            for ji, (mi, col) in enumerate(jobs):
                b = ji % 2
                h_w = P.dma("sync", d_w[b], wbuf[b][:], w_ada_v[:, :, col * D:(col + 1) * D], h_free[b])
                for j in range(KC):
                    pb = npm % 2
                    npm += 1
                    P.wait("tensor", h_w, h_sc, h_pmfree[pb])
                    for k in range(KC):
                        mm = nc.tensor.matmul(pm[pb][:, 0:2], lhsT=wbuf[b][:, k, j * 128:(j + 1) * 128],
                                              rhs=scT[:, k, :], start=(k == 0), stop=(k == KC - 1))
                    h_mm = P.sig("tensor", mm)
                    h_e = P.do("vector", [h_mm, h_const], "tensor_scalar", out=modT[:, mi, j, :], in0=pm[pb][:, 0:2],
                               scalar1=vecs[:, V_BADA + col * 8 + j:V_BADA + col * 8 + j + 1], scalar2=None,
                               op0=ALU.add)
                    h_pmfree[pb] = h_e
                    h_mod.append(h_e)
                h_free[b] = h_mm
            h_ss = []
            for (dst_sc, dst_sh, mi_sh, mi_sc, gcol) in ((scale1, shift1, 0, 1, V_N1G), (scale2, shift2, 2, 3, V_N2G)):
                h1 = P.do("vector", h_mod, "tensor_scalar", out=dst_sc[:], in0=modT[:, mi_sc, :, :], scalar1=1.0,
                          scalar2=None, op0=ALU.add)
                h2 = P.do("vector", [h1], "tensor_tensor", out=dst_sc[:], in0=dst_sc[:],
                          in1=vecs[:, gcol:gcol + KC].unsqueeze(2).broadcast_to([128, KC, 2]), op=ALU.mult)
                h3 = P.do("vector", h_mod, "tensor_copy", out=dst_sh[:], in_=modT[:, mi_sh, :, :])
                h_ss += [h2, h3]
            d_b = P.dsem("p0b")
            for wi, col in enumerate((2, 5)):
                h_b = P.dma("sync", d_b, brow[:, wi, :], badarow_in[0:1, col * D:(col + 1) * D].broadcast_to([128, D]))
            h_gt = []
            for wi, col in enumerate((2, 5)):
                b = wi % 2
                h_w = P.dma("sync", d_w[b], wbuf[b][:], w_ada_v[:, :, col * D:(col + 1) * D], h_free[b])
                for s in range(2):
                    for half in range(2):
                        pb = npm % 2
                        npm += 1
                        P.wait("tensor", h_w, h_rep, h_pmfree[pb])
                        for k in range(KC):
                            mm = nc.tensor.matmul(pm[pb][:, :], lhsT=screp[:, s, k, :],
                                                  rhs=wbuf[b][:, k, half * 512:(half + 1) * 512],
                                                  start=(k == 0), stop=(k == KC - 1))
                        h_mm = P.sig("tensor", mm)
                        h_e = P.do("vector", [h_mm, h_b], "tensor_tensor",
                                   out=gtrows[wi][:, s, half * 512:(half + 1) * 512], in0=pm[pb][:, :],
                                   in1=brow[:, wi, half * 512:(half + 1) * 512], op=ALU.add)
                        h_pmfree[pb] = h_e
                        h_gt.append(h_e)
                h_free[b] = h_mm
            P.barrier()

        if upto <= 0:
            gt1stack.close()
            return finish(nc, P)

        hstack = ExitStack()
        hT = sb("hT", [128, KC, T], BF16, st=hstack)

        def norm_tile(xsrc, ss_col, rs_col, xn_dst, junk, deps, h_junk):
            h1 = P.do("scalar", deps + [h_junk], "activation", out=junk[:], in_=xsrc, func=AF.Square, accum_out=ss_col)
            h2 = P.do("gpsimd", [h1], "tensor_scalar", out=rs_col, in0=ss_col, scalar1=1.0 / D, scalar2=EPS,
                      op0=ALU.mult, op1=ALU.add)
            h4 = P.do("gpsimd", [h2, h_cpow], "tensor_tensor", out=rs_col, in0=rs_col, in1=cpow[:, 0:1], op=ALU.pow)
            return h1, h4

        with ExitStack() as s1:
            NB = 6
            xt = [sb(f"xt{i}", [128, D], st=s1) for i in range(NB)]
            xn = [sb(f"xn{i}", [128, D], st=s1) for i in range(3)]
            junk = sb("junk", [128, D], BF16, st=s1)
            ss = sb("ss", [128, NT], st=s1)
            rs = sb("rs", [128, NT], st=s1)
            pt_ = [ps(f"ptr{i}", [128, KC, 128], st=s1) for i in range(2)]
            d_x = [P.dsem(f"p1x{i}") for i in range(NB)]
            h_xfree = [None] * NB
            h_xnfree = [None] * 3
            h_ptfree = [None] * 2
            h_junk = None
            h_xn_ = {}

            def N1(i):
                nonlocal h_junk
                b = i % NB
                xb_ = i % 3
                h_x = P.dma("sync", d_x[b], xt[b][:], x_in[i * 128:(i + 1) * 128, :], h_xfree[b])
                h_junk, h_rs = norm_tile(xt[b][:], ss[:, i:i + 1], rs[:, i:i + 1], None, junk, [h_x], h_junk)
                h_xn = P.do("vector", [h_rs, h_xnfree[xb_]], "tensor_scalar", out=xn[xb_][:], in0=xt[b][:],
                            scalar1=rs[:, i:i + 1], scalar2=None, op0=ALU.mult)
                h_xfree[b] = h_xn
                h_xn_[i] = h_xn

            def T1(i):
                nb = i % 2
                xb_ = i % 3
                seg = i // (NT // 2)
                P.wait("tensor", h_xn_.pop(i), h_idf, h_ptfree[nb])
                for k in range(KC):
                    tr = nc.tensor.transpose(pt_[nb][:, k, :], xn[xb_][:, k * 128:(k + 1) * 128], idf[:])
                h_tr = P.sig("tensor", tr)
                h_xnfree[xb_] = h_tr
                hs = []
                for k in range(KC):
                    if nb == 0:
                        hs.append(P.do("scalar", [h_tr, h_ss], "activation", out=hT[:, k, i * 128:(i + 1) * 128],
                                       in_=pt_[nb][:, k, :], func=AF.Identity,
                                       scale=scale1[:, k, seg:seg + 1], bias=shift1[:, k, seg:seg + 1]))
                    else:
                        hs.append(P.do("vector", [h_tr, h_ss], "tensor_scalar", out=hT[:, k, i * 128:(i + 1) * 128],
                                       in0=pt_[nb][:, k, :], scalar1=scale1[:, k, seg:seg + 1],
                                       scalar2=shift1[:, k, seg:seg + 1], op0=ALU.mult, op1=ALU.add))
                h_ptfree[nb] = hs

            N1(0)
            N1(1)
            for i in range(NT):
                T1(i)
                if i + 2 < NT:
                    N1(i + 2)
            P.barrier()
            if hT_dbg is not None:
                d_dbg = P.dsem("dbg")
                for k in range(KC):
                    P.dma("sync", d_dbg, hT_dbg[k * 128:(k + 1) * 128, :], hT[:, k, :])
                P.barrier()

        if upto <= 1:
            hstack.close()
            gt1stack.close()
            return finish(nc, P)

        w_in_v = w_in.rearrange("(k p) n -> p k n", p=128)
        with ExitStack() as s2:
            wsl = [sb(f"wsl{i}", [128, KC, 128], BF16, st=s2) for i in range(2)]
            stg = [sb(f"stg{i}", [128, 2048], st=s2) for i in range(2)]
            pz = [ps(f"pz{i}", [128, 512], st=s2) for i in range(2)]
            d_wsl = [P.dsem(f"p2w{i}") for i in range(2)]
            d_stg = [P.dsem(f"p2s{i}") for i in range(2)]
            jobs = []
            for j in range(8):
                jobs.append((xrT, j * 128, j * 128, "copy"))
            for j in range(8):
                jobs.append((grT, j * 128, 1024 + j * 128, "gelu"))
            for j in range(16):
                jobs.append((sgT, j * 128, 6656 + j * 128, "sigmoid"))
            h_wfree = [None, None]
            h_pzfree = [None, None]
            h_stgfree = [None, None]
            ntile = 0
            nstage = 0
            for ji, (dst, row0, col0, mode) in enumerate(jobs):
                b = ji % 2
                h_w = P.dma("gpsimd", d_wsl[b], wsl[b][:], w_in_v[:, :, col0:col0 + 128], h_wfree[b])
                evs = []
                for tt in range(16):
                    pb = ntile % 2
                    ntile += 1
                    sbi = nstage % 2
                    P.wait("tensor", h_w, h_pzfree[pb])
                    for k in range(KC):
                        mm = nc.tensor.matmul(pz[pb][:, :], lhsT=wsl[b][:, k, :], rhs=hT[:, k, tt * 512:(tt + 1) * 512],
                                              start=(k == 0), stop=(k == KC - 1))
                    h_mm = P.sig("tensor", mm)
                    o = stg[sbi][:, (tt % 4) * 512:(tt % 4 + 1) * 512]
                    if mode == "copy" and tt % 2 == 1:
                        h_e = P.do("vector", [h_mm, h_stgfree[sbi]], "tensor_copy", out=o, in_=pz[pb][:, :])
                    else:
                        fn = {"copy": AF.Copy, "gelu": AF.Gelu_apprx_tanh, "sigmoid": AF.Sigmoid}[mode]
                        h_e = P.do("scalar", [h_mm, h_stgfree[sbi]], "activation", out=o, in_=pz[pb][:, :], func=fn)
                    h_pzfree[pb] = h_e
                    evs.append(h_e)
                    if tt % 4 == 3:
                        h_st = P.dma("sync", d_stg[sbi], dst[row0:row0 + 128, (tt // 4) * 2048:(tt // 4 + 1) * 2048],
                                     stg[sbi][:], evs)
                        h_stgfree[sbi] = h_st
                        nstage += 1
                        evs = []
                h_wfree[b] = h_mm
            P.barrier()

        with ExitStack() as s2:
            wq = sb("wq", [128, KC, 1536], BF16, st=s2)
            ropet = sb("ropet", [128, NT, 16], st=s2)
            TB = 2
            stq = [sb(f"stq{i}", [128, TB, 2, 512], BF16, st=s2) for i in range(2)]
            stv = [sb(f"stv{i}", [128, TB, 512], BF16, st=s2) for i in range(2)]
            rtmp = [sb(f"rtmp{i}", [128, 4, 16, 8], st=s2) for i in range(2)]
            pq = [ps(f"pq{i}", [128, 3, 512], st=s2) for i in range(2)]
            d_wq = P.dsem("p2wq")
            d_rp = P.dsem("p2rp")
            d_sq = [P.dsem(f"p2sq{i}") for i in range(2)]
            d_sk = [P.dsem(f"p2sk{i}") for i in range(2)]
            d_sv = [P.dsem(f"p2sv{i}") for i in range(2)]
            h_rp = P.dma("sync", d_rp, ropet[:], rope_in[:, :, :])
            h_wqfree = None
            h_pqfree = [None, None]
            h_stfree = [[None, None], [None, None]]
            h_rtfree = [None, None]
            h_rffree = [None, None]
            rf = [sb(f"rf{i}", [128, 16, 16], st=s2) for i in range(2)]
            nt_ = 0
            for g in range(3):
                hw = []
                for c, base in enumerate((2048, 3584, 5120)):
                    hw.append(P.dma("gpsimd", d_wq, wq[:, :, c * 512:(c + 1) * 512],
                                    w_in_v[:, :, base + g * 512:base + (g + 1) * 512], h_wqfree))
                h_w = hw[-1]
                evq, evv = [], []
                for i in range(NT):
                    pb = nt_ % 2
                    sbi = (nt_ // TB) % 2
                    ti = i % TB
                    nt_ += 1
                    P.wait("tensor", h_w, h_pqfree[pb])
                    for c in range(3):
                        for k in range(KC):
                            mm = nc.tensor.matmul(pq[pb][:, c, :], lhsT=hT[:, k, i * 128:(i + 1) * 128],
                                                  rhs=wq[:, k, c * 512:(c + 1) * 512],
                                                  start=(k == 0), stop=(k == KC - 1))
                    h_mm = P.sig("tensor", mm)
                    h_v = P.do("scalar", [h_mm, h_stfree[sbi][1]], "activation", out=stv[sbi][:, ti, :],
                               in_=pq[pb][:, 2, :], func=AF.Copy)
                    src = pq[pb][:, 0:2, :].rearrange("p a (h e) -> p (a h) e", e=64)
                    dsto = stq[sbi][:, ti, :, :].rearrange("p a (h e) -> p (a h) e", e=64)
                    h_r = P.do("scalar", [h_mm, h_stfree[sbi][0]], "activation", out=stq[sbi][:, ti, :, :],
                               in_=pq[pb][:, 0:2, :], func=AF.Copy)
                    h_rf = P.do("scalar", [h_mm, h_rffree[pb]], "activation", out=rf[pb][:, :, :],
                                in_=src[:, :, 0:16], func=AF.Copy)
                    cosb = ropet[:, i, 0:8].unsqueeze(1).broadcast_to([128, 16, 8])
                    sinb = ropet[:, i, 8:16].unsqueeze(1).broadcast_to([128, 16, 8])
                    x1 = rf[pb][:, :, 0:8]
                    x2 = rf[pb][:, :, 8:16]
                    rt = rtmp[pb]
                    dep0 = [h_rf, h_rp, h_rtfree[pb]]
                    ha = P.do("vector", dep0, "tensor_tensor", out=rt[:, 0, :, :], in0=x1, in1=cosb, op=ALU.mult)
                    hb = P.do("vector", dep0, "tensor_tensor", out=rt[:, 1, :, :], in0=x2, in1=sinb, op=ALU.mult)
                    hc = P.do("vector", dep0, "tensor_tensor", out=rt[:, 2, :, :], in0=x2, in1=cosb, op=ALU.mult)
                    hd = P.do("vector", dep0, "tensor_tensor", out=rt[:, 3, :, :], in0=x1, in1=sinb, op=ALU.mult)
                    ho1 = P.do("vector", [ha, hb, h_r], "tensor_tensor", out=dsto[:, :, 0:8],
                               in0=rt[:, 0, :, :], in1=rt[:, 1, :, :], op=ALU.subtract)
                    ho2 = P.do("vector", [hc, hd, h_r], "tensor_tensor", out=dsto[:, :, 8:16],
                               in0=rt[:, 2, :, :], in1=rt[:, 3, :, :], op=ALU.add)
                    h_rtfree[pb] = [ho1, ho2]
                    h_rffree[pb] = [ha, hb, hc, hd]
                    h_pqfree[pb] = [h_v, h_r, h_rf]
                    evq += [h_r, ho1, ho2]
                    evv.append(h_v)
                    if ti == TB - 1:
                        i0 = i - (TB - 1)
                        rows = slice(i0 * 128, (i0 + TB) * 128)
                        h1 = P.dma("sync", d_sq[sbi], qn[g, rows, :].rearrange("(t p) c -> p t c", p=128),
                                   stq[sbi][:, :, 0, :], evq)
                        h2 = P.dma("sync", d_sk[sbi], kn[g, rows, :].rearrange("(t p) c -> p t c", p=128),
                                   stq[sbi][:, :, 1, :], evq)
                        h3 = P.dma("sync", d_sv[sbi], vn[g, rows, :].rearrange("(t p) c -> p t c", p=128),
                                   stv[sbi][:, :, :], evv)
                        h_stfree[sbi] = [[h1, h2], h3]
                        evq, evv = [], []
                h_wqfree = h_mm
            P.barrier()
        hstack.close()

        if upto <= 2:
            gt1stack.close()
            return finish(nc, P)

        TP = 1024
        NWS = 4
        NPC = T // TP
        with ExitStack() as s3:
            XC = sb("XC", [128, T], st=s3)
            XCB = sb("XCB", [128, T], BF16, st=s3)
            HF = sb("HF", [128, T], st=s3)
            ws = []
            for i in range(NWS):
                ws.append(dict(XR=sb(f"XR{i}", [128, TP + 3], st=s3), R=sb(f"R{i}", [128, TP], st=s3),
                               Pb=sb(f"Pb{i}", [128, TP], st=s3), I=sb(f"I{i}", [128, TP], st=s3),
                               GL=sb(f"GL{i}", [128, TP], st=s3), OUT=sb(f"OUT{i}", [128, TP], BF16, st=s3)))
            Wg = [sb(f"Wg{i}", [128, 4, 128], BF16, st=s3) for i in range(2)]
            carry = sb("carry", [128, 64], st=s3)
            pg = [[ps(f"pg{i}{t}", [128, 512], st=s3) for t in range(2)] for i in range(3)]
            pc = [ps(f"pc{i}", [128, 512], st=s3) for i in range(2)]
            Dk = [sb(f"Dk{i}", [128, 4, 128], st=s3) for i in range(2)]
            h_pcfree = [None, None]
            h_dkfree = [None, None]
            ccount = 0
            d_wg = [P.dsem(f"p3wg{i}") for i in range(2)]
            d_xr = [P.dsem(f"p3xr{i}") for i in range(NWS)]
            d_gl = [P.dsem(f"p3gl{i}") for i in range(NWS)]
            d_out = [P.dsem(f"p3o{i}") for i in range(NWS)]
            h_wgz = [P.do("gpsimd", [], "memset", Wg[i][:], 0.0) for i in range(2)]
            h_wgfree = [None, None]
            h_pgfree = [None] * 3
            free = {k: [None] * NWS for k in ("XR", "R", "Pb", "I", "GL", "OUT")}
            h_xcfree = [None] * NPC
            h_xcbfree = [None] * NPC
            h_hffree = [None] * NPC
            cnt = 0
            xcnt = 0
            gcount = 0

            def gates(j, b, dirn, p, w, wi, h_wg, h_xcb_p):
                nonlocal gcount
                c0 = p * TP
                h_rs, h_is = [], []
                for s_ in range(TP // 512):
                    pb = gcount % 3
                    gcount += 1
                    P.wait("tensor", h_xcb_p, h_wg, h_pgfree[pb])
                    cols = slice(c0 + s_ * 512, c0 + (s_ + 1) * 512)
                    nc.tensor.matmul(pg[pb][0][:, :], lhsT=Wg[b][:, 2 * dirn, :], rhs=XCB[:, cols], start=True, stop=True)
                    h_mm = P.sig("tensor", nc.tensor.matmul(pg[pb][1][:, :], lhsT=Wg[b][:, 2 * dirn + 1, :],
                                                            rhs=XCB[:, cols], start=True, stop=True))
                    sl = slice(s_ * 512, (s_ + 1) * 512)
                    h_r = P.do("scalar", [h_mm, free["R"][wi]], "activation", out=w["R"][:, sl], in_=pg[pb][0][:, :],
                               func=AF.Tanh, scale=0.5, bias=hbias[:, dirn * 8 + j:dirn * 8 + j + 1])
                    h_i = P.do("scalar", [h_mm, free["I"][wi]], "activation", out=w["I"][:, sl], in_=pg[pb][1][:, :],
                               func=AF.Tanh, scale=0.5, bias=hbias[:, 16 + dirn * 8 + j:16 + dirn * 8 + j + 1])
                    h_pgfree[pb] = [h_r, h_i]
                    h_rs.append(h_r)
                    h_is.append(h_i)
                return h_rs, h_is, h_mm

            def au_front(j, dirn, p, w, wi, h_rs):
                sc = spv[:, dirn * 8 + j:dirn * 8 + j + 1]
                hsc = hspv[:, dirn * 8 + j:dirn * 8 + j + 1]
                h_p1 = P.do("scalar", [h_rs, h_spv, free["Pb"][wi]], "activation", out=w["Pb"][:, :], in_=w["R"][:, :],
                            func=AF.Exp, scale=sc, bias=sc)
                h_a = P.do("scalar", [h_p1, h_rs], "activation", out=w["R"][:, :], in_=w["R"][:, :], func=AF.Exp, scale=hsc,
                           bias=hsc)
                return h_p1, h_a

            def au_sqrt(w, h_p1):
                return P.do("scalar", [h_p1], "activation", out=w["Pb"][:, :], in_=w["Pb"][:, :], func=AF.Sqrt,
                            scale=-0.25, bias=0.25)

            def au_back(p, w, h_is, h_xc_p, h_p3):
                c0 = p * TP
                h_i1 = P.do("vector", [h_is, h_xc_p], "scalar_tensor_tensor", out=w["I"][:, :], in0=w["I"][:, :], scalar=1.0,
                            in1=XC[:, c0:c0 + TP], op0=ALU.add, op1=ALU.mult)
                h_i2 = P.do("vector", [h_i1, h_p3], "tensor_tensor", out=w["I"][:, :], in0=w["I"][:, :],
                            in1=w["Pb"][:, :], op=ALU.mult)
                return h_i1, h_i2

            def schedule(n):
                ev = [("F", 0), ("F", 1), ("S", 0), ("S", 1), ("B", 0)]
                k = 2
                while k < n:
                    ev += [("F", k), ("B", k - 1), ("F", k + 1), ("S", k), ("S", k + 1), ("B", k)]
                    k += 2
                ev.append(("B", n - 1))
                return ev

            def prep_chunk(j):
                b = j % 2
                for t, (src, dirn) in enumerate(((rg_wa, 0), (rg_wx, 0), (rg_wa, 1), (rg_wx, 1))):
                    for blk in range(2):
                        h_wg_ = P.dma("gpsimd", d_wg[b], Wg[b][64 * blk:64 * blk + 64, t, 64 * blk:64 * blk + 64],
                                      src[dirn, 2 * j + blk, :, :], h_wgz[b], h_wgfree[b])
                h_dk_ = [P.do("gpsimd", [h_idf, h_const, h_dkfree[b]], "tensor_scalar", out=Dk[b][:, k, :], in0=idf[:],
                              scalar1=vecs[:, V_CONVW + k * 8 + j:V_CONVW + k * 8 + j + 1], scalar2=None, op0=ALU.mult)
                         for k in range(4)]
                return h_wg_, h_dk_

            chunk_prep = {0: prep_chunk(0)}
            chunk_prep_keep = {}
            cstate = {}
            for j in range(KC):
                b = j % 2
                rows = slice(j * 128, (j + 1) * 128)
                h_wg, h_dk = chunk_prep.pop(j)
                chunk_prep_keep[j] = (h_wg, h_dk)
                cw = [vecs[:, V_CONVW + k * 8 + j:V_CONVW + k * 8 + j + 1] for k in range(4)]
                cb = vecs[:, V_CONVB + j:V_CONVB + j + 1]
                cstate.setdefault(j, dict(h_xc=[None] * NPC, h_xcb=[None] * NPC, info={}, done=set()))
                h_xc = cstate[j]["h_xc"]
                h_xcb = cstate[j]["h_xcb"]
                info = cstate[j]["info"]
                h_scan = [None] * NPC
                d1 = j % 2
                d2 = 1 - d1
                order1 = list(range(NPC)) if d1 == 0 else list(range(NPC - 1, -1, -1))
                order2 = list(range(NPC)) if d2 == 0 else list(range(NPC - 1, -1, -1))

                def C_conv(p, jj=None, evac_eng="vector", phase=None):
                    nonlocal xcnt, ccount
                    jj = j if jj is None else jj
                    cs_ = cstate.setdefault(jj, dict(h_xc=[None] * NPC, h_xcb=[None] * NPC, info={}, done=set()))
                    pend = cs_.setdefault("pend", {})
                    if phase == "ev":
                        if p not in pend:
                            return
                        mms = pend.pop(p)
                    else:
                        if p in cs_["done"]:
                            return
                        cs_["done"].add(p)
                        mms = None
                    b = jj % 2
                    rows = slice(jj * 128, (jj + 1) * 128)
                    cb = vecs[:, V_CONVB + jj:V_CONVB + jj + 1]
                    h_dk = (chunk_prep_keep[jj] if jj in chunk_prep_keep else chunk_prep[jj])[1]
                    h_xc, h_xcb, info = cs_["h_xc"], cs_["h_xcb"], cs_["info"]
                    c0 = p * TP
                    if mms is None:
                        wi = xcnt % NWS
                        xcnt += 1
                        w = ws[wi]
                        lo = max(c0 - 2, 0)
                        hi = min(c0 + TP + 1, T)
                        off = lo - (c0 - 2)
                        h_xr = P.dma("sync", d_xr[wi], w["XR"][:, off:off + (hi - lo)], xrT[rows, lo:hi], free["XR"][wi])
                        if p == 0:
                            h_fix = P.do("gpsimd", [free["XR"][wi]], "memset", w["XR"][:, 0:2], 0.0)
                        elif p == NPC - 1:
                            h_fix = P.do("gpsimd", [free["XR"][wi]], "memset", w["XR"][:, TP + 2:TP + 3], 0.0)
                        elif p == NPC // 2 - 1:
                            h_fix = P.do("gpsimd", [h_xr], "tensor_scalar", out=w["XR"][:, TP + 2:TP + 3],
                                         in0=w["XR"][:, TP + 2:TP + 3], scalar1=conn[:, 0:1], scalar2=None, op0=ALU.mult)
                        elif p == NPC // 2:
                            h_fix = P.do("gpsimd", [h_xr], "tensor_scalar", out=w["XR"][:, 0:2], in0=w["XR"][:, 0:2],
                                         scalar1=conn[:, 0:1], scalar2=None, op0=ALU.mult)
                        else:
                            h_fix = None
                        mms = []
                        for s_ in range(TP // 512):
                            cb_ = ccount % 2
                            ccount += 1
                            P.wait("tensor", h_xr, h_fix, h_dk, h_pcfree[cb_])
                            for k in range(4):
                                mm = nc.tensor.matmul(pc[cb_][:, :], lhsT=Dk[b][:, k, :],
                                                      rhs=w["XR"][:, k + s_ * 512:k + s_ * 512 + 512], start=(k == 0),
                                                      stop=(k == 3))
                            h_cm = P.sig("tensor", mm)
                            mms.append((cb_, h_cm))
                        free["XR"][wi] = h_cm
                        h_dkfree[b] = h_cm
                        info[p] = dict()
                        if phase == "mm":
                            pend[p] = mms
                            return
                    hs_c = []
                    hs_b = []
                    for s_, (cb_, h_cm) in enumerate(mms):
                        if evac_eng == "scalar":
                            h_e = P.do("scalar", [h_cm, h_xcfree[p], h_const], "activation",
                                       out=XC[:, c0 + s_ * 512:c0 + (s_ + 1) * 512], in_=pc[cb_][:, :], func=AF.Identity,
                                       bias=cb)
                            h_e2 = P.do("scalar", [h_cm, h_xcbfree[p], h_const], "activation",
                                        out=XCB[:, c0 + s_ * 512:c0 + (s_ + 1) * 512], in_=pc[cb_][:, :], func=AF.Identity,
                                        bias=cb)
                        else:
                            h_e = P.do("vector", [h_cm, h_xcfree[p], h_const], "tensor_scalar",
                                       out=XC[:, c0 + s_ * 512:c0 + (s_ + 1) * 512], in0=pc[cb_][:, :], scalar1=cb,
                                       scalar2=None, op0=ALU.add)
                            h_e2 = P.do("vector", [h_cm, h_xcbfree[p], h_const], "tensor_scalar",
                                        out=XCB[:, c0 + s_ * 512:c0 + (s_ + 1) * 512], in0=pc[cb_][:, :], scalar1=cb,
                                        scalar2=None, op0=ALU.add)
                        h_pcfree[cb_] = [h_e, h_e2]
                        hs_c.append(h_e)
                        hs_b.append(h_e2)
                    h_xc[p] = hs_c
                    h_xcb[p] = hs_b

                def F1(k):
                    nonlocal cnt
                    p = order1[k]
                    wi = cnt % NWS
                    cnt += 1
                    w = ws[wi]
                    info[p].update(wi=wi, w=w)
                    h_rs, h_is, h_lmm = gates(j, b, d1, p, w, wi, h_wg, h_xcb[p])
                    h_p1, h_a = au_front(j, d1, p, w, wi, h_rs)
                    info[p].update(h_is=h_is, h_p1=h_p1, h_a=h_a)

                def S1(k):
                    p = order1[k]
                    info[p]["h_p3"] = au_sqrt(info[p]["w"], info[p]["h_p1"])

                def B1(k):
                    p = order1[k]
                    d_ = info.pop(p)
                    wi, w = d_["wi"], d_["w"]
                    c0 = p * TP
                    h_i1, h_i2 = au_back(p, w, d_["h_is"], h_xc[p], d_["h_p3"])
                    if k == 0:
                        init = 0.0
                        h_init = None
                    else:
                        pp = order1[k - 1]
                        src = HF[:, c0 - 1:c0] if d1 == 0 else HF[:, c0 + TP:c0 + TP + 1]
                        if (d1 == 0 and p == NPC // 2) or (d1 == 1 and p == NPC // 2 - 1):
                            cc = carry[:, j * 8:j * 8 + 1]
                            h_init = P.do("vector", [h_scan[pp]], "tensor_scalar", out=cc, in0=src, scalar1=conn[:, 0:1],
                                          scalar2=None, op0=ALU.mult)
                            init = cc
                        else:
                            init = src
                            h_init = h_scan[pp]
                    hfv = HF[:, c0:c0 + TP]
                    if d1 == 0:
                        o_, a_, u_ = hfv, w["R"][:, :], w["I"][:, :]
                    else:
                        o_, a_, u_ = hfv[:, ::-1], w["R"][:, ::-1], w["I"][:, ::-1]
                    h_scan[p] = P.do("vector", [d_["h_a"], h_i2, h_init, h_hffree[p]], "tensor_tensor_scan",
                                     out=o_, data0=a_, data1=u_, initial=init, op0=ALU.mult, op1=ALU.add)
                    free["R"][wi] = h_scan[p]
                    free["I"][wi] = h_scan[p]
                    free["Pb"][wi] = h_i2

                C_conv(order1[0])
                C_conv(order1[1])
                for (kind, k) in schedule(NPC):
                    if kind == "F":
                        F1(k)
                        if k + 2 < NPC:
                            C_conv(order1[k + 2])
                    else:
                        {"S": S1, "B": B1}[kind](k)

                if j + 1 < KC:
                    chunk_prep[j + 1] = prep_chunk(j + 1)
                bst = dict(h_bprev=None, wi_prev=None, h_lmm=None)

                def F2(k):
                    nonlocal cnt
                    p = order2[k]
                    wi = cnt % NWS
                    cnt += 1
                    w = ws[wi]
                    c0 = p * TP
                    h_gl = P.dma("sync", d_gl[wi], w["GL"][:, :], grT[rows, c0:c0 + TP], free["GL"][wi])
                    h_rs, h_is, h_lmm = gates(j, b, d2, p, w, wi, h_wg, h_xcb[p])
                    h_xcbfree[p] = h_lmm
                    bst["h_lmm"] = h_lmm
                    h_p1, h_a = au_front(j, d2, p, w, wi, h_rs)
                    info[p] = dict(wi=wi, w=w, h_is=h_is, h_p1=h_p1, h_a=h_a, h_gl=h_gl)

                def S2(k):
                    p = order2[k]
                    info[p]["h_p3"] = au_sqrt(info[p]["w"], info[p]["h_p1"])

                def B2(k):
                    p = order2[k]
                    d_ = info.pop(p)
                    wi, w = d_["wi"], d_["w"]
                    c0 = p * TP
                    h_i1, h_i2 = au_back(p, w, d_["h_is"], h_xc[p], d_["h_p3"])
                    h_xcfree[p] = h_i1
                    if k == 0:
                        init = 0.0
                        h_init = None
                    else:
                        wp = bst["wi_prev"]
                        cc = carry[:, j * 8 + 1 + (k % 7):j * 8 + 2 + (k % 7)]
                        src = ws[wp]["Pb"][:, TP - 1:TP] if d2 == 0 else ws[wp]["Pb"][:, 0:1]
                        if (d2 == 0 and p == NPC // 2) or (d2 == 1 and p == NPC // 2 - 1):
                            h_init = P.do("vector", [bst["h_bprev"], h_i2], "tensor_scalar", out=cc, in0=src,
                                          scalar1=conn[:, 0:1], scalar2=None, op0=ALU.mult)
                        else:
                            h_init = P.do("vector", [bst["h_bprev"], h_i2], "tensor_copy", out=cc, in_=src)
                        init = cc
                        free["Pb"][wp] = [free["Pb"][wp], h_init]
                    if d2 == 0:
                        o_, a_, u_ = w["Pb"][:, :], w["R"][:, :], w["I"][:, :]
                    else:
                        o_, a_, u_ = w["Pb"][:, ::-1], w["R"][:, ::-1], w["I"][:, ::-1]
                    h_bs = P.do("vector", [d_["h_a"], h_i2, h_init], "tensor_tensor_scan", out=o_, data0=a_, data1=u_,
                                initial=init, op0=ALU.mult, op1=ALU.add)
                    h_rec = P.do("vector", [h_bs, h_scan[p]], "tensor_tensor", out=w["I"][:, :], in0=HF[:, c0:c0 + TP],
                                 in1=w["Pb"][:, :], op=ALU.add)
                    h_o = P.do("vector", [h_rec, d_["h_gl"], free["OUT"][wi]], "tensor_tensor", out=w["OUT"][:, :],
                               in0=w["I"][:, :], in1=w["GL"][:, :], op=ALU.mult)
                    h_st = P.dma("gpsimd", d_out[wi], rnnT[rows, c0:c0 + TP], w["OUT"][:, :], h_o)
                    free["OUT"][wi] = h_st
                    free["GL"][wi] = h_o
                    free["R"][wi] = h_bs
                    free["I"][wi] = h_o
                    bst["h_bprev"] = h_bs
                    bst["wi_prev"] = wi
                    free["Pb"][wi] = h_rec
                    h_hffree[p] = h_rec

                for (kind, k) in schedule(NPC):
                    {"F": F2, "S": S2, "B": B2}[kind](k)
                    if kind == "B" and k == 1 and j + 1 < KC:
                        nfirst_ = list(range(NPC)) if (j + 1) % 2 == 0 else list(range(NPC - 1, -1, -1))
                        C_conv(nfirst_[0], j + 1, phase="mm")
                    if kind == "S" and k == NPC - 1 and j + 1 < KC:
                        nfirst = list(range(NPC)) if (j + 1) % 2 == 0 else list(range(NPC - 1, -1, -1))
                        C_conv(nfirst[0], j + 1, evac_eng="scalar", phase="ev")
                        C_conv(nfirst[1], j + 1, evac_eng="scalar")
                h_wgfree[b] = bst["h_lmm"]
            P.barrier()

        if upto <= 3:
            gt1stack.close()
            return finish(nc, P)

        with ExitStack() as s4:
            acc = sb("acc", [128, 2, T], st=s4)
            QT = sb("QT", [128, 10240], BF16, st=s4)
            KT = sb("KT", [128, T], BF16, st=s4)
            VA = sb("VA", [128, NT, 2, 128], BF16, st=s4)
            tok = [dict(q=sb(f"tq{i}", [128, 16, 128], BF16, st=s4), k=sb(f"tk{i}", [128, 16, 128], BF16, st=s4),
                        v=sb(f"tv{i}", [128, 16, 128], BF16, st=s4)) for i in range(2)]
            mf = sb("mf", [128, 2, 128], st=s4)
            mstd = sb("mstd", [128, 2, 128], BF16, st=s4)
            mmid = sb("mmid", [128, 2, 128], BF16, st=s4)
            cm1 = sb("cm1", [128, 1], st=s4)
            pt = [sb(f"pt{i}", [128, 2, 2, 128], BF16, st=s4) for i in range(2)]
            ntmp = sb("ntmp", [128, 2048], st=s4)
            nout = [sb(f"nout{i}", [128, 2048], BF16, st=s4) for i in range(2)]
            ptr4 = [ps(f"ptr4{i}", [128, 2, 4, 128], BF16, st=s4) for i in range(2)]
            pss_ = [ps(f"pss{i}", [128, 2, 512], st=s4) for i in range(2)]
            pss = [t_[:, :, 0:256].rearrange("p h (t q) -> p h t q", q=128) for t_ in pss_]
            pso = [ps(f"pso{i}", [128, 4, 128], st=s4) for i in range(2)]
            d_tok = [P.dsem(f"p4t{i}") for i in range(2)]
            d_no = [P.dsem(f"p4n{i}") for i in range(2)]

            h_m = P.do("gpsimd", [], "memset", mf[:], 0.0)
            hm = [P.do("gpsimd", [h_m], "affine_select", out=mf[:, 0, :], in_=mf[:, 0, :], pattern=[[-1, 128]],
                       compare_op=ALU.is_ge, fill=-30000.0, base=0, channel_multiplier=1),
                  P.do("gpsimd", [h_m], "affine_select", out=mf[:, 1, :], in_=mf[:, 1, :], pattern=[[1, 128]],
                       compare_op=ALU.is_ge, fill=-30000.0, base=0, channel_multiplier=-1)]
            h_ms1 = P.do("gpsimd", hm, "tensor_copy", out=mstd[:], in_=mf[:])
            h_cm = P.do("gpsimd", [h_const], "tensor_scalar", out=cm1[:], in0=conn[:], scalar1=-1.0, scalar2=30000.0,
                        op0=ALU.add, op1=ALU.mult)
            hm2 = [P.do("gpsimd", hm + [h_cm, h_ms1], "tensor_scalar", out=mf[:, 0, 64:128], in0=mf[:, 0, 64:128],
                        scalar1=cm1[:, 0:1], scalar2=None, op0=ALU.add),
                   P.do("gpsimd", hm + [h_cm, h_ms1], "tensor_scalar", out=mf[:, 1, 0:64], in0=mf[:, 1, 0:64],
                        scalar1=cm1[:, 0:1], scalar2=None, op0=ALU.add)]
            h_ms2 = P.do("gpsimd", hm2 + [h_ms1], "tensor_copy", out=mmid[:], in_=mf[:])
            h_masks = [h_ms1, h_ms2, h_idb]
            h_va1 = [P.do("gpsimd", [], "memset", VA[:, :, 0, 64:128], 1.0),
                     P.do("gpsimd", [], "memset", VA[:, :, 1, 0:64], 1.0)]

            pgs = [(p, g) for p in range(4) for g in range(3)]

            def geom(g):
                d = GROUPS[g][1]
                n = NT // d
                RL = T // d
                return d, n, RL, RL + 128

            qt_rd = [None] * 20
            kt_rd = [None] * 16
            va_rd = [None] * 16
            ready = {}
            st4 = dict(tokfree=[None, None], trfree=[None, None], pssfree=[None, None], ptfree=[None, None],
                       psofree=[None, None], accfree=None, ntmpfree=None, noutfree=[None, None], lastacc=None,
                       npiece=0, nbatch=0, nblk=0, nnorm=0)

            def prep_gen(idx):
                p, g = pgs[idx]
                d, n, RL, QS = geom(g)
                qv = qn[g].rearrange("(m r) c -> r m c", r=d)
                kv = kn[g].rearrange("(m r) c -> r m c", r=d)
                vv = vn[g].rearrange("(m r) c -> r m c", r=d)
                rd = dict(h_qt=[None] * NT, h_kt=[None] * NT, h_vt=[None] * NT, h_qz=None)
                ready[idx] = rd
                for c in range(4):
                    wi = st4["npiece"] % 2
                    st4["npiece"] += 1
                    tk = tok[wi]
                    for r in range(d):
                        t_lo = max(16 * c, r * n)
                        t_hi = min(16 * c + 16, (r + 1) * n)
                        if t_lo >= t_hi:
                            continue
                        lt0 = t_lo - r * n
                        cntt = t_hi - t_lo
                        to = t_lo - 16 * c
                        for (srcv, dstt) in ((qv, tk["q"]), (kv, tk["k"]), (vv, tk["v"])):
                            h_ld = P.dma("sync", d_tok[wi], dstt[:, to:to + cntt, :],
                                         srcv[r, lt0 * 128:(lt0 + cntt) * 128, p * 128:(p + 1) * 128]
                                         .rearrange("(t q) c -> q t c", q=128), st4["tokfree"][wi])
                    hfree = []
                    for b4 in range(4):
                        pbq = st4["nbatch"] % 2
                        st4["nbatch"] += 1
                        ti0 = 16 * c + 4 * b4
                        r = ti0 // n
                        lt = ti0 % n
                        P.wait("tensor", h_ld, h_idb, st4["trfree"][pbq])
                        for t in range(4):
                            nc.tensor.transpose(ptr4[pbq][:, 0, t, :], tk["q"][:, b4 * 4 + t, :], idb[:])
                        for t in range(4):
                            tr = nc.tensor.transpose(ptr4[pbq][:, 1, t, :], tk["k"][:, b4 * 4 + t, :], idb[:])
                        h_tr = P.sig("tensor", tr)
                        qc = r * QS + 64 + lt * 128
                        kc = r * RL + lt * 128
                        dq = [qt_rd[qc // 512], qt_rd[(qc + 511) // 512]]
                        dk = [kt_rd[kc // 512]]
                        if pbq == 0:
                            h_eq = P.do("scalar", [h_tr] + dq, "activation", out=QT[:, qc:qc + 512],
                                        in_=ptr4[pbq][:, 0, :, :], func=AF.Copy)
                            h_ek = P.do("scalar", [h_tr] + dk, "activation", out=KT[:, kc:kc + 512],
                                        in_=ptr4[pbq][:, 1, :, :], func=AF.Copy)
                        else:
                            h_eq = P.do("vector", [h_tr] + dq, "tensor_copy", out=QT[:, qc:qc + 512],
                                        in_=ptr4[pbq][:, 0, :, :])
                            h_ek = P.do("vector", [h_tr] + dk, "tensor_copy", out=KT[:, kc:kc + 512],
                                        in_=ptr4[pbq][:, 1, :, :])
                        st4["trfree"][pbq] = [h_eq, h_ek]
                        for t in range(4):
                            rd["h_qt"][ti0 + t] = h_eq
                            rd["h_kt"][ti0 + t] = h_ek
                        hfree.append(h_tr)
                        yield
                    dv = [va_rd[4 * c + x] for x in range(4)]
                    h_v0 = P.do("gpsimd", [h_ld, h_va1] + dv, "tensor_copy", out=VA[:, 16 * c:16 * c + 16, 0, 0:64],
                                in_=tk["v"][:, :, 0:64])
                    h_v1 = P.do("gpsimd", [h_ld, h_va1] + dv, "tensor_copy", out=VA[:, 16 * c:16 * c + 16, 1, 64:128],
                                in_=tk["v"][:, :, 64:128])
                    for t in range(16):
                        rd["h_vt"][16 * c + t] = [h_v0, h_v1]
                    st4["tokfree"][wi] = hfree + [h_v0, h_v1]
                    yield

            def prep_reqs(idx_next, idx_cur):
                _, g2 = pgs[idx_next]
                d2, n2, RL2, QS2 = geom(g2)
                _, g1 = pgs[idx_cur]
                d1, n1, RL1, QS1 = geom(g1)
                lastq = [-1] * 20
                lastk = [-1] * 16
                bi = 0
                for r in range(d1):
                    for i in range(-1, n1):
                        qcol = r * QS1 + 128 * (i + 1)
                        for ch in (qcol // 512, (qcol + 127) // 512):
                            lastq[ch] = bi
                        for tt_ in (i, i + 1):
                            if 0 <= tt_ <= n1 - 1:
                                lastk[(r * n1 + tt_) // 4] = bi
                        bi += 1
                reqs = []
                for c in range(4):
                    for b4 in range(4):
                        ti0 = 16 * c + 4 * b4
                        r = ti0 // n2
                        lt = ti0 % n2
                        qc = r * QS2 + 64 + lt * 128
                        kc = r * RL2 + lt * 128
                        reqs.append(max(lastq[qc // 512], lastq[(qc + 511) // 512], lastk[kc // 512]))
                    reqs.append(max(lastk[4 * c + x] for x in range(4)))
                return reqs

            def pads(idx):
                p, g = pgs[idx]
                d, n, RL, QS = geom(g)
                qv_ = QT[:, 0:d * QS].rearrange("p (r c) -> p r c", c=QS)
                deps = [x for x in qt_rd]
                ready[idx]["h_qz"] = [P.do("gpsimd", deps, "memset", qv_[:, :, 0:64], 0.0),
                                      P.do("gpsimd", deps, "memset", qv_[:, :, QS - 64:QS], 0.0)]

            def run_blocks(idx, nxt, nxt_reqs):
                p, g = pgs[idx]
                d, n, RL, QS = geom(g)
                rd = ready[idx]
                blocks = [(r, i) for r in range(d) for i in range(-1, n)]
                nb_ = len(blocks)
                state = {}
                step = [0]

                def emit_S(bi):
                    r, i = blocks[bi]
                    sbi = (st4["nblk"] + bi) % 2
                    hasA = i >= 0
                    hasB = i + 1 <= n - 1
                    qcol = r * QS + 128 * (i + 1)
                    deps = [st4["pssfree"][sbi], h_masks]
                    tiles = [tt_ for tt_ in (i, i + 1) if 0 <= tt_ <= n - 1]
                    for tt_ in tiles:
                        deps += [rd["h_qt"][r * n + tt_], rd["h_kt"][r * n + tt_]]
                    if i == -1 or i == n - 1:
                        deps.append(rd["h_qz"])
                    P.wait("tensor", deps)
                    M = mmid if i == n // 2 - 1 else mstd
                    mm = None
                    for t_ in ((0,) if hasA else ()) + ((1,) if hasB else ()):
                        kcol = r * RL + 128 * (i + t_)
                        for hh in range(2):
                            rows_ = slice(64 * hh, 64 * hh + 64)
                            nc.tensor.matmul(pss[sbi][:, hh, t_, :], lhsT=KT[rows_, kcol:kcol + 128],
                                             rhs=QT[rows_, qcol:qcol + 128], start=True, stop=False)
                        for hh in range(2):
                            mm = nc.tensor.matmul(pss[sbi][:, hh, t_, :], lhsT=idb[:], rhs=M[:, t_, :], start=False, stop=True)
                    h_s = P.sig("tensor", mm)
                    for ch in (qcol // 512, (qcol + 127) // 512):
                        qt_rd[ch] = h_s
                    for tt_ in tiles:
                        kt_rd[(r * n + tt_) // 4] = h_s
                    if hasA and hasB:
                        sv = lambda a: a[:, :, :, :]
                    elif hasB:
                        sv = lambda a: a[:, :, 1, :]
                    else:
                        sv = lambda a: a[:, :, 0, :]
                    h_e = P.do("scalar", [h_s, st4["ptfree"][sbi]], "activation", out=sv(pt[sbi]), in_=sv(pss[sbi]),
                               func=AF.Exp, scale=0.125)
                    st4["pssfree"][sbi] = h_e
                    state[bi] = h_e

                def emit_PV(bi):
                    r, i = blocks[bi]
                    sbi = (st4["nblk"] + bi) % 2
                    hasA = i >= 0
                    hasB = i + 1 <= n - 1
                    h_e = state.pop(bi)
                    deps = [h_e, st4["psofree"][sbi]]
                    tiles = [tt_ for tt_ in (i, i + 1) if 0 <= tt_ <= n - 1]
                    for tt_ in tiles:
                        deps.append(rd["h_vt"][r * n + tt_])
                    P.wait("tensor", deps)
                    mm = None
                    for hh in range(2):
                        if hasA:
                            mm = nc.tensor.matmul(pso[sbi][:, hh, :], lhsT=VA[:, r * n + i, hh, :],
                                                  rhs=pt[sbi][:, hh, 0, :], start=True, stop=not hasB)
                        if hasB:
                            mm = nc.tensor.matmul(pso[sbi][:, hh, :], lhsT=VA[:, r * n + i + 1, hh, :],
                                                  rhs=pt[sbi][:, hh, 1, :], start=not hasA, stop=True)
                    h_pv = P.sig("tensor", mm)
                    st4["ptfree"][sbi] = h_pv
                    for tt_ in tiles:
                        va_rd[(r * n + tt_) // 4] = h_pv
                    lo = 64 if i == -1 else 0
                    hi = 64 if i == n - 1 else 128
                    m_lo = 128 * i + 64 + lo
                    m_hi = 128 * i + 64 + hi
                    accv = acc[:, :, :].rearrange("p h (m r) -> p h r m", r=d)[:, :, r, m_lo:m_hi]
                    if g == 0:
                        h_u = P.do("vector", [h_pv, st4["accfree"]], "tensor_copy", out=accv, in_=pso[sbi][:, 0:2, lo:hi])
                    else:
                        h_u = P.do("vector", [h_pv, st4["lastacc_prev"]], "tensor_tensor", out=accv, in0=accv,
                                   in1=pso[sbi][:, 0:2, lo:hi], op=ALU.add)
                    st4["psofree"][sbi] = h_u
                    st4["lastacc"] = h_u

                for bi in range(nb_ + 1):
                    if bi < nb_:
                        emit_S(bi)
                    if bi >= 1:
                        emit_PV(bi - 1)
                    if nxt is not None and P4MODE != "nointerleave":
                        budget = 1
                        while budget > 0 and step[0] < len(nxt_reqs) and nxt_reqs[step[0]] <= bi - 1:
                            next(nxt)
                            step[0] += 1
                            budget -= 1
                if nxt is not None:
                    for _ in nxt:
                        pass
                st4["nblk"] += nb_
                st4["lastacc_prev"] = st4["lastacc"]

            st4["lastacc_prev"] = None
            g0 = prep_gen(0)
            for _ in g0:
                pass
            pads(0)
            for idx in range(len(pgs)):
                p, g = pgs[idx]
                if idx + 1 < len(pgs):
                    nxt = prep_gen(idx + 1)
                    reqs = prep_reqs(idx + 1, idx)
                else:
                    nxt, reqs = None, None
                run_blocks(idx, nxt, reqs)
                if idx + 1 < len(pgs):
                    pads(idx + 1)
                if g == 2:
                    hn = []
                    la = st4["lastacc"]
                    for c in range(4):
                        cols = slice(c * 2048, (c + 1) * 2048)
                        oi = st4["nnorm"] % 2
                        st4["nnorm"] += 1
                        h1 = P.do("scalar", [la, st4["ntmpfree"]], "activation", out=ntmp[0:64, :], in_=acc[64:128, 0, cols],
                                  func=AF.Ln)
                        h1 = P.do("scalar", [h1], "activation", out=ntmp[0:64, :], in_=ntmp[0:64, :], func=AF.Exp, scale=-1.0)
                        h2 = P.do("scalar", [la, st4["ntmpfree"]], "activation", out=ntmp[64:128, :], in_=acc[0:64, 1, cols],
                                  func=AF.Ln)
                        h2 = P.do("scalar", [h2], "activation", out=ntmp[64:128, :], in_=ntmp[64:128, :], func=AF.Exp, scale=-1.0)
                        h3 = P.do("vector", [h1, st4["noutfree"][oi]], "tensor_tensor", out=nout[oi][0:64, :],
                                  in0=acc[0:64, 0, cols], in1=ntmp[0:64, :], op=ALU.mult)
                        h4 = P.do("vector", [h2, st4["noutfree"][oi]], "tensor_tensor", out=nout[oi][64:128, :],
                                  in0=acc[64:128, 1, cols], in1=ntmp[64:128, :], op=ALU.mult)
                        st4["ntmpfree"] = [h3, h4]
                        st4["noutfree"][oi] = P.dma("sync", d_no[oi], attT[p * 128:(p + 1) * 128, cols], nout[oi][:, :], h3, h4)
                        hn += [h3, h4]
                    st4["accfree"] = hn
            P.barrier()

        if upto <= 4:
            gt1stack.close()
            return finish(nc, P)

        CH = 512
        NCH = T // CH
        rnn_v = rnnT.rearrange("(k p) t -> p k t", p=128)
        att_v = attT.rearrange("(k p) t -> p k t", p=128)
        sg_v = sgT.rearrange("(a k p) t -> p k a t", p=128, a=2)
        h2_v = h2T.rearrange("(k p) t -> p k t", p=128)
        with ExitStack() as s5:
            Wr = sb("Wr", [128, KC, D], BF16, st=s5)
            Wa = sb("Wa", [128, 4, D], BF16, st=s5)
            Wo = sb("Wo", [128, KC, D], BF16, st=s5)
            ra = [dict(rnn=sb(f"c_rnn{i}", [128, KC, CH], BF16, st=s5), att=sb(f"c_att{i}", [128, 4, CH], BF16, st=s5))
                  for i in range(2)]
            xb = [sb(f"c_x{i}", [128, 4, D], st=s5) for i in range(2)]
            gring = sb("c_g", [128, KC, 2, CH], st=s5)
            merged = sb("merged", [128, KC, CH], BF16, st=s5)
            mt1 = [sb(f"mt1{i}", [128, CH], st=s5) for i in range(2)]
            mt2 = [sb(f"mt2{i}", [128, CH], st=s5) for i in range(2)]
            mt3 = [sb(f"mt3{i}", [128, CH], st=s5) for i in range(2)]
            xn5 = [sb(f"xn5{i}", [128, D], st=s5) for i in range(2)]
            h2st = [sb(f"h2st{i}", [128, KC, CH], BF16, st=s5) for i in range(2)]
            junk5 = sb("junk5", [128, D], BF16, st=s5)
            ss5 = sb("ss5", [128, NT], st=s5)
            rs5 = sb("rs5", [128, NT], st=s5)
            pa = [ps(f"pa{i}", [128, 512], st=s5) for i in range(2)]
            pbb = [ps(f"pbb{i}", [128, 512], st=s5) for i in range(2)]
            pmx = [ps(f"pmx{i}", [128, 512], st=s5) for i in range(2)]
            ptr5 = ps("ptr5", [128, KC, 128], st=s5)
            d_w5 = P.dsem("p5w")
            d_ra = [P.dsem(f"p5l{i}") for i in range(2)]
            d_xl = [P.dsem(f"p5xl{i}") for i in range(2)]
            d_g = [P.dsem(f"p5g{i}") for i in range(KC)]
            d_h2 = [P.dsem(f"p5h{i}") for i in range(2)]
            d_x1 = [P.dsem(f"p5x{i}") for i in range(2)]
            P.dma("gpsimd", d_w5, Wr[:], w_br_rnn.rearrange("(k p) n -> p k n", p=128))
            P.dma("gpsimd", d_w5, Wa[:], w_br_attn.rearrange("(k p) n -> p k n", p=128))
            h_w5 = P.dma("gpsimd", d_w5, Wo[:], w_out.rearrange("(k p) n -> p k n", p=128))
            st5 = dict(rafree=[None, None], xfree=[None, None], gfree=[None] * KC, pafree=[None, None],
                       mt12free=[None, None], pmxfree=[None, None], mt3free=[None, None], mergedfree=None,
                       ptr5free=None, xn5free=[None, None], h2stfree=[None, None], junk=None, na=0, nm=0)
            h_g = [None] * KC
            h_ra = [None, None]
            h_xl = [None, None]
            hx1 = {}
            hev = {}

            def load_ra(c):
                wi = c % 2
                cols = slice(c * CH, (c + 1) * CH)
                P.dma("sync", d_ra[wi], ra[wi]["rnn"][:], rnn_v[:, :, cols], st5["rafree"][wi])
                h_ra[wi] = P.dma("sync", d_ra[wi], ra[wi]["att"][:], att_v[:, :, cols], st5["rafree"][wi])

            def load_x(c):
                wi = c % 2
                cols = slice(c * CH, (c + 1) * CH)
                h_xl[wi] = P.dma("sync", d_xl[wi], xb[wi][:], x_in[cols, :].rearrange("(t p) c -> p t c", p=128),
                                 st5["xfree"][wi])

            def load_g(c, oc):
                cols = slice(c * CH, (c + 1) * CH)
                h_g[oc] = P.dma("sync", d_g[oc], gring[:, oc, :, :], sg_v[:, oc, :, cols], st5["gfree"][oc])

            def A_step(c, oc):
                wi = c % 2
                ab = st5["na"] % 2
                st5["na"] += 1
                ocs = slice(oc * 128, (oc + 1) * 128)
                P.wait("tensor", h_ra[wi], h_w5, st5["pafree"][ab])
                for k in range(KC):
                    nc.tensor.matmul(pa[ab][:, :], lhsT=Wr[:, k, ocs], rhs=ra[wi]["rnn"][:, k, :], start=(k == 0),
                                     stop=(k == KC - 1))
                for k in range(4):
                    mm = nc.tensor.matmul(pbb[ab][:, :], lhsT=Wa[:, k, ocs], rhs=ra[wi]["att"][:, k, :], start=(k == 0),
                                          stop=(k == 3))
                h_mm = P.sig("tensor", mm)
                h1 = P.do("vector", [h_mm, h_g[oc], st5["mt12free"][ab]], "tensor_tensor", out=mt1[ab][:, :], in0=pa[ab][:, :],
                          in1=gring[:, oc, 0, :], op=ALU.mult)
                h2 = P.do("vector", [h_mm, h_g[oc], st5["mt12free"][ab]], "tensor_tensor", out=mt2[ab][:, :], in0=pbb[ab][:, :],
                          in1=gring[:, oc, 1, :], op=ALU.mult)
                st5["pafree"][ab] = [h1, h2]
                st5["gfree"][oc] = [h1, h2]
                h3 = P.do("gpsimd", [h1, h2, st5["mergedfree"]], "tensor_tensor", out=merged[:, oc, :], in0=mt1[ab][:, :],
                          in1=mt2[ab][:, :], op=ALU.add)
                st5["mt12free"][ab] = h3
                if c + 1 < NCH:
                    load_g(c + 1, oc)
                return h_mm, h3

            def B_phase(c, hm_):
                wi = c % 2
                seg = c // (NCH // 2)
                hx1[c] = [[None, None] for _ in range(4)]
                for t in range(4):
                    for half in range(2):
                        mb = st5["nm"] % 2
                        st5["nm"] += 1
                        hs_ = slice(half * 512, (half + 1) * 512)
                        P.wait("tensor", hm_, st5["pmxfree"][mb])
                        for k in range(KC):
                            mm = nc.tensor.matmul(pmx[mb][:, :], lhsT=merged[:, k, t * 128:(t + 1) * 128], rhs=Wo[:, k, hs_],
                                                  start=(k == 0), stop=(k == KC - 1))
                        h_mm = P.sig("tensor", mm)
                        h1 = P.do("vector", [h_mm, st5["mt3free"][mb], h_gt], "tensor_tensor", out=mt3[mb][:, :],
                                  in0=pmx[mb][:, :], in1=gt1row[:, seg, hs_], op=ALU.mult)
                        st5["pmxfree"][mb] = h1
                        h2 = P.do("gpsimd", [h1, h_xl[wi]], "tensor_tensor", out=xb[wi][:, t, hs_], in0=xb[wi][:, t, hs_],
                                  in1=mt3[mb][:, :], op=ALU.add)
                        st5["mt3free"][mb] = h2
                        hx1[c][t][half] = h2
                st5["mergedfree"] = h_mm

            hxn = {}

            hrs5 = {}

            def C_stats(c, t):
                wi = c % 2
                i = c * 4 + t
                st5["junk"], hrs5[(c, t)] = norm_tile(xb[wi][:, t, :], ss5[:, i:i + 1], rs5[:, i:i + 1], None, junk5,
                                                      hx1[c][t], st5["junk"])

            def C_xn(c, t):
                wi = c % 2
                i = c * 4 + t
                xb_ = t % 2
                hxn[(c, t)] = P.do("vector", [hrs5.pop((c, t)), st5["xn5free"][xb_]], "tensor_scalar", out=xn5[xb_][:, :],
                                   in0=xb[wi][:, t, :], scalar1=rs5[:, i:i + 1], scalar2=None, op0=ALU.mult)

            def C_trans(c, t):
                wi = c % 2
                seg = c // (NCH // 2)
                xb_ = t % 2
                hst = h2st[wi]
                P.wait("tensor", hxn.pop((c, t)), h_idf, st5["ptr5free"])
                for k in range(KC):
                    tr = nc.tensor.transpose(ptr5[:, k, :], xn5[xb_][:, k * 128:(k + 1) * 128], idf[:])
                h_tr = P.sig("tensor", tr)
                st5["xn5free"][xb_] = h_tr
                hs2 = []
                for k in range(KC):
                    hs2.append(P.do("scalar", [h_tr, h_ss, st5["h2stfree"][wi]], "activation",
                                    out=hst[:, k, t * 128:(t + 1) * 128], in_=ptr5[:, k, :], func=AF.Identity,
                                    scale=scale2[:, k, seg:seg + 1], bias=shift2[:, k, seg:seg + 1]))
                st5["ptr5free"] = hs2
                hev.setdefault(c, [])
                hev[c] += hs2
                if t == 3:
                    cols = slice(c * CH, (c + 1) * CH)
                    st5["h2stfree"][wi] = P.dma("scalar", d_h2[wi], h2_v[:, :, cols], hst[:, :, :], hev[c])
                    st5["xfree"][wi] = P.dma("scalar", d_x1[wi], x1s[cols, :].rearrange("(t p) c -> p t c", p=128),
                                             xb[wi][:, :, :], hev[c])
                    if c + 2 < NCH:
                        load_x(c + 2)

            load_ra(0)
            load_x(0)
            load_x(1)
            for oc in range(KC):
                load_g(0, oc)
            load_ra(1)
            for c in range(NCH + 1):
                if c < NCH:
                    hm_ = []
                    if c >= 1:
                        for t_ in range(4):
                            C_stats(c - 1, t_)
                    for oc in range(KC):
                        h_mm, h3 = A_step(c, oc)
                        hm_.append(h3)
                        if c >= 1:
                            if oc % 2 == 0:
                                C_xn(c - 1, oc // 2)
                            else:
                                C_trans(c - 1, oc // 2)
                    st5["rafree"][c % 2] = h_mm
                    if c + 2 < NCH:
                        load_ra(c + 2)
                    B_phase(c, hm_)
                else:
                    for t in range(4):
                        C_stats(c - 1, t)
                    for t in range(4):
                        C_xn(c - 1, t)
                        C_trans(c - 1, t)
            P.barrier()

        if upto <= 5:
            gt1stack.close()
            return finish(nc, P)

        gt1stack.close()
        CF = 512
        NCF = T // CF
        with ExitStack() as s6:
            W1 = sb("W1", [128, KC, 2 * D_FF], BF16, st=s6)
            W2 = sb("W2", [128, FC, D], BF16, st=s6)
            h2c_ = [sb(f"h2c{i}", [128, KC, CF], BF16, st=s6) for i in range(2)]
            x1t = [sb(f"x1t{i}", [128, D], st=s6) for i in range(2)]
            hid = sb("hid", [128, FC, CF], BF16, st=s6)
            sgt = [sb(f"sgt{i}", [128, CF], st=s6) for i in range(2)]
            ft = [sb(f"ft{i}", [128, 512], st=s6) for i in range(2)]
            junk6 = sb("junk6", [128, D], BF16, st=s6)
            ss6 = sb("ss6", [128, NT], st=s6)
            rs6 = sb("rs6", [128, NT], st=s6)
            pgg = [ps(f"pgg{i}", [128, 512], st=s6) for i in range(2)]
            puu = [ps(f"puu{i}", [128, 512], st=s6) for i in range(2)]
            poo = [ps(f"poo{i}", [128, 512], st=s6) for i in range(2)]
            d_w1 = P.dsem("p6w1")
            d_w2 = P.dsem("p6w2")
            d_l6 = [P.dsem(f"p6l{i}") for i in range(2)]
            d_xt = [P.dsem(f"p6x{i}") for i in range(2)]
            w1v = w_ffn_in.rearrange("(k p) n -> p k n", p=128)
            for k in range(KC):
                h_w1 = P.dma("gpsimd", d_w1, W1[:, k, :], w1v[:, k, :])
            w2v = w_ffn_out.rearrange("(f p) n -> p f n", p=128)
            for f0 in range(0, FC, 2):
                h_w2 = P.dma("gpsimd", d_w2, W2[:, f0:f0 + 2, :], w2v[:, f0:f0 + 2, :])
            h_h2free = [None, None]
            h_xtfree = [None] * 2
            h_pgfree6 = [None, None]
            h_sgtfree = [None, None]
            h_poofree = [None, None]
            h_ftfree = [None, None]
            h_hidfree = None
            h_junk6 = None
            nf = 0
            no = 0
            nx = 0
            for c in range(NCF):
                seg = c // (NCF // 2)
                cols = slice(c * CF, (c + 1) * CF)
                h2c = h2c_[c % 2]
                if c == 0:
                    h_lh_next = P.dma("sync", d_l6[0], h2c_[0][:], h2_v[:, :, cols], h_h2free[0])
                h_lh = h_lh_next
                if c + 1 < NCF:
                    h_lh_next = P.dma("sync", d_l6[(c + 1) % 2], h2c_[(c + 1) % 2][:],
                                      h2_v[:, :, slice((c + 1) * CF, (c + 2) * CF)], h_h2free[(c + 1) % 2])
                hh_ = []
                for f in range(FC):
                    gb = nf % 2
                    nf += 1
                    P.wait("tensor", h_lh, h_w1, h_pgfree6[gb])
                    for k in range(KC):
                        nc.tensor.matmul(pgg[gb][:, :], lhsT=W1[:, k, f * 128:(f + 1) * 128], rhs=h2c[:, k, :],
                                         start=(k == 0), stop=(k == KC - 1))
                    for k in range(KC):
                        mm = nc.tensor.matmul(puu[gb][:, :], lhsT=W1[:, k, D_FF + f * 128:D_FF + (f + 1) * 128],
                                              rhs=h2c[:, k, :], start=(k == 0), stop=(k == KC - 1))
                    h_mm = P.sig("tensor", mm)
                    h1 = P.do("scalar", [h_mm, h_sgtfree[gb]], "activation", out=sgt[gb][:, :], in_=pgg[gb][:, :], func=AF.Silu)
                    h2 = P.do("vector", [h1, h_mm, h_hidfree], "tensor_tensor", out=hid[:, f, :], in0=puu[gb][:, :],
                              in1=sgt[gb][:, :], op=ALU.mult)
                    h_pgfree6[gb] = [h1, h2]
                    h_sgtfree[gb] = h2
                    hh_.append(h2)
                h_h2free[c % 2] = h_mm
                for t in range(4):
                    i = c * 4 + t
                    xi = nx % 2
                    nx += 1
                    rows_ = slice(i * 128, (i + 1) * 128)
                    h_lx = P.dma("sync", d_xt[xi], x1t[xi][:, :], x1s[rows_, :], h_xtfree[xi])
                    hx2 = []
                    for half in range(2):
                        ob = no % 2
                        no += 1
                        hs_ = slice(half * 512, (half + 1) * 512)
                        P.wait("tensor", hh_, h_w2, h_poofree[ob])
                        for f in range(FC):
                            mm = nc.tensor.matmul(poo[ob][:, :], lhsT=hid[:, f, t * 128:(t + 1) * 128], rhs=W2[:, f, hs_],
                                                  start=(f == 0), stop=(f == FC - 1))
                        h_mm = P.sig("tensor", mm)
                        h1 = P.do("vector", [h_mm, h_ftfree[ob], h_gt], "tensor_tensor", out=ft[ob][:, :], in0=poo[ob][:, :],
                                  in1=gt2row[:, seg, hs_], op=ALU.mult)
                        h_poofree[ob] = h1
                        h2 = P.do("gpsimd", [h1, h_lx], "tensor_tensor", out=x1t[xi][:, hs_], in0=x1t[xi][:, hs_],
                                  in1=ft[ob][:, :], op=ALU.add)
                        h_ftfree[ob] = h2
                        hx2.append(h2)
                    h_junk6, h_rs = norm_tile(x1t[xi][:, :], ss6[:, i:i + 1], rs6[:, i:i + 1], None, junk6, hx2, h_junk6)
                    hy = P.do("vector", [h_rs, h_const], "scalar_tensor_tensor", out=x1t[xi][:, :], in0=x1t[xi][:, :],
                              scalar=rs6[:, i:i + 1], in1=fgrow[:, :], op0=ALU.mult, op1=ALU.mult)
                    h_xtfree[xi] = P.dma("sync", d_xt[xi], y_out[rows_, :], x1t[xi][:, :], hy)
                h_hidfree = h_mm
            P.barrier()

        return finish(nc, P)


def finish(nc, P):
    P.barrier()
    return nc


def _core_inputs(core, inp):
    if core < 4:
        x = np.ascontiguousarray(inp["x_prompt"][core])
        c2 = np.stack([inp["c_prompt"][core], inp["c_prompt"][core]], 0)
        connv = 1.0
        pos = np.arange(T)
    else:
        a, b = 2 * (core - 4), 2 * (core - 4) + 1
        x = np.ascontiguousarray(np.concatenate([inp["x_sample"][a], inp["x_sample"][b]], 0))
        c2 = np.stack([inp["c_sample"][a], inp["c_sample"][b]], 0)
        connv = 0.0
        pos = np.concatenate([np.arange(SEG), np.arange(SEG)])
    return x, c2, connv, pos


def _fm(v):
    return np.ascontiguousarray(np.asarray(v, np.float32).reshape(-1, 128).T)


def _shared_inputs(inp):
    vecs = np.zeros((128, NV), np.float32)
    vecs[:, V_BADA:V_BADA + 48] = _fm(inp["b_ada"][0])
    vecs[:, V_N1G:V_N1G + 8] = _fm(inp["norm1_g"][0])
    vecs[:, V_N2G:V_N2G + 8] = _fm(inp["norm2_g"][0])
    for k in range(4):
        vecs[:, V_CONVW + k * 8:V_CONVW + k * 8 + 8] = _fm(inp["conv_w"][0, k])
    vecs[:, V_CONVB:V_CONVB + 8] = _fm(inp["conv_b"][0])
    for d in range(2):
        vecs[:, V_BA + d * 8:V_BA + d * 8 + 8] = _fm(inp["rg_ba"][0, d].reshape(-1))
        vecs[:, V_BX + d * 8:V_BX + d * 8 + 8] = _fm(inp["rg_bx"][0, d].reshape(-1))
        vecs[:, V_LAM + d * 8:V_LAM + d * 8 + 8] = _fm(inp["rg_lambda"][0, d])
    sh = {
        "vecs": vecs,
        "fgrow": np.ascontiguousarray(inp["final_g"].reshape(1, D).astype(np.float32)),
        "badarow": np.ascontiguousarray(inp["b_ada"][0].reshape(1, 6 * D).astype(np.float32)),
        "w_ada": np.ascontiguousarray(inp["w_ada"][0]),
        "w_in": np.ascontiguousarray(inp["w_in"][0]),
        "rg_wa": np.ascontiguousarray(inp["rg_wa"][0]),
        "rg_wx": np.ascontiguousarray(inp["rg_wx"][0]),
        "w_br_rnn": np.ascontiguousarray(inp["w_br_rnn"][0]),
        "w_br_attn": np.ascontiguousarray(inp["w_br_attn"][0]),
        "w_out": np.ascontiguousarray(inp["w_out"][0]),
        "w_ffn_in": np.ascontiguousarray(inp["w_ffn_in"][0]),
        "w_ffn_out": np.ascontiguousarray(inp["w_ffn_out"][0]),
    }
    return sh


def _rope_table(pos):
    inv = (ROPE_THETA ** (-(np.arange(0, 16, 2, dtype=np.float32) / np.float32(16)))).astype(np.float32)
    ang = pos.astype(np.float32)[:, None] * inv[None, :]
    tab = np.concatenate([np.cos(ang), np.sin(ang)], -1).astype(np.float32)
    return np.ascontiguousarray(tab.reshape(NT, 128, 16).transpose(1, 0, 2))


def make_in_maps(inp):
    inp = {k: np.asarray(v) for k, v in inp.items()}
    sh = _shared_inputs(inp)
    maps = []
    for core in range(8):
        x, c2, connv, pos = _core_inputs(core, inp)
        m = dict(sh)
        m["x"] = x
        m["cT"] = np.ascontiguousarray(c2.astype(np.float32).reshape(2, KC, 128).transpose(2, 1, 0))
        m["conn"] = np.full((128, 1), connv, np.float32)
        m["rope"] = _rope_table(pos)
        maps.append(m)
    return maps


def kernel(**inputs):
    nc = build_program()
    maps = make_in_maps(inputs)
    res = run_bass_kernel_spmd(nc, maps, core_ids=list(range(8)))
    ys = [np.asarray(r["y"], np.float32) for r in res.results]
    y_prompt = np.stack(ys[0:4], 0)
    y_sample = np.stack([ys[4 + i // 2][(i % 2) * SEG:(i % 2 + 1) * SEG] for i in range(8)], 0)
    return (y_prompt, y_sample)
```

```python
import numpy as np
import concourse.bass as bass
import concourse.mybir as mybir
from concourse.bass_utils import run_bass_kernel_spmd
from concourse.alu_op_type import AluOpType as ALU
from contextlib import ExitStack

F32 = mybir.dt.float32
BF16 = mybir.dt.bfloat16
AF = mybir.ActivationFunctionType

D = 1024
T = 8192
SEG = 4096
NT = T // 128
KC = D // 128
IN_COLS = 8704
D_FF = 2816
FC = D_FF // 128
EPS = 1e-6
GROUPS = ((128, 1), (512, 4), (2048, 16))
ROPE_THETA = 500000.0
import os as _os
EVAC = _os.environ.get("K_EVAC", "both")
P4MODE = _os.environ.get("K_P4MODE", "")
BISECT = int(_os.environ.get("K_BISECT", "0"))

V_BADA = 0
V_N1G = 48
V_N2G = 56
V_CONVW = 64
V_CONVB = 96
V_BA = 104
V_BX = 120
V_LAM = 136
NV = 152


class Prog:
    ENG = ("sync", "scalar", "vector", "gpsimd", "tensor")

    def __init__(self, nc, stack):
        self.nc = nc
        self.stack = stack
        self.e = {"sync": nc.sync, "scalar": nc.scalar, "vector": nc.vector,
                  "gpsimd": nc.gpsimd, "tensor": nc.tensor}
        self.sem = {n: stack.enter_context(nc.semaphore("s_" + n)) for n in self.ENG}
        self.cnt = {n: 0 for n in self.ENG}
        self.seen = {n: {} for n in self.ENG}
        self.dsems = []

    def dsem(self, name):
        s = self.stack.enter_context(self.nc.semaphore("d_" + name))
        d = {"sem": s, "cnt": 0, "name": "d_" + name}
        self.dsems.append(d)
        return d

    def wait(self, eng, *deps):
        for d in deps:
            if d is None:
                continue
            if isinstance(d, (list,)):
                self.wait(eng, *d)
                continue
            key, s, val = d
            if self.seen[eng].get(key, 0) < val:
                self.seen[eng][key] = val
                self.e[eng].wait_ge(s, val)

    def sig(self, eng, ins):
        self.cnt[eng] += 1
        ins.then_inc(self.sem[eng], 1)
        return (eng, self.sem[eng], self.cnt[eng])

    def do(self, eng, deps, method, *a, **kw):
        self.wait(eng, deps)
        return self.sig(eng, getattr(self.e[eng], method)(*a, **kw))

    def last(self, eng):
        return (eng, self.sem[eng], self.cnt[eng])

    def dma(self, q, ds, out, in_, *deps, **kw):
        self.wait(q, *deps)
        ins = self.e[q].dma_start(out=out, in_=in_, **kw)
        ds["cnt"] += 16
        ins.then_inc(ds["sem"], 16)
        return (ds["name"], ds["sem"], ds["cnt"])

    def dlast(self, ds):
        return (ds["name"], ds["sem"], ds["cnt"])

    def barrier(self):
        hs = [self.last(n) for n in self.ENG] + [self.dlast(d) for d in self.dsems]
        for n in self.ENG:
            self.wait(n, *[h for h in hs if h[0] != n and h[2] > 0])


def build_program(debug=False, upto=99):
    nc = bass.Bass("TRN2", target_bir_lowering=False)
    dbgset = set(debug) if debug else set()

    def din(name, shape, dt=F32):
        return nc.dram_tensor(name, list(shape), dt, kind="ExternalInput").ap()

    def dscr(name, shape, dt=F32):
        return nc.dram_tensor(name, list(shape), dt, kind=("ExternalOutput" if name in dbgset else "Internal")).ap()

    x_in = din("x", [T, D])
    cT_in = din("cT", [128, KC, 2])
    conn_in = din("conn", [128, 1])
    vecs_in = din("vecs", [128, NV])
    rope_in = din("rope", [128, NT, 16])
    fgrow_in = din("fgrow", [1, D])
    badarow_in = din("badarow", [1, 6 * D])
    w_ada = din("w_ada", [D, 6 * D])
    w_in = din("w_in", [D, IN_COLS])
    rg_wa = din("rg_wa", [2, 16, 64, 64])
    rg_wx = din("rg_wx", [2, 16, 64, 64])
    w_br_rnn = din("w_br_rnn", [D, D])
    w_br_attn = din("w_br_attn", [512, D])
    w_out = din("w_out", [D, D])
    w_ffn_in = din("w_ffn_in", [D, 2 * D_FF])
    w_ffn_out = din("w_ffn_out", [D_FF, D])
    y_out = nc.dram_tensor("y", [T, D], F32, kind="ExternalOutput").ap()

    xrT = dscr("xrT", [D, T])
    grT = dscr("grT", [D, T])
    sgT = dscr("sgT", [2 * D, T])
    qn = dscr("qn", [3, T, 512], BF16)
    kn = dscr("kn", [3, T, 512], BF16)
    vn = dscr("vn", [3, T, 512], BF16)
    rnnT = dscr("rnnT", [D, T], BF16)
    attT = dscr("attT", [512, T], BF16)
    x1s = dscr("x1s", [T, D])
    h2T = dscr("h2T", [D, T], BF16)
    hT_dbg = dscr("hT_dbg", [D, T], BF16) if "hT_dbg" in dbgset else None

    stack = ExitStack()
    with stack:
        P = Prog(nc, stack)

        def sb(name, shape, dt=F32, st=stack):
            return st.enter_context(nc.sbuf_tensor("sb_" + name, list(shape), dt))

        def ps(name, shape, dt=F32, st=stack):
            return st.enter_context(nc.psum_tensor("ps_" + name, list(shape), dt))

        vecs = sb("vecs", [128, NV])
        conn = sb("conn", [128, 1])
        idf = sb("idf", [128, 128])
        idb = sb("idb", [128, 128], BF16)
        scale1 = sb("scale1", [128, KC, 2])
        shift1 = sb("shift1", [128, KC, 2])
        scale2 = sb("scale2", [128, KC, 2])
        shift2 = sb("shift2", [128, KC, 2])
        spv = sb("spv", [128, 16])
        hspv = sb("hspv", [128, 16])
        hbias = sb("hbias", [128, 32])
        cpow = sb("cpow", [128, 2])
        gt2row = sb("gt2row", [128, 2, D])
        fgrow = sb("fgrow", [128, D])

        d_const = P.dsem("const")
        P.dma("sync", d_const, vecs[:], vecs_in[:, :])
        P.dma("sync", d_const, conn[:], conn_in[:, :])
        h_const = P.dma("sync", d_const, fgrow[:], fgrow_in[0:1, :].broadcast_to([128, D]))

        h_ms = P.do("gpsimd", [], "memset", idf[:], 1.0)
        h_idf = P.do("gpsimd", [h_ms], "affine_select", out=idf[:], in_=idf[:], pattern=[[-1, 128]],
                     compare_op=ALU.is_equal, fill=0.0, base=0, channel_multiplier=1)
        h_idb = P.do("gpsimd", [h_idf], "tensor_copy", out=idb[:], in_=idf[:])
        h_cpow = [P.do("gpsimd", [], "memset", cpow[:, 0:1], -0.5), P.do("gpsimd", [], "memset", cpow[:, 1:2], 0.5)]
        gt1stack = ExitStack()
        gt1row = sb("gt1row", [128, 2, D], st=gt1stack)
        gtrows = [gt1row, gt2row]

        with ExitStack() as s0:
            cT = sb("cT", [128, KC, 2], st=s0)
            scT = sb("scT", [128, KC, 2], st=s0)
            screp = sb("screp", [128, 2, KC, 128], st=s0)
            wbuf = [sb("wa", [128, KC, D], st=s0), sb("wa2", [128, KC, D], st=s0)]
            modT = sb("modT", [128, 4, KC, 2], st=s0)
            brow = sb("brow", [128, 2, D], st=s0)
            tmp16 = sb("tmp16", [128, 16], st=s0)
            pm = [ps("pm0", [128, 512], st=s0), ps("pm1", [128, 512], st=s0)]
            d_c = P.dsem("p0c")
            d_w = [P.dsem("p0w0"), P.dsem("p0w1")]

            h_c = P.dma("sync", d_c, cT[:], cT_in[:, :, :])
            h_sc = P.do("scalar", [h_c], "activation", out=scT[:], in_=cT[:], func=AF.Silu)
            h_rep = []
            for s in range(2):
                h_rep.append(P.do("vector", [h_sc], "tensor_copy", out=screp[:, s, :, :],
                                  in_=scT[:, :, s:s + 1].broadcast_to([128, KC, 128])))
            h_t = P.do("scalar", [h_const], "activation", out=tmp16[:], in_=vecs[:, V_LAM:V_LAM + 16],
                       func=AF.Exp, scale=-1.0)
            h_t = P.do("scalar", [h_t], "activation", out=tmp16[:], in_=tmp16[:], func=AF.Ln, bias=1.0)
            P.do("vector", [h_t], "tensor_scalar", out=spv[:], in0=tmp16[:], scalar1=-8.0, scalar2=None, op0=ALU.mult)
            h_spv = [P.do("vector", [h_t], "tensor_scalar", out=hspv[:], in0=tmp16[:], scalar1=-4.0, scalar2=None,
                          op0=ALU.mult),
                     P.do("vector", [h_const], "tensor_scalar", out=hbias[:], in0=vecs[:, V_BA:V_BA + 32], scalar1=0.5,
                          scalar2=None, op0=ALU.mult)]

            w_ada_v = w_ada.rearrange("(k p) n -> p k n", p=128)
            jobs = [(0, 0), (1, 1), (2, 3), (3, 4)]
            h_free = [None, None]
            h_pmfree = [None, None]
            npm = 0
            h_mod = []
            for ji, (mi, col) in enumerate(jobs):
                b = ji % 2
                h_w = P.dma("sync", d_w[b], wbuf[b][:], w_ada_v[:, :, col * D:(col + 1) * D], h_free[b])
                for j in range(KC):
                    pb = npm % 2
                    npm += 1
                    P.wait("tensor", h_w, h_sc, h_pmfree[pb])
                    for k in range(KC):
                        mm = nc.tensor.matmul(pm[pb][:, 0:2], lhsT=wbuf[b][:, k, j * 128:(j + 1) * 128],
                                              rhs=scT[:, k, :], start=(k == 0), stop=(k == KC - 1))
                    h_mm = P.sig("tensor", mm)
                    h_e = P.do("vector", [h_mm, h_const], "tensor_scalar", out=modT[:, mi, j, :], in0=pm[pb][:, 0:2],
                               scalar1=vecs[:, V_BADA + col * 8 + j:V_BADA + col * 8 + j + 1], scalar2=None,
                               op0=ALU.add)
                    h_pmfree[pb] = h_e
                    h_mod.append(h_e)
                h_free[b] = h_mm
            h_ss = []
            for (dst_sc, dst_sh, mi_sh, mi_sc, gcol) in ((scale1, shift1, 0, 1, V_N1G), (scale2, shift2, 2, 3, V_N2G)):
                h1 = P.do("vector", h_mod, "tensor_scalar", out=dst_sc[:], in0=modT[:, mi_sc, :, :], scalar1=1.0,
                          scalar2=None, op0=ALU.add)
                h2 = P.do("vector", [h1], "tensor_tensor", out=dst_sc[:], in0=dst_sc[:],
                          in1=vecs[:, gcol:gcol + KC].unsqueeze(2).broadcast_to([128, KC, 2]), op=ALU.mult)
                h3 = P.do("vector", h_mod, "tensor_copy", out=dst_sh[:], in_=modT[:, mi_sh, :, :])
                h_ss += [h2, h3]
            d_b = P.dsem("p0b")
            for wi, col in enumerate((2, 5)):
                h_b = P.dma("sync", d_b, brow[:, wi, :], badarow_in[0:1, col * D:(col + 1) * D].broadcast_to([128, D]))
            h_gt = []
            for wi, col in enumerate((2, 5)):
                b = wi % 2
                h_w = P.dma("sync", d_w[b], wbuf[b][:], w_ada_v[:, :, col * D:(col + 1) * D], h_free[b])
                for s in range(2):
                    for half in range(2):
                        pb = npm % 2
                        npm += 1
                        P.wait("tensor", h_w, h_rep, h_pmfree[pb])
                        for k in range(KC):
                            mm = nc.tensor.matmul(pm[pb][:, :], lhsT=screp[:, s, k, :],
                                                  rhs=wbuf[b][:, k, half * 512:(half + 1) * 512],
                                                  start=(k == 0), stop=(k == KC - 1))
                        h_mm = P.sig("tensor", mm)
                        h_e = P.do("vector", [h_mm, h_b], "tensor_tensor",
                                   out=gtrows[wi][:, s, half * 512:(half + 1) * 512], in0=pm[pb][:, :],
                                   in1=brow[:, wi, half * 512:(half + 1) * 512], op=ALU.add)
                        h_pmfree[pb] = h_e
                        h_gt.append(h_e)
                h_free[b] = h_mm
            P.barrier()

        if upto <= 0:
            gt1stack.close()
            return finish(nc, P)

        hstack = ExitStack()
        hT = sb("hT", [128, KC, T], BF16, st=hstack)

        def norm_tile(xsrc, ss_col, rs_col, xn_dst, junk, deps, h_junk):
            h1 = P.do("scalar", deps + [h_junk], "activation", out=junk[:], in_=xsrc, func=AF.Square, accum_out=ss_col)
            h2 = P.do("gpsimd", [h1], "tensor_scalar", out=rs_col, in0=ss_col, scalar1=1.0 / D, scalar2=EPS,
                      op0=ALU.mult, op1=ALU.add)
            h4 = P.do("gpsimd", [h2, h_cpow], "tensor_tensor", out=rs_col, in0=rs_col, in1=cpow[:, 0:1], op=ALU.pow)
            return h1, h4

        with ExitStack() as s1:
            NB = 6
            xt = [sb(f"xt{i}", [128, D], st=s1) for i in range(NB)]
            xn = [sb(f"xn{i}", [128, D], st=s1) for i in range(3)]
            junk = sb("junk", [128, D], BF16, st=s1)
            ss = sb("ss", [128, NT], st=s1)
            rs = sb("rs", [128, NT], st=s1)
            pt_ = [ps(f"ptr{i}", [128, KC, 128], st=s1) for i in range(2)]
            d_x = [P.dsem(f"p1x{i}") for i in range(NB)]
            h_xfree = [None] * NB
            h_xnfree = [None] * 3
            h_ptfree = [None] * 2
            h_junk = None
            h_xn_ = {}
            h_rs_ = {}

            def N1(i):
                nonlocal h_junk
                b = i % NB
                xb_ = i % 3
                h_x = P.dma("sync", d_x[b], xt[b][:], x_in[i * 128:(i + 1) * 128, :], h_xfree[b])
                h_junk, h_rs = norm_tile(xt[b][:], ss[:, i:i + 1], rs[:, i:i + 1], None, junk, [h_x], h_junk)
                h_rs_[i] = h_rs

            def N1x(i):
                b = i % NB
                xb_ = i % 3
                h_xn = P.do("vector", [h_rs_.pop(i), h_xnfree[xb_]], "tensor_scalar", out=xn[xb_][:], in0=xt[b][:],
                            scalar1=rs[:, i:i + 1], scalar2=None, op0=ALU.mult)
                h_xfree[b] = h_xn
                h_xn_[i] = h_xn

            def T1(i):
                nb = i % 2
                xb_ = i % 3
                seg = i // (NT // 2)
                P.wait("tensor", h_xn_.pop(i), h_idf, h_ptfree[nb])
                for k in range(KC):
                    tr = nc.tensor.transpose(pt_[nb][:, k, :], xn[xb_][:, k * 128:(k + 1) * 128], idf[:])
                h_tr = P.sig("tensor", tr)
                h_xnfree[xb_] = h_tr
                hs = []
                for k in range(KC):
                    if nb == 0:
                        hs.append(P.do("scalar", [h_tr, h_ss], "activation", out=hT[:, k, i * 128:(i + 1) * 128],
                                       in_=pt_[nb][:, k, :], func=AF.Identity,
                                       scale=scale1[:, k, seg:seg + 1], bias=shift1[:, k, seg:seg + 1]))
                    else:
                        hs.append(P.do("vector", [h_tr, h_ss], "tensor_scalar", out=hT[:, k, i * 128:(i + 1) * 128],
                                       in0=pt_[nb][:, k, :], scalar1=scale1[:, k, seg:seg + 1],
                                       scalar2=shift1[:, k, seg:seg + 1], op0=ALU.mult, op1=ALU.add))
                h_ptfree[nb] = hs

            N1(0)
            N1(1)
            N1(2)
            N1x(0)
            N1x(1)
            for i in range(NT):
                T1(i)
                if i + 3 < NT:
                    N1(i + 3)
                if i + 2 < NT:
                    N1x(i + 2)
            P.barrier()
            if hT_dbg is not None:
                d_dbg = P.dsem("dbg")
                for k in range(KC):
                    P.dma("sync", d_dbg, hT_dbg[k * 128:(k + 1) * 128, :], hT[:, k, :])
                P.barrier()

        if upto <= 1:
            hstack.close()
            gt1stack.close()
            return finish(nc, P)

        w_in_v = w_in.rearrange("(k p) n -> p k n", p=128)
        with ExitStack() as s2:
            wsl = [sb(f"wsl{i}", [128, KC, 128], BF16, st=s2) for i in range(2)]
            stg = [sb(f"stg{i}", [128, 2048], st=s2) for i in range(2)]
            pz = [ps(f"pz{i}", [128, 512], st=s2) for i in range(2)]
            d_wsl = [P.dsem(f"p2w{i}") for i in range(2)]
            d_stg = [P.dsem(f"p2s{i}") for i in range(2)]
            jobs = []
            for j in range(8):
                jobs.append((xrT, j * 128, j * 128, "copy"))
            for j in range(8):
                jobs.append((grT, j * 128, 1024 + j * 128, "gelu"))
            for j in range(16):
                jobs.append((sgT, j * 128, 6656 + j * 128, "sigmoid"))
            h_wfree = [None, None]
            h_pzfree = [None, None]
            h_stgfree = [None, None]
            ntile = 0
            nstage = 0
            for ji, (dst, row0, col0, mode) in enumerate(jobs):
                b = ji % 2
                h_w = P.dma("gpsimd", d_wsl[b], wsl[b][:], w_in_v[:, :, col0:col0 + 128], h_wfree[b])
                evs = []
                for tt in range(16):
                    pb = ntile % 2
                    ntile += 1
                    sbi = nstage % 2
                    P.wait("tensor", h_w, h_pzfree[pb])
                    for k in range(KC):
                        mm = nc.tensor.matmul(pz[pb][:, :], lhsT=wsl[b][:, k, :], rhs=hT[:, k, tt * 512:(tt + 1) * 512],
                                              start=(k == 0), stop=(k == KC - 1))
                    h_mm = P.sig("tensor", mm)
                    o = stg[sbi][:, (tt % 4) * 512:(tt % 4 + 1) * 512]
                    if mode == "copy" and tt % 2 == 1:
                        h_e = P.do("vector", [h_mm, h_stgfree[sbi]], "tensor_copy", out=o, in_=pz[pb][:, :])
                    else:
                        fn = {"copy": AF.Copy, "gelu": AF.Gelu_apprx_tanh, "sigmoid": AF.Sigmoid}[mode]
                        h_e = P.do("scalar", [h_mm, h_stgfree[sbi]], "activation", out=o, in_=pz[pb][:, :], func=fn)
                    h_pzfree[pb] = h_e
                    evs.append(h_e)
                    if tt % 4 == 3:
                        h_st = P.dma("sync", d_stg[sbi], dst[row0:row0 + 128, (tt // 4) * 2048:(tt // 4 + 1) * 2048],
                                     stg[sbi][:], evs)
                        h_stgfree[sbi] = h_st
                        nstage += 1
                        evs = []
                h_wfree[b] = h_mm
            P.barrier()

        with ExitStack() as s2:
            wq = sb("wq", [128, KC, 1536], BF16, st=s2)
            ropet = sb("ropet", [128, NT, 16], st=s2)
            TB = 2
            stq = [sb(f"stq{i}", [128, TB, 2, 512], BF16, st=s2) for i in range(2)]
            stv = [sb(f"stv{i}", [128, TB, 512], BF16, st=s2) for i in range(2)]
            rtmp = [sb(f"rtmp{i}", [128, 4, 16, 8], st=s2) for i in range(2)]
            pq = [ps(f"pq{i}", [128, 3, 512], st=s2) for i in range(2)]
            d_wq = P.dsem("p2wq")
            d_rp = P.dsem("p2rp")
            d_sq = [P.dsem(f"p2sq{i}") for i in range(2)]
            d_sk = [P.dsem(f"p2sk{i}") for i in range(2)]
            d_sv = [P.dsem(f"p2sv{i}") for i in range(2)]
            h_rp = P.dma("sync", d_rp, ropet[:], rope_in[:, :, :])
            h_wqfree = None
            h_pqfree = [None, None]
            h_stfree = [[None, None], [None, None]]
            h_rtfree = [None, None]
            h_rffree = [None, None]
            rf = [sb(f"rf{i}", [128, 16, 16], st=s2) for i in range(2)]
            nt_ = 0
            for g in range(3):
                hw = []
                for c, base in enumerate((2048, 3584, 5120)):
                    hw.append(P.dma("gpsimd", d_wq, wq[:, :, c * 512:(c + 1) * 512],
                                    w_in_v[:, :, base + g * 512:base + (g + 1) * 512], h_wqfree))
                h_w = hw[-1]
                evq, evv = [], []
                for i in range(NT):
                    pb = nt_ % 2
                    sbi = (nt_ // TB) % 2
                    ti = i % TB
                    nt_ += 1
                    P.wait("tensor", h_w, h_pqfree[pb])
                    for c in range(3):
                        for k in range(KC):
                            mm = nc.tensor.matmul(pq[pb][:, c, :], lhsT=hT[:, k, i * 128:(i + 1) * 128],
                                                  rhs=wq[:, k, c * 512:(c + 1) * 512],
                                                  start=(k == 0), stop=(k == KC - 1))
                    h_mm = P.sig("tensor", mm)
                    h_v = P.do("scalar", [h_mm, h_stfree[sbi][1]], "activation", out=stv[sbi][:, ti, :],
                               in_=pq[pb][:, 2, :], func=AF.Copy)
                    src = pq[pb][:, 0:2, :].rearrange("p a (h e) -> p (a h) e", e=64)
                    dsto = stq[sbi][:, ti, :, :].rearrange("p a (h e) -> p (a h) e", e=64)
                    h_r = P.do("scalar", [h_mm, h_stfree[sbi][0]], "activation", out=stq[sbi][:, ti, :, :],
                               in_=pq[pb][:, 0:2, :], func=AF.Copy)
                    h_rf = P.do("scalar", [h_mm, h_rffree[pb]], "activation", out=rf[pb][:, :, :],
                                in_=src[:, :, 0:16], func=AF.Copy)
                    cosb = ropet[:, i, 0:8].unsqueeze(1).broadcast_to([128, 16, 8])
                    sinb = ropet[:, i, 8:16].unsqueeze(1).broadcast_to([128, 16, 8])
                    x1 = rf[pb][:, :, 0:8]
                    x2 = rf[pb][:, :, 8:16]
                    rt = rtmp[pb]
                    dep0 = [h_rf, h_rp, h_rtfree[pb]]
                    ha = P.do("vector", dep0, "tensor_tensor", out=rt[:, 0, :, :], in0=x1, in1=cosb, op=ALU.mult)
                    hb = P.do("vector", dep0, "tensor_tensor", out=rt[:, 1, :, :], in0=x2, in1=sinb, op=ALU.mult)
                    hc = P.do("vector", dep0, "tensor_tensor", out=rt[:, 2, :, :], in0=x2, in1=cosb, op=ALU.mult)
                    hd = P.do("vector", dep0, "tensor_tensor", out=rt[:, 3, :, :], in0=x1, in1=sinb, op=ALU.mult)
                    ho1 = P.do("vector", [ha, hb, h_r], "tensor_tensor", out=dsto[:, :, 0:8],
                               in0=rt[:, 0, :, :], in1=rt[:, 1, :, :], op=ALU.subtract)
                    ho2 = P.do("vector", [hc, hd, h_r], "tensor_tensor", out=dsto[:, :, 8:16],
                               in0=rt[:, 2, :, :], in1=rt[:, 3, :, :], op=ALU.add)
                    h_rtfree[pb] = [ho1, ho2]
                    h_rffree[pb] = [ha, hb, hc, hd]
                    h_pqfree[pb] = [h_v, h_r, h_rf]
                    evq += [h_r, ho1, ho2]
                    evv.append(h_v)
                    if ti == TB - 1:
                        i0 = i - (TB - 1)
                        rows = slice(i0 * 128, (i0 + TB) * 128)
                        h1 = P.dma("sync", d_sq[sbi], qn[g, rows, :].rearrange("(t p) c -> p t c", p=128),
                                   stq[sbi][:, :, 0, :], evq)
                        h2 = P.dma("sync", d_sk[sbi], kn[g, rows, :].rearrange("(t p) c -> p t c", p=128),
                                   stq[sbi][:, :, 1, :], evq)
                        h3 = P.dma("sync", d_sv[sbi], vn[g, rows, :].rearrange("(t p) c -> p t c", p=128),
                                   stv[sbi][:, :, :], evv)
                        h_stfree[sbi] = [[h1, h2], h3]
                        evq, evv = [], []
                h_wqfree = h_mm
            P.barrier()
        hstack.close()

        if upto <= 2:
            gt1stack.close()
            return finish(nc, P)

        TP = 1024
        NWS = 4
        NPC = T // TP
        with ExitStack() as s3:
            XC = sb("XC", [128, T], st=s3)
            XCB = sb("XCB", [128, T], BF16, st=s3)
            HF = sb("HF", [128, T], st=s3)
            ws = []
            for i in range(NWS):
                ws.append(dict(XR=sb(f"XR{i}", [128, TP + 3], st=s3), R=sb(f"R{i}", [128, TP], st=s3),
                               Pb=sb(f"Pb{i}", [128, TP], st=s3), I=sb(f"I{i}", [128, TP], st=s3),
                               GL=sb(f"GL{i}", [128, TP], st=s3), OUT=sb(f"OUT{i}", [128, TP], BF16, st=s3)))
            Wg = [sb(f"Wg{i}", [128, 4, 128], BF16, st=s3) for i in range(2)]
            carry = sb("carry", [128, 64], st=s3)
            pg = [[ps(f"pg{i}{t}", [128, 512], st=s3) for t in range(2)] for i in range(3)]
            pc = [ps(f"pc{i}", [128, 512], st=s3) for i in range(2)]
            Dk = [sb(f"Dk{i}", [128, 4, 128], st=s3) for i in range(2)]
            h_pcfree = [None, None]
            h_dkfree = [None, None]
            ccount = 0
            d_wg = [P.dsem(f"p3wg{i}") for i in range(2)]
            d_xr = [P.dsem(f"p3xr{i}") for i in range(NWS)]
            d_gl = [P.dsem(f"p3gl{i}") for i in range(NWS)]
            d_out = [P.dsem(f"p3o{i}") for i in range(NWS)]
            h_wgz = [P.do("gpsimd", [], "memset", Wg[i][:], 0.0) for i in range(2)]
            h_wgfree = [None, None]
            h_pgfree = [None] * 3
            free = {k: [None] * NWS for k in ("XR", "R", "Pb", "I", "GL", "OUT")}
            h_xcfree = [None] * NPC
            h_xcbfree = [None] * NPC
            h_hffree = [None] * NPC
            cnt = 0
            xcnt = 0
            gcount = 0

            def gates(j, b, dirn, p, w, wi, h_wg, h_xcb_p):
                nonlocal gcount
                c0 = p * TP
                h_rs, h_is = [], []
                for s_ in range(TP // 512):
                    pb = gcount % 3
                    gcount += 1
                    P.wait("tensor", h_xcb_p, h_wg, h_pgfree[pb])
                    cols = slice(c0 + s_ * 512, c0 + (s_ + 1) * 512)
                    nc.tensor.matmul(pg[pb][0][:, :], lhsT=Wg[b][:, 2 * dirn, :], rhs=XCB[:, cols], start=True, stop=True)
                    h_mm = P.sig("tensor", nc.tensor.matmul(pg[pb][1][:, :], lhsT=Wg[b][:, 2 * dirn + 1, :],
                                                            rhs=XCB[:, cols], start=True, stop=True))
                    sl = slice(s_ * 512, (s_ + 1) * 512)
                    h_r = P.do("scalar", [h_mm, free["R"][wi]], "activation", out=w["R"][:, sl], in_=pg[pb][0][:, :],
                               func=AF.Tanh, scale=0.5, bias=hbias[:, dirn * 8 + j:dirn * 8 + j + 1])
                    h_i = P.do("scalar", [h_mm, free["I"][wi]], "activation", out=w["I"][:, sl], in_=pg[pb][1][:, :],
                               func=AF.Tanh, scale=0.5, bias=hbias[:, 16 + dirn * 8 + j:16 + dirn * 8 + j + 1])
                    h_pgfree[pb] = [h_r, h_i]
                    h_rs.append(h_r)
                    h_is.append(h_i)
                return h_rs, h_is, h_mm

            def au_front(j, dirn, p, w, wi, h_rs):
                sc = spv[:, dirn * 8 + j:dirn * 8 + j + 1]
                hsc = hspv[:, dirn * 8 + j:dirn * 8 + j + 1]
                h_p1 = P.do("scalar", [h_rs, h_spv, free["Pb"][wi]], "activation", out=w["Pb"][:, :], in_=w["R"][:, :],
                            func=AF.Exp, scale=sc, bias=sc)
                h_a = P.do("scalar", [h_p1, h_rs], "activation", out=w["R"][:, :], in_=w["R"][:, :], func=AF.Exp, scale=hsc,
                           bias=hsc)
                return h_p1, h_a

            def au_sqrt(w, h_p1):
                return P.do("scalar", [h_p1], "activation", out=w["Pb"][:, :], in_=w["Pb"][:, :], func=AF.Sqrt,
                            scale=-0.25, bias=0.25)

            def au_back(p, w, h_is, h_xc_p, h_p3):
                c0 = p * TP
                h_i1 = P.do("vector", [h_is, h_xc_p], "scalar_tensor_tensor", out=w["I"][:, :], in0=w["I"][:, :], scalar=1.0,
                            in1=XC[:, c0:c0 + TP], op0=ALU.add, op1=ALU.mult)
                h_i2 = P.do("vector", [h_i1, h_p3], "tensor_tensor", out=w["I"][:, :], in0=w["I"][:, :],
                            in1=w["Pb"][:, :], op=ALU.mult)
                return h_i1, h_i2

            def schedule(n):
                ev = [("F", 0), ("F", 1), ("S", 0), ("S", 1), ("B", 0)]
                k = 2
                while k < n:
                    ev += [("F", k), ("B", k - 1), ("F", k + 1), ("S", k), ("S", k + 1), ("B", k)]
                    k += 2
                ev.append(("B", n - 1))
                return ev

            def prep_chunk(j):
                b = j % 2
                for t, (src, dirn) in enumerate(((rg_wa, 0), (rg_wx, 0), (rg_wa, 1), (rg_wx, 1))):
                    for blk in range(2):
                        h_wg_ = P.dma("gpsimd", d_wg[b], Wg[b][64 * blk:64 * blk + 64, t, 64 * blk:64 * blk + 64],
                                      src[dirn, 2 * j + blk, :, :], h_wgz[b], h_wgfree[b])
                h_dk_ = [P.do("gpsimd", [h_idf, h_const, h_dkfree[b]], "tensor_scalar", out=Dk[b][:, k, :], in0=idf[:],
                              scalar1=vecs[:, V_CONVW + k * 8 + j:V_CONVW + k * 8 + j + 1], scalar2=None, op0=ALU.mult)
                         for k in range(4)]
                return h_wg_, h_dk_

            chunk_prep = {0: prep_chunk(0)}
            chunk_prep_keep = {}
            cstate = {}
            for j in range(KC):
                b = j % 2
                rows = slice(j * 128, (j + 1) * 128)
                h_wg, h_dk = chunk_prep.pop(j)
                chunk_prep_keep[j] = (h_wg, h_dk)
                cw = [vecs[:, V_CONVW + k * 8 + j:V_CONVW + k * 8 + j + 1] for k in range(4)]
                cb = vecs[:, V_CONVB + j:V_CONVB + j + 1]
                cstate.setdefault(j, dict(h_xc=[None] * NPC, h_xcb=[None] * NPC, info={}, done=set()))
                h_xc = cstate[j]["h_xc"]
                h_xcb = cstate[j]["h_xcb"]
                info = cstate[j]["info"]
                h_scan = [None] * NPC
                d1 = j % 2
                d2 = 1 - d1
                order1 = list(range(NPC)) if d1 == 0 else list(range(NPC - 1, -1, -1))
                order2 = list(range(NPC)) if d2 == 0 else list(range(NPC - 1, -1, -1))

                def C_conv(p, jj=None, evac_eng="vector", phase=None):
                    nonlocal xcnt, ccount
                    jj = j if jj is None else jj
                    cs_ = cstate.setdefault(jj, dict(h_xc=[None] * NPC, h_xcb=[None] * NPC, info={}, done=set()))
                    pend = cs_.setdefault("pend", {})
                    if phase == "ev":
                        if p not in pend:
                            return
                        mms = pend.pop(p)
                    else:
                        if p in cs_["done"]:
                            return
                        cs_["done"].add(p)
                        mms = None
                    b = jj % 2
                    rows = slice(jj * 128, (jj + 1) * 128)
                    cb = vecs[:, V_CONVB + jj:V_CONVB + jj + 1]
                    h_dk = (chunk_prep_keep[jj] if jj in chunk_prep_keep else chunk_prep[jj])[1]
                    h_xc, h_xcb, info = cs_["h_xc"], cs_["h_xcb"], cs_["info"]
                    c0 = p * TP
                    if mms is None:
                        wi = xcnt % NWS
                        xcnt += 1
                        w = ws[wi]
                        lo = max(c0 - 2, 0)
                        hi = min(c0 + TP + 1, T)
                        off = lo - (c0 - 2)
                        h_xr = P.dma("sync", d_xr[wi], w["XR"][:, off:off + (hi - lo)], xrT[rows, lo:hi], free["XR"][wi])
                        if p == 0:
                            h_fix = P.do("gpsimd", [free["XR"][wi]], "memset", w["XR"][:, 0:2], 0.0)
                        elif p == NPC - 1:
                            h_fix = P.do("gpsimd", [free["XR"][wi]], "memset", w["XR"][:, TP + 2:TP + 3], 0.0)
                        elif p == NPC // 2 - 1:
                            h_fix = P.do("gpsimd", [h_xr], "tensor_scalar", out=w["XR"][:, TP + 2:TP + 3],
                                         in0=w["XR"][:, TP + 2:TP + 3], scalar1=conn[:, 0:1], scalar2=None, op0=ALU.mult)
                        elif p == NPC // 2:
                            h_fix = P.do("gpsimd", [h_xr], "tensor_scalar", out=w["XR"][:, 0:2], in0=w["XR"][:, 0:2],
                                         scalar1=conn[:, 0:1], scalar2=None, op0=ALU.mult)
                        else:
                            h_fix = None
                        mms = []
                        for s_ in range(TP // 512):
                            cb_ = ccount % 2
                            ccount += 1
                            P.wait("tensor", h_xr, h_fix, h_dk, h_pcfree[cb_])
                            for k in range(4):
                                mm = nc.tensor.matmul(pc[cb_][:, :], lhsT=Dk[b][:, k, :],
                                                      rhs=w["XR"][:, k + s_ * 512:k + s_ * 512 + 512], start=(k == 0),
                                                      stop=(k == 3))
                            h_cm = P.sig("tensor", mm)
                            mms.append((cb_, h_cm))
                        free["XR"][wi] = h_cm
                        h_dkfree[b] = h_cm
                        info[p] = dict()
                        if phase == "mm":
                            pend[p] = mms
                            return
                    hs_c = []
                    hs_b = []
                    for s_, (cb_, h_cm) in enumerate(mms):
                        if evac_eng == "scalar":
                            h_e = P.do("scalar", [h_cm, h_xcfree[p], h_const], "activation",
                                       out=XC[:, c0 + s_ * 512:c0 + (s_ + 1) * 512], in_=pc[cb_][:, :], func=AF.Identity,
                                       bias=cb)
                            h_e2 = P.do("scalar", [h_cm, h_xcbfree[p], h_const], "activation",
                                        out=XCB[:, c0 + s_ * 512:c0 + (s_ + 1) * 512], in_=pc[cb_][:, :], func=AF.Identity,
                                        bias=cb)
                        else:
                            h_e = P.do("vector", [h_cm, h_xcfree[p], h_const], "tensor_scalar",
                                       out=XC[:, c0 + s_ * 512:c0 + (s_ + 1) * 512], in0=pc[cb_][:, :], scalar1=cb,
                                       scalar2=None, op0=ALU.add)
                            h_e2 = P.do("vector", [h_cm, h_xcbfree[p], h_const], "tensor_scalar",
                                        out=XCB[:, c0 + s_ * 512:c0 + (s_ + 1) * 512], in0=pc[cb_][:, :], scalar1=cb,
                                        scalar2=None, op0=ALU.add)
                        h_pcfree[cb_] = [h_e, h_e2]
                        hs_c.append(h_e)
                        hs_b.append(h_e2)
                    h_xc[p] = hs_c
                    h_xcb[p] = hs_b

                def F1(k):
                    nonlocal cnt
                    p = order1[k]
                    wi = cnt % NWS
                    cnt += 1
                    w = ws[wi]
                    info[p].update(wi=wi, w=w)
                    h_rs, h_is, h_lmm = gates(j, b, d1, p, w, wi, h_wg, h_xcb[p])
                    h_p1, h_a = au_front(j, d1, p, w, wi, h_rs)
                    info[p].update(h_is=h_is, h_p1=h_p1, h_a=h_a)

                def S1(k):
                    p = order1[k]
                    info[p]["h_p3"] = au_sqrt(info[p]["w"], info[p]["h_p1"])

                def B1(k):
                    p = order1[k]
                    d_ = info.pop(p)
                    wi, w = d_["wi"], d_["w"]
                    c0 = p * TP
                    h_i1, h_i2 = au_back(p, w, d_["h_is"], h_xc[p], d_["h_p3"])
                    if k == 0:
                        init = 0.0
                        h_init = None
                    else:
                        pp = order1[k - 1]
                        src = HF[:, c0 - 1:c0] if d1 == 0 else HF[:, c0 + TP:c0 + TP + 1]
                        if (d1 == 0 and p == NPC // 2) or (d1 == 1 and p == NPC // 2 - 1):
                            cc = carry[:, j * 8:j * 8 + 1]
                            h_init = P.do("vector", [h_scan[pp]], "tensor_scalar", out=cc, in0=src, scalar1=conn[:, 0:1],
                                          scalar2=None, op0=ALU.mult)
                            init = cc
                        else:
                            init = src
                            h_init = h_scan[pp]
                    hfv = HF[:, c0:c0 + TP]
                    if d1 == 0:
                        o_, a_, u_ = hfv, w["R"][:, :], w["I"][:, :]
                    else:
                        o_, a_, u_ = hfv[:, ::-1], w["R"][:, ::-1], w["I"][:, ::-1]
                    h_scan[p] = P.do("vector", [d_["h_a"], h_i2, h_init, h_hffree[p]], "tensor_tensor_scan",
                                     out=o_, data0=a_, data1=u_, initial=init, op0=ALU.mult, op1=ALU.add)
                    free["R"][wi] = h_scan[p]
                    free["I"][wi] = h_scan[p]
                    free["Pb"][wi] = h_i2

                C_conv(order1[0])
                C_conv(order1[1])
                for (kind, k) in schedule(NPC):
                    if kind == "F":
                        F1(k)
                        if k + 2 < NPC:
                            C_conv(order1[k + 2])
                    else:
                        {"S": S1, "B": B1}[kind](k)

                if j + 1 < KC:
                    chunk_prep[j + 1] = prep_chunk(j + 1)
                bst = dict(h_bprev=None, wi_prev=None, h_lmm=None)

                def F2(k):
                    nonlocal cnt
                    p = order2[k]
                    wi = cnt % NWS
                    cnt += 1
                    w = ws[wi]
                    c0 = p * TP
                    h_gl = P.dma("sync", d_gl[wi], w["GL"][:, :], grT[rows, c0:c0 + TP], free["GL"][wi])
                    h_rs, h_is, h_lmm = gates(j, b, d2, p, w, wi, h_wg, h_xcb[p])
                    h_xcbfree[p] = h_lmm
                    bst["h_lmm"] = h_lmm
                    h_p1, h_a = au_front(j, d2, p, w, wi, h_rs)
                    info[p] = dict(wi=wi, w=w, h_is=h_is, h_p1=h_p1, h_a=h_a, h_gl=h_gl)

                def S2(k):
                    p = order2[k]
                    info[p]["h_p3"] = au_sqrt(info[p]["w"], info[p]["h_p1"])

                def B2(k):
                    p = order2[k]
                    d_ = info.pop(p)
                    wi, w = d_["wi"], d_["w"]
                    c0 = p * TP
                    h_i1, h_i2 = au_back(p, w, d_["h_is"], h_xc[p], d_["h_p3"])
                    h_xcfree[p] = h_i1
                    if k == 0:
                        init = 0.0
                        h_init = None
                    else:
                        wp = bst["wi_prev"]
                        cc = carry[:, j * 8 + 1 + (k % 7):j * 8 + 2 + (k % 7)]
                        src = ws[wp]["Pb"][:, TP - 1:TP] if d2 == 0 else ws[wp]["Pb"][:, 0:1]
                        if (d2 == 0 and p == NPC // 2) or (d2 == 1 and p == NPC // 2 - 1):
                            h_init = P.do("vector", [bst["h_bprev"], h_i2], "tensor_scalar", out=cc, in0=src,
                                          scalar1=conn[:, 0:1], scalar2=None, op0=ALU.mult)
                        else:
                            h_init = P.do("vector", [bst["h_bprev"], h_i2], "tensor_copy", out=cc, in_=src)
                        init = cc
                        free["Pb"][wp] = [free["Pb"][wp], h_init]
                    if d2 == 0:
                        o_, a_, u_ = w["Pb"][:, :], w["R"][:, :], w["I"][:, :]
                    else:
                        o_, a_, u_ = w["Pb"][:, ::-1], w["R"][:, ::-1], w["I"][:, ::-1]
                    h_bs = P.do("vector", [d_["h_a"], h_i2, h_init], "tensor_tensor_scan", out=o_, data0=a_, data1=u_,
                                initial=init, op0=ALU.mult, op1=ALU.add)
                    h_rec = P.do("vector", [h_bs, h_scan[p]], "tensor_tensor", out=w["I"][:, :], in0=HF[:, c0:c0 + TP],
                                 in1=w["Pb"][:, :], op=ALU.add)
                    h_o = P.do("vector", [h_rec, d_["h_gl"], free["OUT"][wi]], "tensor_tensor", out=w["OUT"][:, :],
                               in0=w["I"][:, :], in1=w["GL"][:, :], op=ALU.mult)
                    h_st = P.dma("gpsimd", d_out[wi], rnnT[rows, c0:c0 + TP], w["OUT"][:, :], h_o)
                    free["OUT"][wi] = h_st
                    free["GL"][wi] = h_o
                    free["R"][wi] = h_bs
                    free["I"][wi] = h_o
                    bst["h_bprev"] = h_bs
                    bst["wi_prev"] = wi
                    free["Pb"][wi] = h_rec
                    h_hffree[p] = h_rec

                for (kind, k) in schedule(NPC):
                    {"F": F2, "S": S2, "B": B2}[kind](k)
                    if kind == "B" and k == 1 and j + 1 < KC:
                        nfirst_ = list(range(NPC)) if (j + 1) % 2 == 0 else list(range(NPC - 1, -1, -1))
                        C_conv(nfirst_[0], j + 1, phase="mm")
                    if kind == "S" and k == NPC - 1 and j + 1 < KC:
                        nfirst = list(range(NPC)) if (j + 1) % 2 == 0 else list(range(NPC - 1, -1, -1))
                        C_conv(nfirst[0], j + 1, evac_eng="scalar", phase="ev")
                        C_conv(nfirst[1], j + 1, evac_eng="scalar")
                h_wgfree[b] = bst["h_lmm"]
            P.barrier()

        if upto <= 3:
            gt1stack.close()
            return finish(nc, P)

        with ExitStack() as s4:
            acc = sb("acc", [128, 2, T], st=s4)
            QT = sb("QT", [128, 10240], BF16, st=s4)
            KT = sb("KT", [128, T], BF16, st=s4)
            VA = sb("VA", [128, NT, 2, 128], BF16, st=s4)
            tok = [dict(q=sb(f"tq{i}", [128, 16, 128], BF16, st=s4), k=sb(f"tk{i}", [128, 16, 128], BF16, st=s4),
                        v=sb(f"tv{i}", [128, 16, 128], BF16, st=s4)) for i in range(2)]
            mf = sb("mf", [128, 2, 128], st=s4)
            mstd = sb("mstd", [128, 2, 128], BF16, st=s4)
            mmid = sb("mmid", [128, 2, 128], BF16, st=s4)
            cm1 = sb("cm1", [128, 1], st=s4)
            pt = [sb(f"pt{i}", [128, 2, 2, 128], BF16, st=s4) for i in range(2)]
            ntmp = sb("ntmp", [128, 2048], st=s4)
            nout = [sb(f"nout{i}", [128, 2048], BF16, st=s4) for i in range(2)]
            ptr4 = [ps(f"ptr4{i}", [128, 2, 4, 128], BF16, st=s4) for i in range(2)]
            pss_ = [ps(f"pss{i}", [128, 2, 512], st=s4) for i in range(2)]
            pss = [t_[:, :, 0:256].rearrange("p h (t q) -> p h t q", q=128) for t_ in pss_]
            pso = [ps(f"pso{i}", [128, 4, 128], st=s4) for i in range(2)]
            d_tok = [P.dsem(f"p4t{i}") for i in range(2)]
            d_no = [P.dsem(f"p4n{i}") for i in range(2)]

            h_m = P.do("gpsimd", [], "memset", mf[:], 0.0)
            hm = [P.do("gpsimd", [h_m], "affine_select", out=mf[:, 0, :], in_=mf[:, 0, :], pattern=[[-1, 128]],
                       compare_op=ALU.is_ge, fill=-30000.0, base=0, channel_multiplier=1),
                  P.do("gpsimd", [h_m], "affine_select", out=mf[:, 1, :], in_=mf[:, 1, :], pattern=[[1, 128]],
                       compare_op=ALU.is_ge, fill=-30000.0, base=0, channel_multiplier=-1)]
            h_ms1 = P.do("gpsimd", hm, "tensor_copy", out=mstd[:], in_=mf[:])
            h_cm = P.do("gpsimd", [h_const], "tensor_scalar", out=cm1[:], in0=conn[:], scalar1=-1.0, scalar2=30000.0,
                        op0=ALU.add, op1=ALU.mult)
            hm2 = [P.do("gpsimd", hm + [h_cm, h_ms1], "tensor_scalar", out=mf[:, 0, 64:128], in0=mf[:, 0, 64:128],
                        scalar1=cm1[:, 0:1], scalar2=None, op0=ALU.add),
                   P.do("gpsimd", hm + [h_cm, h_ms1], "tensor_scalar", out=mf[:, 1, 0:64], in0=mf[:, 1, 0:64],
                        scalar1=cm1[:, 0:1], scalar2=None, op0=ALU.add)]
            h_ms2 = P.do("gpsimd", hm2 + [h_ms1], "tensor_copy", out=mmid[:], in_=mf[:])
            h_masks = [h_ms1, h_ms2, h_idb]
            h_va1 = [P.do("gpsimd", [], "memset", VA[:, :, 0, 64:128], 1.0),
                     P.do("gpsimd", [], "memset", VA[:, :, 1, 0:64], 1.0)]

            pgs = [(p, g) for p in range(4) for g in range(3)]

            def geom(g):
                d = GROUPS[g][1]
                n = NT // d
                RL = T // d
                return d, n, RL, RL + 128

            qt_rd = [None] * 20
            kt_rd = [None] * 16
            va_rd = [None] * 16
            ready = {}
            st4 = dict(tokfree=[None, None], trfree=[None, None], pssfree=[None, None], ptfree=[None, None],
                       psofree=[None, None], accfree=None, ntmpfree=None, noutfree=[None, None], lastacc=None,
                       npiece=0, nbatch=0, nblk=0, nnorm=0)

            def prep_gen(idx):
                p, g = pgs[idx]
                d, n, RL, QS = geom(g)
                qv = qn[g].rearrange("(m r) c -> r m c", r=d)
                kv = kn[g].rearrange("(m r) c -> r m c", r=d)
                vv = vn[g].rearrange("(m r) c -> r m c", r=d)
                rd = dict(h_qt=[None] * NT, h_kt=[None] * NT, h_vt=[None] * NT, h_qz=None)
                ready[idx] = rd
                for c in range(4):
                    wi = st4["npiece"] % 2
                    st4["npiece"] += 1
                    tk = tok[wi]
                    for r in range(d):
                        t_lo = max(16 * c, r * n)
                        t_hi = min(16 * c + 16, (r + 1) * n)
                        if t_lo >= t_hi:
                            continue
                        lt0 = t_lo - r * n
                        cntt = t_hi - t_lo
                        to = t_lo - 16 * c
                        for (srcv, dstt) in ((qv, tk["q"]), (kv, tk["k"]), (vv, tk["v"])):
                            h_ld = P.dma("sync", d_tok[wi], dstt[:, to:to + cntt, :],
                                         srcv[r, lt0 * 128:(lt0 + cntt) * 128, p * 128:(p + 1) * 128]
                                         .rearrange("(t q) c -> q t c", q=128), st4["tokfree"][wi])
                    hfree = []
                    for b4 in range(4):
                        pbq = st4["nbatch"] % 2
                        st4["nbatch"] += 1
                        ti0 = 16 * c + 4 * b4
                        r = ti0 // n
                        lt = ti0 % n
                        P.wait("tensor", h_ld, h_idb, st4["trfree"][pbq])
                        for t in range(4):
                            nc.tensor.transpose(ptr4[pbq][:, 0, t, :], tk["q"][:, b4 * 4 + t, :], idb[:])
                        for t in range(4):
                            tr = nc.tensor.transpose(ptr4[pbq][:, 1, t, :], tk["k"][:, b4 * 4 + t, :], idb[:])
                        h_tr = P.sig("tensor", tr)
                        qc = r * QS + 64 + lt * 128
                        kc = r * RL + lt * 128
                        dq = [qt_rd[qc // 512], qt_rd[(qc + 511) // 512]]
                        dk = [kt_rd[kc // 512]]
                        if pbq == 0:
                            h_eq = P.do("scalar", [h_tr] + dq, "activation", out=QT[:, qc:qc + 512],
                                        in_=ptr4[pbq][:, 0, :, :], func=AF.Copy)
                            h_ek = P.do("scalar", [h_tr] + dk, "activation", out=KT[:, kc:kc + 512],
                                        in_=ptr4[pbq][:, 1, :, :], func=AF.Copy)
                        else:
                            h_eq = P.do("vector", [h_tr] + dq, "tensor_copy", out=QT[:, qc:qc + 512],
                                        in_=ptr4[pbq][:, 0, :, :])
                            h_ek = P.do("vector", [h_tr] + dk, "tensor_copy", out=KT[:, kc:kc + 512],
                                        in_=ptr4[pbq][:, 1, :, :])
                        st4["trfree"][pbq] = [h_eq, h_ek]
                        for t in range(4):
                            rd["h_qt"][ti0 + t] = h_eq
                            rd["h_kt"][ti0 + t] = h_ek
                        hfree.append(h_tr)
                        yield
                    dv = [va_rd[4 * c + x] for x in range(4)]
                    h_v0 = P.do("gpsimd", [h_ld, h_va1] + dv, "tensor_copy", out=VA[:, 16 * c:16 * c + 16, 0, 0:64],
                                in_=tk["v"][:, :, 0:64])
                    h_v1 = P.do("gpsimd", [h_ld, h_va1] + dv, "tensor_copy", out=VA[:, 16 * c:16 * c + 16, 1, 64:128],
                                in_=tk["v"][:, :, 64:128])
                    for t in range(16):
                        rd["h_vt"][16 * c + t] = [h_v0, h_v1]
                    st4["tokfree"][wi] = hfree + [h_v0, h_v1]
                    yield

            def prep_reqs(idx_next, idx_cur):
                _, g2 = pgs[idx_next]
                d2, n2, RL2, QS2 = geom(g2)
                _, g1 = pgs[idx_cur]
                d1, n1, RL1, QS1 = geom(g1)
                lastq = [-1] * 20
                lastk = [-1] * 16
                bi = 0
                for r in range(d1):
                    for i in range(-1, n1):
                        qcol = r * QS1 + 128 * (i + 1)
                        for ch in (qcol // 512, (qcol + 127) // 512):
                            lastq[ch] = bi
                        for tt_ in (i, i + 1):
                            if 0 <= tt_ <= n1 - 1:
                                lastk[(r * n1 + tt_) // 4] = bi
                        bi += 1
                reqs = []
                for c in range(4):
                    for b4 in range(4):
                        ti0 = 16 * c + 4 * b4
                        r = ti0 // n2
                        lt = ti0 % n2
                        qc = r * QS2 + 64 + lt * 128
                        kc = r * RL2 + lt * 128
                        reqs.append(max(lastq[qc // 512], lastq[(qc + 511) // 512], lastk[kc // 512]))
                    reqs.append(max(lastk[4 * c + x] for x in range(4)))
                return reqs

            def pads(idx):
                p, g = pgs[idx]
                d, n, RL, QS = geom(g)
                qv_ = QT[:, 0:d * QS].rearrange("p (r c) -> p r c", c=QS)
                deps = [x for x in qt_rd]
                ready[idx]["h_qz"] = [P.do("gpsimd", deps, "memset", qv_[:, :, 0:64], 0.0),
                                      P.do("gpsimd", deps, "memset", qv_[:, :, QS - 64:QS], 0.0)]

            def run_blocks(idx, nxt, nxt_reqs):
                p, g = pgs[idx]
                d, n, RL, QS = geom(g)
                rd = ready[idx]
                blocks = [(r, i) for r in range(d) for i in range(-1, n)]
                nb_ = len(blocks)
                state = {}
                step = [0]

                def emit_S(bi):
                    r, i = blocks[bi]
                    sbi = (st4["nblk"] + bi) % 2
                    hasA = i >= 0
                    hasB = i + 1 <= n - 1
                    qcol = r * QS + 128 * (i + 1)
                    deps = [st4["pssfree"][sbi], h_masks]
                    tiles = [tt_ for tt_ in (i, i + 1) if 0 <= tt_ <= n - 1]
                    for tt_ in tiles:
                        deps += [rd["h_qt"][r * n + tt_], rd["h_kt"][r * n + tt_]]
                    if i == -1 or i == n - 1:
                        deps.append(rd["h_qz"])
                    P.wait("tensor", deps)
                    M = mmid if i == n // 2 - 1 else mstd
                    mm = None
                    for t_ in ((0,) if hasA else ()) + ((1,) if hasB else ()):
                        kcol = r * RL + 128 * (i + t_)
                        for hh in range(2):
                            rows_ = slice(64 * hh, 64 * hh + 64)
                            nc.tensor.matmul(pss[sbi][:, hh, t_, :], lhsT=KT[rows_, kcol:kcol + 128],
                                             rhs=QT[rows_, qcol:qcol + 128], start=True, stop=False)
                        for hh in range(2):
                            mm = nc.tensor.matmul(pss[sbi][:, hh, t_, :], lhsT=idb[:], rhs=M[:, t_, :], start=False, stop=True)
                    h_s = P.sig("tensor", mm)
                    for ch in (qcol // 512, (qcol + 127) // 512):
                        qt_rd[ch] = h_s
                    for tt_ in tiles:
                        kt_rd[(r * n + tt_) // 4] = h_s
                    if hasA and hasB:
                        sv = lambda a: a[:, :, :, :]
                    elif hasB:
                        sv = lambda a: a[:, :, 1, :]
                    else:
                        sv = lambda a: a[:, :, 0, :]
                    h_e = P.do("scalar", [h_s, st4["ptfree"][sbi]], "activation", out=sv(pt[sbi]), in_=sv(pss[sbi]),
                               func=AF.Exp, scale=0.125)
                    st4["pssfree"][sbi] = h_e
                    state[bi] = h_e

                def emit_PV(bi):
                    r, i = blocks[bi]
                    sbi = (st4["nblk"] + bi) % 2
                    hasA = i >= 0
                    hasB = i + 1 <= n - 1
                    h_e = state.pop(bi)
                    deps = [h_e, st4["psofree"][sbi]]
                    tiles = [tt_ for tt_ in (i, i + 1) if 0 <= tt_ <= n - 1]
                    for tt_ in tiles:
                        deps.append(rd["h_vt"][r * n + tt_])
                    P.wait("tensor", deps)
                    mm = None
                    for hh in range(2):
                        if hasA:
                            mm = nc.tensor.matmul(pso[sbi][:, hh, :], lhsT=VA[:, r * n + i, hh, :],
                                                  rhs=pt[sbi][:, hh, 0, :], start=True, stop=not hasB)
                        if hasB:
                            mm = nc.tensor.matmul(pso[sbi][:, hh, :], lhsT=VA[:, r * n + i + 1, hh, :],
                                                  rhs=pt[sbi][:, hh, 1, :], start=not hasA, stop=True)
                    h_pv = P.sig("tensor", mm)
                    st4["ptfree"][sbi] = h_pv
                    for tt_ in tiles:
                        va_rd[(r * n + tt_) // 4] = h_pv
                    lo = 64 if i == -1 else 0
                    hi = 64 if i == n - 1 else 128
                    m_lo = 128 * i + 64 + lo
                    m_hi = 128 * i + 64 + hi
                    accv = acc[:, :, :].rearrange("p h (m r) -> p h r m", r=d)[:, :, r, m_lo:m_hi]
                    if g == 0:
                        h_u = P.do("vector", [h_pv, st4["accfree"]], "tensor_copy", out=accv, in_=pso[sbi][:, 0:2, lo:hi])
                    else:
                        h_u = P.do("vector", [h_pv, st4["lastacc_prev"]], "tensor_tensor", out=accv, in0=accv,
                                   in1=pso[sbi][:, 0:2, lo:hi], op=ALU.add)
                    st4["psofree"][sbi] = h_u
                    st4["lastacc"] = h_u

                for bi in range(nb_ + 1):
                    if bi < nb_:
                        emit_S(bi)
                    if bi >= 1:
                        emit_PV(bi - 1)
                    if nxt is not None and P4MODE != "nointerleave":
                        budget = 1
                        while budget > 0 and step[0] < len(nxt_reqs) and nxt_reqs[step[0]] <= bi - 1:
                            next(nxt)
                            step[0] += 1
                            budget -= 1
                if nxt is not None:
                    for _ in nxt:
                        pass
                st4["nblk"] += nb_
                st4["lastacc_prev"] = st4["lastacc"]

            st4["lastacc_prev"] = None
            g0 = prep_gen(0)
            for _ in g0:
                pass
            pads(0)
            for idx in range(len(pgs)):
                p, g = pgs[idx]
                if idx + 1 < len(pgs):
                    nxt = prep_gen(idx + 1)
                    reqs = prep_reqs(idx + 1, idx)
                else:
                    nxt, reqs = None, None
                run_blocks(idx, nxt, reqs)
                if idx + 1 < len(pgs):
                    pads(idx + 1)
                if g == 2:
                    hn = []
                    la = st4["lastacc"]
                    for c in range(4):
                        cols = slice(c * 2048, (c + 1) * 2048)
                        oi = st4["nnorm"] % 2
                        st4["nnorm"] += 1
                        h1 = P.do("scalar", [la, st4["ntmpfree"]], "activation", out=ntmp[0:64, :], in_=acc[64:128, 0, cols],
                                  func=AF.Ln)
                        h1 = P.do("scalar", [h1], "activation", out=ntmp[0:64, :], in_=ntmp[0:64, :], func=AF.Exp, scale=-1.0)
                        h2 = P.do("scalar", [la, st4["ntmpfree"]], "activation", out=ntmp[64:128, :], in_=acc[0:64, 1, cols],
                                  func=AF.Ln)
                        h2 = P.do("scalar", [h2], "activation", out=ntmp[64:128, :], in_=ntmp[64:128, :], func=AF.Exp, scale=-1.0)
                        h3 = P.do("vector", [h1, st4["noutfree"][oi]], "tensor_tensor", out=nout[oi][0:64, :],
                                  in0=acc[0:64, 0, cols], in1=ntmp[0:64, :], op=ALU.mult)
                        h4 = P.do("vector", [h2, st4["noutfree"][oi]], "tensor_tensor", out=nout[oi][64:128, :],
                                  in0=acc[64:128, 1, cols], in1=ntmp[64:128, :], op=ALU.mult)
                        st4["ntmpfree"] = [h3, h4]
                        st4["noutfree"][oi] = P.dma("sync", d_no[oi], attT[p * 128:(p + 1) * 128, cols], nout[oi][:, :], h3, h4)
                        hn += [h3, h4]
                    st4["accfree"] = hn
            P.barrier()

        if upto <= 4:
            gt1stack.close()
            return finish(nc, P)

        CH = 512
        NCH = T // CH
        rnn_v = rnnT.rearrange("(k p) t -> p k t", p=128)
        att_v = attT.rearrange("(k p) t -> p k t", p=128)
        sg_v = sgT.rearrange("(a k p) t -> p k a t", p=128, a=2)
        h2_v = h2T.rearrange("(k p) t -> p k t", p=128)
        with ExitStack() as s5:
            Wr = sb("Wr", [128, KC, D], BF16, st=s5)
            Wa = sb("Wa", [128, 4, D], BF16, st=s5)
            Wo = sb("Wo", [128, KC, D], BF16, st=s5)
            ra = [dict(rnn=sb(f"c_rnn{i}", [128, KC, CH], BF16, st=s5), att=sb(f"c_att{i}", [128, 4, CH], BF16, st=s5))
                  for i in range(2)]
            xb = [sb(f"c_x{i}", [128, 4, D], st=s5) for i in range(2)]
            gring = sb("c_g", [128, KC, 2, CH], st=s5)
            merged = sb("merged", [128, KC, CH], BF16, st=s5)
            mt1 = [sb(f"mt1{i}", [128, CH], st=s5) for i in range(2)]
            mt2 = [sb(f"mt2{i}", [128, CH], st=s5) for i in range(2)]
            mt3 = [sb(f"mt3{i}", [128, CH], st=s5) for i in range(2)]
            xn5 = [sb(f"xn5{i}", [128, D], st=s5) for i in range(2)]
            h2st = [sb(f"h2st{i}", [128, KC, CH], BF16, st=s5) for i in range(2)]
            junk5 = sb("junk5", [128, D], BF16, st=s5)
            ss5 = sb("ss5", [128, NT], st=s5)
            rs5 = sb("rs5", [128, NT], st=s5)
            pa = [ps(f"pa{i}", [128, 512], st=s5) for i in range(2)]
            pbb = [ps(f"pbb{i}", [128, 512], st=s5) for i in range(2)]
            pmx = [ps(f"pmx{i}", [128, 512], st=s5) for i in range(2)]
            ptr5 = ps("ptr5", [128, KC, 128], st=s5)
            d_w5 = P.dsem("p5w")
            d_ra = [P.dsem(f"p5l{i}") for i in range(2)]
            d_xl = [P.dsem(f"p5xl{i}") for i in range(2)]
            d_g = [P.dsem(f"p5g{i}") for i in range(KC)]
            d_h2 = [P.dsem(f"p5h{i}") for i in range(2)]
            d_x1 = [P.dsem(f"p5x{i}") for i in range(2)]
            P.dma("gpsimd", d_w5, Wr[:], w_br_rnn.rearrange("(k p) n -> p k n", p=128))
            P.dma("gpsimd", d_w5, Wa[:], w_br_attn.rearrange("(k p) n -> p k n", p=128))
            h_w5 = P.dma("gpsimd", d_w5, Wo[:], w_out.rearrange("(k p) n -> p k n", p=128))
            st5 = dict(rafree=[None, None], xfree=[None, None], gfree=[None] * KC, pafree=[None, None],
                       mt12free=[None, None], pmxfree=[None, None], mt3free=[None, None], mergedfree=None,
                       ptr5free=None, xn5free=[None, None], h2stfree=[None, None], junk=None, na=0, nm=0)
            h_g = [None] * KC
            h_ra = [None, None]
            h_xl = [None, None]
            hx1 = {}
            hev = {}

            def load_ra(c):
                wi = c % 2
                cols = slice(c * CH, (c + 1) * CH)
                P.dma("sync", d_ra[wi], ra[wi]["rnn"][:], rnn_v[:, :, cols], st5["rafree"][wi])
                h_ra[wi] = P.dma("sync", d_ra[wi], ra[wi]["att"][:], att_v[:, :, cols], st5["rafree"][wi])

            def load_x(c):
                wi = c % 2
                cols = slice(c * CH, (c + 1) * CH)
                h_xl[wi] = P.dma("sync", d_xl[wi], xb[wi][:], x_in[cols, :].rearrange("(t p) c -> p t c", p=128),
                                 st5["xfree"][wi])

            def load_g(c, oc):
                cols = slice(c * CH, (c + 1) * CH)
                h_g[oc] = P.dma("sync", d_g[oc], gring[:, oc, :, :], sg_v[:, oc, :, cols], st5["gfree"][oc])

            def A_step(c, oc):
                wi = c % 2
                ab = st5["na"] % 2
                st5["na"] += 1
                ocs = slice(oc * 128, (oc + 1) * 128)
                P.wait("tensor", h_ra[wi], h_w5, st5["pafree"][ab])
                for k in range(KC):
                    nc.tensor.matmul(pa[ab][:, :], lhsT=Wr[:, k, ocs], rhs=ra[wi]["rnn"][:, k, :], start=(k == 0),
                                     stop=(k == KC - 1))
                for k in range(4):
                    mm = nc.tensor.matmul(pbb[ab][:, :], lhsT=Wa[:, k, ocs], rhs=ra[wi]["att"][:, k, :], start=(k == 0),
                                          stop=(k == 3))
                h_mm = P.sig("tensor", mm)
                h1 = P.do("vector", [h_mm, h_g[oc], st5["mt12free"][ab]], "tensor_tensor", out=mt1[ab][:, :], in0=pa[ab][:, :],
                          in1=gring[:, oc, 0, :], op=ALU.mult)
                h2 = P.do("vector", [h_mm, h_g[oc], st5["mt12free"][ab]], "tensor_tensor", out=mt2[ab][:, :], in0=pbb[ab][:, :],
                          in1=gring[:, oc, 1, :], op=ALU.mult)
                st5["pafree"][ab] = [h1, h2]
                st5["gfree"][oc] = [h1, h2]
                h3 = P.do("gpsimd", [h1, h2, st5["mergedfree"]], "tensor_tensor", out=merged[:, oc, :], in0=mt1[ab][:, :],
                          in1=mt2[ab][:, :], op=ALU.add)
                st5["mt12free"][ab] = h3
                if c + 1 < NCH:
                    load_g(c + 1, oc)
                return h_mm, h3

            def B_phase(c, hm_):
                wi = c % 2
                seg = c // (NCH // 2)
                hx1[c] = [[None, None] for _ in range(4)]
                for t in range(4):
                    for half in range(2):
                        mb = st5["nm"] % 2
                        st5["nm"] += 1
                        hs_ = slice(half * 512, (half + 1) * 512)
                        P.wait("tensor", hm_, st5["pmxfree"][mb])
                        for k in range(KC):
                            mm = nc.tensor.matmul(pmx[mb][:, :], lhsT=merged[:, k, t * 128:(t + 1) * 128], rhs=Wo[:, k, hs_],
                                                  start=(k == 0), stop=(k == KC - 1))
                        h_mm = P.sig("tensor", mm)
                        h1 = P.do("vector", [h_mm, st5["mt3free"][mb], h_gt], "tensor_tensor", out=mt3[mb][:, :],
                                  in0=pmx[mb][:, :], in1=gt1row[:, seg, hs_], op=ALU.mult)
                        st5["pmxfree"][mb] = h1
                        h2 = P.do("gpsimd", [h1, h_xl[wi]], "tensor_tensor", out=xb[wi][:, t, hs_], in0=xb[wi][:, t, hs_],
                                  in1=mt3[mb][:, :], op=ALU.add)
                        st5["mt3free"][mb] = h2
                        hx1[c][t][half] = h2
                st5["mergedfree"] = h_mm

            hxn = {}

            hrs5 = {}

            def C_stats(c, t):
                wi = c % 2
                i = c * 4 + t
                st5["junk"], hrs5[(c, t)] = norm_tile(xb[wi][:, t, :], ss5[:, i:i + 1], rs5[:, i:i + 1], None, junk5,
                                                      hx1[c][t], st5["junk"])

            def C_xn(c, t):
                wi = c % 2
                i = c * 4 + t
                xb_ = t % 2
                hxn[(c, t)] = P.do("vector", [hrs5.pop((c, t)), st5["xn5free"][xb_]], "tensor_scalar", out=xn5[xb_][:, :],
                                   in0=xb[wi][:, t, :], scalar1=rs5[:, i:i + 1], scalar2=None, op0=ALU.mult)

            def C_trans(c, t):
                wi = c % 2
                seg = c // (NCH // 2)
                xb_ = t % 2
                hst = h2st[wi]
                P.wait("tensor", hxn.pop((c, t)), h_idf, st5["ptr5free"])
                for k in range(KC):
                    tr = nc.tensor.transpose(ptr5[:, k, :], xn5[xb_][:, k * 128:(k + 1) * 128], idf[:])
                h_tr = P.sig("tensor", tr)
                st5["xn5free"][xb_] = h_tr
                hs2 = []
                for k in range(KC):
                    hs2.append(P.do("scalar", [h_tr, h_ss, st5["h2stfree"][wi]], "activation",
                                    out=hst[:, k, t * 128:(t + 1) * 128], in_=ptr5[:, k, :], func=AF.Identity,
                                    scale=scale2[:, k, seg:seg + 1], bias=shift2[:, k, seg:seg + 1]))
                st5["ptr5free"] = hs2
                hev.setdefault(c, [])
                hev[c] += hs2
                if t == 3:
                    cols = slice(c * CH, (c + 1) * CH)
                    st5["h2stfree"][wi] = P.dma("scalar", d_h2[wi], h2_v[:, :, cols], hst[:, :, :], hev[c])
                    st5["xfree"][wi] = P.dma("scalar", d_x1[wi], x1s[cols, :].rearrange("(t p) c -> p t c", p=128),
                                             xb[wi][:, :, :], hev[c])
                    if c + 2 < NCH:
                        load_x(c + 2)

            load_ra(0)
            load_x(0)
            load_x(1)
            for oc in range(KC):
                load_g(0, oc)
            load_ra(1)
            for c in range(NCH + 1):
                if c < NCH:
                    hm_ = []
                    if c >= 1:
                        for t_ in range(4):
                            C_stats(c - 1, t_)
                    for oc in range(KC):
                        h_mm, h3 = A_step(c, oc)
                        hm_.append(h3)
                        if c >= 1:
                            if oc % 2 == 0:
                                C_xn(c - 1, oc // 2)
                            else:
                                C_trans(c - 1, oc // 2)
                    st5["rafree"][c % 2] = h_mm
                    if c + 2 < NCH:
                        load_ra(c + 2)
                    B_phase(c, hm_)
                else:
                    for t in range(4):
                        C_stats(c - 1, t)
                    for t in range(4):
                        C_xn(c - 1, t)
                        C_trans(c - 1, t)
            P.barrier()

        if upto <= 5:
            gt1stack.close()
            return finish(nc, P)

        gt1stack.close()
        CF = 512
        NCF = T // CF
        with ExitStack() as s6:
            W1 = sb("W1", [128, KC, 2 * D_FF], BF16, st=s6)
            W2 = sb("W2", [128, FC, D], BF16, st=s6)
            h2c_ = [sb(f"h2c{i}", [128, KC, CF], BF16, st=s6) for i in range(2)]
            x1t = [sb(f"x1t{i}", [128, D], st=s6) for i in range(2)]
            hid = sb("hid", [128, FC, CF], BF16, st=s6)
            sgt = [sb(f"sgt{i}", [128, CF], st=s6) for i in range(2)]
            ft = [sb(f"ft{i}", [128, 512], st=s6) for i in range(2)]
            junk6 = sb("junk6", [128, D], BF16, st=s6)
            ss6 = sb("ss6", [128, NT], st=s6)
            rs6 = sb("rs6", [128, NT], st=s6)
            pgg = [ps(f"pgg{i}", [128, 512], st=s6) for i in range(2)]
            puu = [ps(f"puu{i}", [128, 512], st=s6) for i in range(2)]
            poo = [ps(f"poo{i}", [128, 512], st=s6) for i in range(2)]
            d_w1 = P.dsem("p6w1")
            d_w2 = P.dsem("p6w2")
            d_l6 = [P.dsem(f"p6l{i}") for i in range(2)]
            d_xt = [P.dsem(f"p6x{i}") for i in range(2)]
            w1v = w_ffn_in.rearrange("(k p) n -> p k n", p=128)
            for k in range(KC):
                h_w1 = P.dma("gpsimd", d_w1, W1[:, k, :], w1v[:, k, :])
            w2v = w_ffn_out.rearrange("(f p) n -> p f n", p=128)
            for f0 in range(0, FC, 2):
                h_w2 = P.dma("gpsimd", d_w2, W2[:, f0:f0 + 2, :], w2v[:, f0:f0 + 2, :])
            h_h2free = [None, None]
            h_xtfree = [None] * 2
            h_pgfree6 = [None, None]
            h_sgtfree = [None, None]
            h_poofree = [None, None]
            h_ftfree = [None, None]
            h_hidfree = None
            h_junk6 = None
            nf = 0
            no = 0
            nx = 0
            for c in range(NCF):
                seg = c // (NCF // 2)
                cols = slice(c * CF, (c + 1) * CF)
                h2c = h2c_[c % 2]
                if c == 0:
                    h_lh_next = P.dma("sync", d_l6[0], h2c_[0][:], h2_v[:, :, cols], h_h2free[0])
                h_lh = h_lh_next
                if c + 1 < NCF:
                    h_lh_next = P.dma("sync", d_l6[(c + 1) % 2], h2c_[(c + 1) % 2][:],
                                      h2_v[:, :, slice((c + 1) * CF, (c + 2) * CF)], h_h2free[(c + 1) % 2])
                hh_ = []
                for f in range(FC):
                    gb = nf % 2
                    nf += 1
                    P.wait("tensor", h_lh, h_w1, h_pgfree6[gb])
                    for k in range(KC):
                        nc.tensor.matmul(pgg[gb][:, :], lhsT=W1[:, k, f * 128:(f + 1) * 128], rhs=h2c[:, k, :],
                                         start=(k == 0), stop=(k == KC - 1))
                    for k in range(KC):
                        mm = nc.tensor.matmul(puu[gb][:, :], lhsT=W1[:, k, D_FF + f * 128:D_FF + (f + 1) * 128],
                                              rhs=h2c[:, k, :], start=(k == 0), stop=(k == KC - 1))
                    h_mm = P.sig("tensor", mm)
                    h1 = P.do("scalar", [h_mm, h_sgtfree[gb]], "activation", out=sgt[gb][:, :], in_=pgg[gb][:, :], func=AF.Silu)
                    h2 = P.do("vector", [h1, h_mm, h_hidfree], "tensor_tensor", out=hid[:, f, :], in0=puu[gb][:, :],
                              in1=sgt[gb][:, :], op=ALU.mult)
                    h_pgfree6[gb] = [h1, h2]
                    h_sgtfree[gb] = h2
                    hh_.append(h2)
                h_h2free[c % 2] = h_mm
                for t in range(4):
                    i = c * 4 + t
                    xi = nx % 2
                    nx += 1
                    rows_ = slice(i * 128, (i + 1) * 128)
                    h_lx = P.dma("sync", d_xt[xi], x1t[xi][:, :], x1s[rows_, :], h_xtfree[xi])
                    hx2 = []
                    for half in range(2):
                        ob = no % 2
                        no += 1
                        hs_ = slice(half * 512, (half + 1) * 512)
                        P.wait("tensor", hh_, h_w2, h_poofree[ob])
                        for f in range(FC):
                            mm = nc.tensor.matmul(poo[ob][:, :], lhsT=hid[:, f, t * 128:(t + 1) * 128], rhs=W2[:, f, hs_],
                                                  start=(f == 0), stop=(f == FC - 1))
                        h_mm = P.sig("tensor", mm)
                        h1 = P.do("vector", [h_mm, h_ftfree[ob], h_gt], "tensor_tensor", out=ft[ob][:, :], in0=poo[ob][:, :],
                                  in1=gt2row[:, seg, hs_], op=ALU.mult)
                        h_poofree[ob] = h1
                        h2 = P.do("gpsimd", [h1, h_lx], "tensor_tensor", out=x1t[xi][:, hs_], in0=x1t[xi][:, hs_],
                                  in1=ft[ob][:, :], op=ALU.add)
                        h_ftfree[ob] = h2
                        hx2.append(h2)
                    h_junk6, h_rs = norm_tile(x1t[xi][:, :], ss6[:, i:i + 1], rs6[:, i:i + 1], None, junk6, hx2, h_junk6)
                    hy = P.do("vector", [h_rs, h_const], "scalar_tensor_tensor", out=x1t[xi][:, :], in0=x1t[xi][:, :],
                              scalar=rs6[:, i:i + 1], in1=fgrow[:, :], op0=ALU.mult, op1=ALU.mult)
                    h_xtfree[xi] = P.dma("sync", d_xt[xi], y_out[rows_, :], x1t[xi][:, :], hy)
                h_hidfree = h_mm
            P.barrier()

        return finish(nc, P)


def finish(nc, P):
    P.barrier()
    return nc


def _core_inputs(core, inp):
    if core < 4:
        x = np.ascontiguousarray(inp["x_prompt"][core])
        c2 = np.stack([inp["c_prompt"][core], inp["c_prompt"][core]], 0)
        connv = 1.0
        pos = np.arange(T)
    else:
        a, b = 2 * (core - 4), 2 * (core - 4) + 1
        x = np.ascontiguousarray(np.concatenate([inp["x_sample"][a], inp["x_sample"][b]], 0))
        c2 = np.stack([inp["c_sample"][a], inp["c_sample"][b]], 0)
        connv = 0.0
        pos = np.concatenate([np.arange(SEG), np.arange(SEG)])
    return x, c2, connv, pos


def _fm(v):
    return np.ascontiguousarray(np.asarray(v, np.float32).reshape(-1, 128).T)


def _shared_inputs(inp):
    vecs = np.zeros((128, NV), np.float32)
    vecs[:, V_BADA:V_BADA + 48] = _fm(inp["b_ada"][0])
    vecs[:, V_N1G:V_N1G + 8] = _fm(inp["norm1_g"][0])
    vecs[:, V_N2G:V_N2G + 8] = _fm(inp["norm2_g"][0])
    for k in range(4):
        vecs[:, V_CONVW + k * 8:V_CONVW + k * 8 + 8] = _fm(inp["conv_w"][0, k])
    vecs[:, V_CONVB:V_CONVB + 8] = _fm(inp["conv_b"][0])
    for d in range(2):
        vecs[:, V_BA + d * 8:V_BA + d * 8 + 8] = _fm(inp["rg_ba"][0, d].reshape(-1))
        vecs[:, V_BX + d * 8:V_BX + d * 8 + 8] = _fm(inp["rg_bx"][0, d].reshape(-1))
        vecs[:, V_LAM + d * 8:V_LAM + d * 8 + 8] = _fm(inp["rg_lambda"][0, d])
    sh = {
        "vecs": vecs,
        "fgrow": np.ascontiguousarray(inp["final_g"].reshape(1, D).astype(np.float32)),
        "badarow": np.ascontiguousarray(inp["b_ada"][0].reshape(1, 6 * D).astype(np.float32)),
        "w_ada": np.ascontiguousarray(inp["w_ada"][0]),
        "w_in": np.ascontiguousarray(inp["w_in"][0]),
        "rg_wa": np.ascontiguousarray(inp["rg_wa"][0]),
        "rg_wx": np.ascontiguousarray(inp["rg_wx"][0]),
        "w_br_rnn": np.ascontiguousarray(inp["w_br_rnn"][0]),
        "w_br_attn": np.ascontiguousarray(inp["w_br_attn"][0]),
        "w_out": np.ascontiguousarray(inp["w_out"][0]),
        "w_ffn_in": np.ascontiguousarray(inp["w_ffn_in"][0]),
        "w_ffn_out": np.ascontiguousarray(inp["w_ffn_out"][0]),
    }
    return sh


def _rope_table(pos):
    inv = (ROPE_THETA ** (-(np.arange(0, 16, 2, dtype=np.float32) / np.float32(16)))).astype(np.float32)
    ang = pos.astype(np.float32)[:, None] * inv[None, :]
    tab = np.concatenate([np.cos(ang), np.sin(ang)], -1).astype(np.float32)
    return np.ascontiguousarray(tab.reshape(NT, 128, 16).transpose(1, 0, 2))


def make_in_maps(inp):
    inp = {k: np.asarray(v) for k, v in inp.items()}
    sh = _shared_inputs(inp)
    maps = []
    for core in range(8):
        x, c2, connv, pos = _core_inputs(core, inp)
        m = dict(sh)
        m["x"] = x
        m["cT"] = np.ascontiguousarray(c2.astype(np.float32).reshape(2, KC, 128).transpose(2, 1, 0))
        m["conn"] = np.full((128, 1), connv, np.float32)
        m["rope"] = _rope_table(pos)
        maps.append(m)
    return maps


def kernel(**inputs):
    nc = build_program()
    maps = make_in_maps(inputs)
    res = run_bass_kernel_spmd(nc, maps, core_ids=list(range(8)))
    ys = [np.asarray(r["y"], np.float32) for r in res.results]
    y_prompt = np.stack(ys[0:4], 0)
    y_sample = np.stack([ys[4 + i // 2][(i % 2) * SEG:(i % 2 + 1) * SEG] for i in range(8)], 0)
    return (y_prompt, y_sample)
```
